# Optimizing a Trainium2 kernel written in Bass

```python
import math
import jax, jax.numpy as jnp
from jax import lax
import numpy as np

D_MODEL = 1024
BATCH = 8
SEQ = 4096
DEPTH = 2

N_A_LAYERS = DEPTH // 2
N_B_LAYERS = DEPTH - N_A_LAYERS
SSM_WIDTH = D_MODEL
SSM_GROUP = 16
SSM_GROUPS = SSM_WIDTH // SSM_GROUP
SSM_STATE = 64
DT_MIN = 1e-3
DT_MAX = 1e-1
N_HEADS = 8
HEAD_DIM = D_MODEL // (2 * N_HEADS)
V_DIM = 2 * HEAD_DIM
QK_WIDTH = N_HEADS * 2 * HEAD_DIM
ATTN_WIDTH = N_HEADS * V_DIM
Q_BLOCK = 128
EPS = 1e-6

kernel_name = "yoco_s5_diffattn_sandwich_adaln"


def rmsnorm(x, g):
    xf = x.astype(jnp.float32)
    y = xf * lax.rsqrt(jnp.mean(xf * xf, axis=-1, keepdims=True) + EPS)
    return (y * g.astype(jnp.float32)).astype(x.dtype)


def adaln(c, w, b):
    mod = jax.nn.silu(c) @ w + b
    shift, scale, gate = jnp.split(mod, 3, axis=-1)
    return shift[:, None, :], scale[:, None, :], gate[:, None, :]


def _linear_recurrence_op(left, right):
    a_l, b_l = left
    a_r, b_r = right
    return a_l * a_r, a_r * b_l + b_r


def s5_mixer(h, w_in, lam_re, lam_im, log_dt, b_re, b_im, c_re, c_im, d_skip, w_glu, b_glu, w_out):
    bsz, seq, _ = h.shape
    u, z = jnp.split(h @ w_in, 2, axis=-1)
    uf = u.astype(jnp.float32).reshape(bsz, seq, SSM_GROUPS, SSM_GROUP)
    lam = lax.complex(lam_re.astype(jnp.float32), lam_im.astype(jnp.float32))
    dt = jnp.exp(log_dt.astype(jnp.float32))[:, None]
    lam_bar = jnp.exp(lam * dt)
    b_mat = lax.complex(b_re.astype(jnp.float32), b_im.astype(jnp.float32))
    b_bar = ((lam_bar - 1.0) / lam)[..., None] * b_mat
    bu = jnp.einsum('blgc,gpc->blgp', uf.astype(jnp.complex64), b_bar)
    a_seq = jnp.broadcast_to(lam_bar, bu.shape)
    _, states = lax.associative_scan(_linear_recurrence_op, (a_seq, bu), axis=1)
    c_mat = lax.complex(c_re.astype(jnp.float32), c_im.astype(jnp.float32))
    y = jnp.einsum('blgp,gcp->blgc', states, c_mat).real
    y = y + d_skip.astype(jnp.float32).reshape(SSM_GROUPS, SSM_GROUP) * uf
    y = jax.nn.gelu(y.reshape(bsz, seq, SSM_WIDTH))
    y = y * jax.nn.sigmoid(y @ w_glu.astype(jnp.float32) + b_glu.astype(jnp.float32))
    y = y * jax.nn.silu(z.astype(jnp.float32))
    return y.astype(h.dtype) @ w_out


def diff_attention(h, k, v, w_in, lq1, lk1, lq2, lk2, g_sub, w_out, lambda_init):
    bsz, seq, _ = h.shape
    q, z = jnp.split(h @ w_in, [QK_WIDTH], axis=-1)
    q = q.astype(jnp.float32).reshape(bsz, seq, N_HEADS, 2, HEAD_DIM)
    lam = (jnp.exp(jnp.sum(lq1.astype(jnp.float32) * lk1.astype(jnp.float32)))
           - jnp.exp(jnp.sum(lq2.astype(jnp.float32) * lk2.astype(jnp.float32))) + lambda_init)
    n_blocks = seq // Q_BLOCK
    qb = q.reshape(bsz, n_blocks, Q_BLOCK, N_HEADS, 2, HEAD_DIM).transpose(1, 0, 2, 3, 4, 5)
    kf = k.astype(jnp.float32)
    vf = v.astype(jnp.float32)
    key_pos = jnp.arange(seq)
    scale = HEAD_DIM ** -0.5

    def block(args):
        q_blk, blk = args
        s = jnp.einsum('bqhcd,bkhcd->bhcqk', q_blk, kf) * scale
        q_pos = blk * Q_BLOCK + jnp.arange(Q_BLOCK)
        mask = key_pos[None, :] <= q_pos[:, None]
        s = jnp.where(mask, s, -jnp.inf)
        p = jax.nn.softmax(s, axis=-1)
        att = p[:, :, 0] - lam * p[:, :, 1]
        return jnp.einsum('bhqk,bkhe->bqhe', att, vf)

    o = lax.map(block, (qb, jnp.arange(n_blocks)))
    o = o.transpose(1, 0, 2, 3, 4).reshape(bsz, seq, N_HEADS, V_DIM)
    o = rmsnorm(o, g_sub) * (1.0 - lambda_init)
    o = o.reshape(bsz, seq, ATTN_WIDTH) * jax.nn.silu(z.astype(jnp.float32))
    return o.astype(h.dtype) @ w_out


def setup_inputs(seed: int = 0) -> dict:
    key = jax.random.key(seed)
    ks = jax.random.split(key, 32)
    D = D_MODEL
    nrm = lambda k, shape, s: jax.random.normal(k, shape, jnp.float32) * s
    lam_im_base = jnp.pi * jnp.arange(SSM_STATE, dtype=jnp.float32)
    return {
        "x": nrm(ks[0], (BATCH, SEQ, D), 1.0),
        "c": nrm(ks[1], (BATCH, D), 1.0),
        "ada_w": nrm(ks[2], (DEPTH, D, 3 * D), D ** -0.5),
        "ada_b": nrm(ks[3], (DEPTH, 3 * D), 0.02),
        "g_pre": 1.0 + nrm(ks[4], (DEPTH, D), 0.02),
        "g_post": 1.0 + nrm(ks[5], (DEPTH, D), 0.02),
        "a_w_in": nrm(ks[6], (N_A_LAYERS, D, 2 * SSM_WIDTH), D ** -0.5),
        "a_lam_re": -0.5 + nrm(ks[7], (N_A_LAYERS, SSM_GROUPS, SSM_STATE), 0.01),
        "a_lam_im": lam_im_base + nrm(ks[8], (N_A_LAYERS, SSM_GROUPS, SSM_STATE), 0.01),
        "a_log_dt": jax.random.uniform(ks[9], (N_A_LAYERS, SSM_GROUPS), jnp.float32,
                                       math.log(DT_MIN), math.log(DT_MAX)),
        "a_b_re": nrm(ks[10], (N_A_LAYERS, SSM_GROUPS, SSM_STATE, SSM_GROUP), (2 * SSM_GROUP) ** -0.5),
        "a_b_im": nrm(ks[11], (N_A_LAYERS, SSM_GROUPS, SSM_STATE, SSM_GROUP), (2 * SSM_GROUP) ** -0.5),
        "a_c_re": nrm(ks[12], (N_A_LAYERS, SSM_GROUPS, SSM_GROUP, SSM_STATE), (2 * SSM_STATE) ** -0.5),
        "a_c_im": nrm(ks[13], (N_A_LAYERS, SSM_GROUPS, SSM_GROUP, SSM_STATE), (2 * SSM_STATE) ** -0.5),
        "a_d": nrm(ks[14], (N_A_LAYERS, SSM_WIDTH), 1.0),
        "a_w_glu": nrm(ks[15], (N_A_LAYERS, SSM_WIDTH, SSM_WIDTH), SSM_WIDTH ** -0.5),
        "a_b_glu": nrm(ks[16], (N_A_LAYERS, SSM_WIDTH), 0.02),
        "a_w_out": nrm(ks[17], (N_A_LAYERS, SSM_WIDTH, D), SSM_WIDTH ** -0.5),
        "g_kv": 1.0 + nrm(ks[18], (D,), 0.02),
        "w_k": nrm(ks[19], (D, QK_WIDTH), D ** -0.5),
        "w_v": nrm(ks[20], (D, ATTN_WIDTH), D ** -0.5),
        "b_w_in": nrm(ks[21], (N_B_LAYERS, D, QK_WIDTH + ATTN_WIDTH), D ** -0.5),
        "b_lq1": nrm(ks[22], (N_B_LAYERS, HEAD_DIM), 0.1),
        "b_lk1": nrm(ks[23], (N_B_LAYERS, HEAD_DIM), 0.1),
        "b_lq2": nrm(ks[24], (N_B_LAYERS, HEAD_DIM), 0.1),
        "b_lk2": nrm(ks[25], (N_B_LAYERS, HEAD_DIM), 0.1),
        "b_g_sub": 1.0 + nrm(ks[26], (N_B_LAYERS, V_DIM), 0.02),
        "b_w_out": nrm(ks[27], (N_B_LAYERS, ATTN_WIDTH, D), ATTN_WIDTH ** -0.5),
    }


def reference(x, c, ada_w, ada_b, g_pre, g_post, a_w_in, a_lam_re, a_lam_im, a_log_dt,
              a_b_re, a_b_im, a_c_re, a_c_im, a_d, a_w_glu, a_b_glu, a_w_out,
              g_kv, w_k, w_v, b_w_in, b_lq1, b_lk1, b_lq2, b_lk2, b_g_sub, b_w_out):
    bsz, seq, _ = x.shape
    h = x
    k = None
    v = None
    for layer in range(DEPTH):
        shift, scale, gate = adaln(c, ada_w[layer], ada_b[layer])
        h_in = rmsnorm(h, g_pre[layer]) * (1.0 + scale) + shift
        if layer < N_A_LAYERS:
            i = layer
            y = s5_mixer(h_in, a_w_in[i], a_lam_re[i], a_lam_im[i], a_log_dt[i], a_b_re[i], a_b_im[i],
                         a_c_re[i], a_c_im[i], a_d[i], a_w_glu[i], a_b_glu[i], a_w_out[i])
        else:
            if layer == N_A_LAYERS:
                kv_in = rmsnorm(h, g_kv)
                k = (kv_in @ w_k).reshape(bsz, seq, N_HEADS, 2, HEAD_DIM)
                v = (kv_in @ w_v).reshape(bsz, seq, N_HEADS, V_DIM)
            j = layer - N_A_LAYERS
            lambda_init = 0.8 - 0.6 * math.exp(-0.3 * layer)
            y = diff_attention(h_in, k, v, b_w_in[j], b_lq1[j], b_lk1[j], b_lq2[j], b_lk2[j],
                               b_g_sub[j], b_w_out[j], lambda_init)
        h = h + gate * rmsnorm(y, g_post[layer])
    return h
```

```python
import math
import os
from contextlib import ExitStack

import numpy as np
import concourse.bass as bass
import concourse.mybir as mybir
from concourse.bass_utils import run_bass_kernel_spmd

F32 = mybir.dt.float32
BF16 = mybir.dt.bfloat16
I32 = mybir.dt.int32
AF = mybir.ActivationFunctionType
ALU = mybir.AluOpType
AX = mybir.AxisListType

L = 4096
D = 1024
EPS = 1e-6
LAMBDA_INIT = 0.8 - 0.6 * math.exp(-0.3 * 1)
NEG = -30000.0


class Buf:
    __slots__ = ("name", "w", "r")

    def __init__(self, name=""):
        self.name = name
        self.w = None
        self.r = []


class Tok:
    __slots__ = ("eng", "seq", "sem", "val")

    def __init__(self, eng, seq, sem, val):
        self.eng = eng
        self.seq = seq
        self.sem = sem
        self.val = val


class DmaSem:
    def __init__(self, sem, key):
        self.sem = sem
        self.key = key
        self.n = 0


class EngW:
    def __init__(self, sched, key, sem):
        self.sched = sched
        self.key = key
        self.sem = sem
        self.seq = 0
        self.cnt = 0
        self.waited_seq = {}
        self.waited_dma = {}
        self.prog = []
        self.last = None

    def _gather(self, reads, writes):
        deps = []
        for b in reads:
            if b.w is not None:
                deps.append(b.w)
        for b in writes:
            if b.w is not None:
                deps.append(b.w)
            deps.extend(b.r)
        return deps

    def _wait(self, tok):
        if tok.eng is None:
            k = tok.sem.key
            if self.waited_dma.get(k, 0) >= tok.val:
                return
            self.waited_dma[k] = tok.val
            sem, val = tok.sem.sem, tok.val
            self.prog.append(lambda e, sem=sem, val=val: e.wait_ge(sem, val))
            return
        if tok.eng is self and self.key == "pe":
            return
        k = tok.eng.key
        if self.waited_seq.get(k, -1) >= tok.seq:
            return
        self.waited_seq[k] = tok.seq
        self.sched.record.add((k, tok.seq))
        if tok.val is None:
            raise RuntimeError(f"token {k}:{tok.seq} not marked")
        sem, val = tok.sem, tok.val
        self.prog.append(lambda e, sem=sem, val=val: e.wait_ge(sem, val))

    def op(self, fn, reads=(), writes=()):
        for t in self._gather(reads, writes):
            self._wait(t)
        seq = self.seq
        self.seq += 1
        needed = self.sched.needed
        mark = needed is None or (self.key, seq) in needed
        if mark:
            self.cnt += 1
            sem = self.sem
            self.prog.append(lambda e, fn=fn, sem=sem: fn(e).then_inc(sem, 1))
            tok = Tok(self, seq, self.sem, self.cnt)
        else:
            self.prog.append(lambda e, fn=fn: fn(e))
            tok = Tok(self, seq, self.sem, None)
        self.last = tok
        for b in reads:
            b.r.append(tok)
        for b in writes:
            b.w = tok
            b.r = []
        return tok

    def dma(self, out, in_, dsem, reads=(), writes=()):
        for t in self._gather(reads, writes):
            self._wait(t)
        dsem.n += 1
        val = 16 * dsem.n
        sem = dsem.sem
        self.prog.append(lambda e, out=out, in_=in_, sem=sem: e.dma_start(out=out, in_=in_).then_inc(sem, 16))
        tok = Tok(None, -1, dsem, val)
        for b in reads:
            b.r.append(tok)
        for b in writes:
            b.w = tok
            b.r = []
        return tok


class Sched:
    def __init__(self, nc, stack, needed=None):
        self.nc = nc
        self.needed = needed
        self.record = set()
        self.stack = stack
        mk = lambda n: stack.enter_context(nc.semaphore(n))
        self.pe = EngW(self, "pe", mk("s_pe"))
        self.act = EngW(self, "act", mk("s_act"))
        self.dve = EngW(self, "dve", mk("s_dve"))
        self.pool = EngW(self, "pool", mk("s_pool"))
        self.sp = EngW(self, "sp", mk("s_sp"))
        self.engs = [self.pe, self.act, self.dve, self.pool, self.sp]
        self.ndsem = 0
        self.dsems = []

    def dma_sem(self):
        self.ndsem += 1
        key = f"dq{self.ndsem}"
        ds = DmaSem(self.stack.enter_context(self.nc.semaphore(key)), key)
        self.dsems.append(ds)
        return ds

    def barrier(self):
        toks = [w.last for w in self.engs[:4] if w.last is not None]
        for w in self.engs:
            for t in toks:
                if t.eng is not w:
                    w._wait(t)
            for ds in self.dsems:
                if ds.n > 0:
                    w._wait(Tok(None, -1, ds, 16 * ds.n))

    def flush(self):
        nc = self.nc
        with nc.Block() as block:
            @block.tensor
            def _(e):
                for f in self.pe.prog:
                    f(e)

            @block.scalar
            def _(e):
                for f in self.act.prog:
                    f(e)

            @block.vector
            def _(e):
                for f in self.dve.prog:
                    f(e)

            @block.gpsimd
            def _(e):
                for f in self.pool.prog:
                    f(e)

            @block.sync
            def _(e):
                for f in self.sp.prog:
                    f(e)
        for w in self.engs:
            w.prog = []


class Rot:
    def __init__(self, items):
        self.items = items
        self.i = 0

    def get(self):
        it = self.items[self.i % len(self.items)]
        self.i += 1
        return it


def build(needed=None, stop=None):
    nc = bass.Bass("TRN2", target_bir_lowering=False)
    di = lambda n, s: nc.dram_tensor(n, s, F32, kind="ExternalInput").ap()
    x_d = di("x", [L, D])
    cT_d = di("cT", [128, 8])
    adaw_d = di("ada_w", [2, D, 3 * D])
    adab_d = di("ada_b", [2, 3 * D])
    gpre_d = di("g_pre", [2, D])
    gpost_d = di("g_post", [2, D])
    gkvc_d = di("gkv_col", [128, 8])
    awin_d = di("a_w_in", [D, 2 * D])
    aglu_d = di("a_w_glu", [D, D])
    aout_d = di("a_w_out", [D, D])
    bgluc_d = di("bglu_col", [128, 8])
    wk_d = di("w_k", [D, D])
    wv_d = di("w_v", [D, D])
    bwin_d = di("b_w_in", [D, 2 * D])
    bout_d = di("b_w_out", [D, D])
    lamre_d = di("lamre2", [128, 64])
    lamim_d = di("lamim2", [128, 64])
    logdt_d = di("logdt2", [128, 64])
    bst_d = di("bst", [128, 1024])
    bsw_d = di("bsw", [128, 1024])
    cst_d = di("cst", [128, 1024])
    csw_d = di("csw", [128, 1024])
    dcol_d = di("dcol", [128, 64])
    lqk_d = di("lqk", [1, 256])
    gsub_d = di("gsub", [1, 128])
    out_d = nc.dram_tensor("out", [L, D], F32, kind="ExternalOutput").ap()
    scr = lambda n, s, d: nc.dram_tensor(n, s, d, kind="Internal").ap()
    U_d = scr("U_s", [4, 128, 8192], BF16)
    zs_d = scr("zs_s", [128, 8, L], BF16)
    gel_d = scr("gel_s", [128, 8, L], BF16)
    if stop == "h1":
        h1_d = out_d
    else:
        h1_d = scr("h1_s", [L, D], F32)
    qT_d = scr("qT_s", [128, 8, L], BF16)
    zs1_d = scr("zs1_s", [L, D], BF16)
    kT_d = scr("kT_s", [128, 8, L], BF16)
    v_d = scr("v_s", [8, 128, 32, 129], BF16)
    bkT = [Buf() for _ in range(8)]
    bvd = [Buf() for _ in range(8)]
    bU = [Buf() for _ in range(4)]
    bzs = [Buf() for _ in range(4)]
    bgel = [Buf() for _ in range(4)]
    bh1 = [Buf() for _ in range(32)]
    bqT = [Buf() for _ in range(8)]
    bzs1 = [Buf() for _ in range(8)]

    top = ExitStack()
    with top:
        S = Sched(nc, top, needed)
        pe, act, dve, pool, sp = S.pe, S.act, S.dve, S.pool, S.sp

        uid = [0]

        def T(st, n, s, d=F32):
            uid[0] += 1
            return st.enter_context(nc.sbuf_tensor(f"sb{uid[0]}_{n}", s, d))

        def PS(st, n, s, d=F32):
            uid[0] += 1
            return st.enter_context(nc.psum_tensor(f"ps{uid[0]}_{n}", s, d))

        def mm(out, lhsT, rhs, start, stop_, reads, writes, sgc=False):
            return pe.op(lambda e: e.matmul(out, lhsT=lhsT, rhs=rhs, start=start, stop=stop_, skip_group_check=sgc), reads, writes)

        def trp(out, in_, ident, reads, writes):
            return pe.op(lambda e: e.transpose(out, in_, ident), reads, writes)

        def actf(out, in_, func, reads, writes, **kw):
            return act.op(lambda e: e.activation(out=out, in_=in_, func=func, **kw), reads, writes)

        def tt(eng, out, in0, in1, op, reads, writes):
            return eng.op(lambda e: e.tensor_tensor(out=out, in0=in0, in1=in1, op=op), reads, writes)

        def ts(eng, out, in0, s1, s2, op0, op1, reads, writes):
            if s2 is None:
                return eng.op(lambda e: e.tensor_scalar(out=out, in0=in0, scalar1=s1, scalar2=None, op0=op0), reads, writes)
            return eng.op(lambda e: e.tensor_scalar(out=out, in0=in0, scalar1=s1, scalar2=s2, op0=op0, op1=op1), reads, writes)

        def stt(eng, out, in0, scalar, in1, op0, op1, reads, writes):
            return eng.op(lambda e: e.scalar_tensor_tensor(out=out, in0=in0, scalar=scalar, in1=in1, op0=op0, op1=op1), reads, writes)

        def cp(eng, out, in_, reads, writes):
            if eng is act:
                return act.op(lambda e: e.copy(out=out, in_=in_), reads, writes)
            return eng.op(lambda e: e.tensor_copy(out=out, in_=in_), reads, writes)

        def recip(out, in_, reads, writes):
            return dve.op(lambda e: e.reciprocal(out=out, in_=in_), reads, writes)

        def mset(eng, ap, val, writes):
            return eng.op(lambda e: e.memset(ap, val), (), writes)

        def rot_sb(st, name, n, shape, dt=F32, dma=False):
            items = []
            for i in range(n):
                t = T(st, f"{name}{i}", shape, dt)
                if dma:
                    items.append((t, Buf(), S.dma_sem()))
                else:
                    items.append((t, Buf()))
            return Rot(items)

        def rot_ps(st, name, n, shape, dt=F32):
            return Rot([(PS(st, f"{name}{i}", shape, dt), Buf()) for i in range(n)])

        def rot_ps_sub(st, name, nbanks, nsub, subshape, dt=F32):
            items = []
            for i in range(nbanks):
                t = PS(st, f"{name}{i}", [128, nsub] + list(subshape), dt)
                for j in range(nsub):
                    items.append((t[:, j], Buf()))
            return Rot(items)

        def load_w_bf16(st_pool, dst, bdst, src_d, ncols, cast_engs):
            for k in range(8):
                stg, bs, ds = st_pool.get()
                sp.dma(stg[:, 0:ncols], src_d[k * 128:(k + 1) * 128, :], ds, writes=[bs])
                eng = cast_engs[k % len(cast_engs)]
                cp(eng, dst[:, k, :], stg[:, 0:ncols], [bs], [bdst])

        ident_b = T(top, "ident_b", [128, 128], BF16); b_identb = Buf()
        ident_f = T(top, "ident_f", [128, 128], F32); b_identf = Buf()
        cmask_b = T(top, "cmask_b", [128, 128], BF16); b_cmask = Buf()
        ones_row = T(top, "ones_row", [1, 128], F32); b_ones = Buf()
        eps_col = T(top, "eps_col", [128, 1], F32); b_eps = Buf()
        sgn_col = T(top, "sgn_col", [128, 1], F32); b_sgn = Buf()
        cols = T(top, "cols", [128, 48], F32); b_cols = Buf()
        GG = T(top, "GG", [128, 2, D], F32); b_GG = Buf()
        GS = T(top, "GS", [128, 128], F32); b_GS = Buf()
        neglam = T(top, "neglam", [128, 2], F32); b_neglam = Buf()

        mset(pool, ident_f[:], 1.0, [b_identf])
        pool.op(lambda e: e.affine_select(out=ident_f[:], in_=ident_f[:], pattern=[[-1, 128]], compare_op=ALU.is_equal,
                                          fill=0.0, base=0, channel_multiplier=1), [b_identf], [b_identf])
        cp(pool, ident_b[:], ident_f[:], [b_identf], [b_identb])
        mset(pool, cmask_b[:], 0.0, [b_cmask])
        pool.op(lambda e: e.affine_select(out=cmask_b[:], in_=cmask_b[:], pattern=[[1, 128]], compare_op=ALU.is_ge,
                                          fill=NEG, base=0, channel_multiplier=-1), [b_cmask], [b_cmask])
        mset(pool, ones_row[:], 1.0, [b_ones])
        mset(pool, eps_col[:], EPS, [b_eps])
        mset(pool, sgn_col[0:64, :], -1.0, [b_sgn])
        mset(pool, sgn_col[64:128, :], 1.0, [b_sgn])

        with ExitStack() as st:
            dq = S.dma_sem()
            ct = T(st, "ct", [128, 8]); b_ct = Buf()
            sc = T(st, "sc", [128, 8]); b_sc = Buf()
            rows = T(st, "rows", [1, 2, 3 * D]); b_rows = Buf()
            adab = T(st, "adab", [1, 2, 3 * D]); b_adab = Buf()
            gpr = T(st, "gpr", [1, 2, D]); b_gpr = Buf()
            gpo = T(st, "gpo", [1, 2, D]); b_gpo = Buf()
            arow = T(st, "arow", [1, 2, D]); b_arow = Buf()
            ggrow = T(st, "ggrow", [1, 2, D]); b_ggrow = Buf()
            lqk = T(st, "lqk", [1, 256]); b_lqk = Buf()
            gsr = T(st, "gsr", [1, 128]); b_gsr = Buf()
            sm = T(st, "sm", [1, 16]); b_sm = Buf()
            awp = rot_sb(st, "awst", 3, [128, 8, 512], F32, dma=True)
            pmod = rot_ps(st, "pmod", 2, [1, 512])
            pcol = PS(st, "pcol", [128, 64]); b_pcol = Buf()
            pbc = rot_ps(st, "pbc", 2, [128, 512])

            sp.dma(ct[:], cT_d, dq, writes=[b_ct])
            sp.dma(adab[0:1, 0, :], adab_d[0:1, :], dq, writes=[b_adab])
            sp.dma(adab[0:1, 1, :], adab_d[1:2, :], dq, writes=[b_adab])
            sp.dma(gpr[0:1, 0, :], gpre_d[0:1, :], dq, writes=[b_gpr])
            sp.dma(gpr[0:1, 1, :], gpre_d[1:2, :], dq, writes=[b_gpr])
            sp.dma(gpo[0:1, 0, :], gpost_d[0:1, :], dq, writes=[b_gpo])
            sp.dma(gpo[0:1, 1, :], gpost_d[1:2, :], dq, writes=[b_gpo])
            sp.dma(lqk[:], lqk_d, dq, writes=[b_lqk])
            sp.dma(gsr[:], gsub_d, dq, writes=[b_gsr])
            sp.dma(cols[:, 32:40], gkvc_d, dq, writes=[b_cols])
            sp.dma(cols[:, 40:48], bgluc_d, dq, writes=[b_cols])
            actf(sc[:], ct[:], AF.Silu, [b_ct], [b_sc])
            for l in range(2):
                for nt in range(6):
                    stg, bs, ds = awp.get()
                    sp.dma(stg[:], adaw_d[l].rearrange("(k p) n -> p k n", p=128)[:, :, nt * 512:(nt + 1) * 512], ds, writes=[bs])
                    pm, bpm = pmod.get()
                    for k in range(8):
                        mm(pm[:], sc[:, k:k + 1], stg[:, k, :], k == 0, k == 7, [b_sc, bs], [bpm])
                    tt(dve, rows[0:1, l, nt * 512:(nt + 1) * 512], pm[:], adab[0:1, l, nt * 512:(nt + 1) * 512], ALU.add,
                       [bpm, b_adab], [b_rows])
            for l in range(2):
                stt(dve, arow[0:1, l, :], rows[0:1, l, D:2 * D], 1.0, gpr[0:1, l, :], ALU.add, ALU.mult, [b_rows, b_gpr], [b_arow])
                tt(dve, ggrow[0:1, l, :], rows[0:1, l, 2 * D:3 * D], gpo[0:1, l, :], ALU.mult, [b_rows, b_gpo], [b_ggrow])
            for l in range(2):
                for which in range(2):
                    idx = l * 2 + which
                    for k in range(8):
                        src = arow[0:1, l, k * 128:(k + 1) * 128] if which == 0 else rows[0:1, l, k * 128:(k + 1) * 128]
                        c0 = (idx * 8 + k) * 2
                        mm(pcol[:, c0:c0 + 2], src, ones_row[0:1, 0:2], True, True, [b_arow, b_rows, b_ones], [b_pcol])
            cp(dve, cols[:, 0:32], pcol[:].rearrange("p (a two) -> p a two", two=2)[:, :, 0], [b_pcol], [b_cols])
            for l in range(2):
                for hf in range(2):
                    pb, bpb = pbc.get()
                    mm(pb[:], ones_row[0:1, 0:128], ggrow[0:1, l, hf * 512:(hf + 1) * 512], True, True, [b_ones, b_ggrow], [bpb])
                    cp(act, GG[:, l, hf * 512:(hf + 1) * 512], pb[:], [bpb], [b_GG])
            tt(dve, lqk[0:1, 0:64], lqk[0:1, 0:64], lqk[0:1, 64:128], ALU.mult, [b_lqk], [b_lqk])
            tt(dve, lqk[0:1, 128:192], lqk[0:1, 128:192], lqk[0:1, 192:256], ALU.mult, [b_lqk], [b_lqk])
            dve.op(lambda e: e.reduce_sum(out=sm[0:1, 0:1], in_=lqk[0:1, 0:64], axis=AX.X), [b_lqk], [b_sm])
            dve.op(lambda e: e.reduce_sum(out=sm[0:1, 1:2], in_=lqk[0:1, 128:192], axis=AX.X), [b_lqk], [b_sm])
            actf(sm[0:1, 2:4], sm[0:1, 0:2], AF.Exp, [b_sm], [b_sm])
            tt(dve, sm[0:1, 4:5], sm[0:1, 3:4], sm[0:1, 2:3], ALU.subtract, [b_sm], [b_sm])
            ts(dve, sm[0:1, 6:8], sm[0:1, 4:5].to_broadcast([1, 2]), -LAMBDA_INIT, None, ALU.add, None, [b_sm], [b_sm])
            pb, bpb = pbc.get()
            mm(pb[:, 0:2], ones_row[0:1, 0:128], sm[0:1, 6:8], True, True, [b_ones, b_sm], [bpb])
            cp(dve, neglam[:], pb[:, 0:2], [bpb], [b_neglam])
            pb, bpb = pbc.get()
            mm(pb[:, 0:128], ones_row[0:1, 0:128], gsr[0:1, :], True, True, [b_ones, b_gsr], [bpb])
            ts(dve, GS[:], pb[:, 0:128], 1.0 - LAMBDA_INIT, None, ALU.mult, None, [bpb], [b_GS])
            S.barrier(); S.flush()

        if stop == "C":
            return nc, S
        A0c, S0c, A1c, S1c, GKVc, BGLUc = (cols[:, 0:8], cols[:, 8:16], cols[:, 16:24], cols[:, 24:32], cols[:, 32:40], cols[:, 40:48])

        def prenormA(xt, bx, junk, ssq, xs_pool):
            jk, bj = junk.get()
            s_t, bs_ = ssq.get()
            mset(pool, s_t[:, 0:1], 0.0, [bs_])
            actf(jk[:], xt[:], AF.Square, [bx], [bj, bs_], accum_out=s_t[:, 0:1])
            actf(s_t[:, 1:2], s_t[:, 0:1], AF.Sqrt, [bs_, b_eps], [bs_], scale=1.0 / D, bias=eps_col[:, 0:1])
            recip(s_t[:, 2:3], s_t[:, 1:2], [bs_], [bs_])
            xs, bxs = xs_pool.get()
            ts(dve, xs[:], xt[:], s_t[:, 2:3], None, ALU.mult, None, [bx, bs_], [bxs])
            return xs, bxs

        def prenormB(xs, bxs, dsts, trps):
            tp, btp = trps.get()
            for k in range(8):
                trp(tp[:, k, :], xs[:, k * 128:(k + 1) * 128], ident_b[:], [bxs, b_identb], [btp])
            for (dfn, bd, scl, bia) in dsts:
                for k in range(8):
                    if bia is None:
                        ts(dve, dfn(k), tp[:, k, :], scl[:, k:k + 1], None, ALU.mult, None, [btp, b_cols], [bd])
                    else:
                        ts(dve, dfn(k), tp[:, k, :], scl[:, k:k + 1], bia[:, k:k + 1], ALU.mult, ALU.add, [btp, b_cols], [bd])

        def prenorm(st_res, xt, bx, dsts, junk, ssq, xs_pool, trps, ridx):
            xs, bxs = prenormA(xt, bx, junk, ssq, xs_pool)
            prenormB(xs, bxs, dsts, trps)

        def postnorm(po, bpo, l, xt, bx, junk, ssq, tbuf, obuf, dst_ap, bdst, dq_out):
            jk, bj = junk.get()
            s_t, bs_ = ssq.get()
            mset(pool, s_t[:, 0:1], 0.0, [bs_])
            bpo = bpo if isinstance(bpo, list) else [bpo]
            actf(jk[:], po, AF.Square, bpo, [bj, bs_], accum_out=s_t[:, 0:1])
            actf(s_t[:, 1:2], s_t[:, 0:1], AF.Sqrt, [bs_, b_eps], [bs_], scale=1.0 / D, bias=eps_col[:, 0:1])
            recip(s_t[:, 2:3], s_t[:, 1:2], [bs_], [bs_])
            tb, btb = tbuf.get()
            stt(dve, tb[:], po, s_t[:, 2:3], GG[:, l, :], ALU.mult, ALU.mult, bpo + [bs_, b_GG], [btb])
            ob, bob, dso = obuf.get()
            tt(pool, ob[:], tb[:], xt[:], ALU.add, [btb, bx], [bob])
            (dq_out if isinstance(dq_out, EngW) else sp).dma(dst_ap, ob[:], dso, reads=[bob], writes=[bdst])

        xrows0 = x_d.rearrange("(b p j) d -> b j p d", p=128, j=8)
        h1rows0 = h1_d.rearrange("(b p j) d -> b j p d", p=128, j=8)
        with ExitStack() as st:
            wst = rot_sb(st, "wst", 2, [128, 2048], F32, dma=True)
            w_in = T(st, "w_in", [128, 8, 2048], BF16); b_win = Buf()
            xp_ = rot_sb(st, "xt", 4, [128, D], F32, dma=True)
            junk = rot_sb(st, "junk", 2, [128, D], BF16)
            ssq = rot_sb(st, "ssq", 4, [128, 4], F32)
            xs_pool = rot_sb(st, "xs", 3, [128, D], BF16)
            hT = rot_sb(st, "hT", 2, [128, 8, 1024], BF16)
            utm = rot_sb(st, "utm", 1, [128, 64, 8, 16], BF16)
            ublk = rot_sb(st, "ublk", 1, [128, 64, 128], BF16, dma=True)
            zsT = rot_sb(st, "zsT", 1, [128, 8, 1024], BF16, dma=True)
            trps = rot_ps(st, "trps", 2, [128, 8, 128], BF16)
            pacc = rot_ps(st, "pacc", 4, [128, 512])
            dq_o = S.dma_sem(); dq_o2 = S.dma_sem()
            load_w_bf16(wst, w_in, b_win, awin_d, 2048, [pool])
            ne = [0]
            hslots = [hT.get() for _ in range(2)]
            hbufs = [[Buf() for _ in range(8)] for _ in range(2)]
            cur_u = {}

            xsd = {}

            def preA(t):
                b, j = divmod(t, 8)
                xt, bx, dsx = xp_.get()
                sp.dma(xt[:], xrows0[b, j], dsx, writes=[bx])
                xsd[t] = prenormA(xt, bx, junk, ssq, xs_pool)

            def preB(t):
                b, j = divmod(t, 8)
                h_t = hslots[b % 2][0]
                bhj = hbufs[b % 2][j]
                xs, bxs = xsd.pop(t)
                prenormB(xs, bxs, [(lambda k, h_t=h_t, j=j: h_t[:, k, j * 128:(j + 1) * 128], bhj, A0c, S0c)], trps)

            def proj(t):
                b, j = divmod(t, 8)
                h_t = hslots[b % 2][0]
                bhj = hbufs[b % 2][j]
                if j == 0:
                    cur_u[b] = utm.get()
                u_t, bu = cur_u[b]
                for nt in range(2):
                    pa, bpa = pacc.get()
                    for k in range(8):
                        mm(pa[:], h_t[:, k, j * 128:(j + 1) * 128], w_in[:, k, nt * 512:(nt + 1) * 512], k == 0, k == 7, [bhj, b_win], [bpa])
                    cp(act, u_t[:, nt * 32:(nt + 1) * 32, j, :], pa[:].rearrange("p (g c) -> p g c", c=16), [bpa], [bu])
                    ne[0] += 1

            def blockend(b):
                h_t = hslots[b % 2][0]
                bhl = hbufs[b % 2]
                u_t, bu = cur_u[b]
                z_t, bz, dsz_ = zsT.get()
                for zc in range(8):
                    for hf in range(2):
                        pa, bpa = pacc.get()
                        for k in range(8):
                            mm(pa[:], w_in[:, k, 1024 + zc * 128:1024 + (zc + 1) * 128], h_t[:, k, hf * 512:(hf + 1) * 512], k == 0, k == 7,
                               bhl[hf * 4:(hf + 1) * 4] + [b_win], [bpa])
                        actf(z_t[:, zc, hf * 512:(hf + 1) * 512], pa[:], AF.Silu, [bpa], [bz])
                sp.dma(zs_d[:, :, b * 1024:(b + 1) * 1024], z_t[:], dsz_, reads=[bz], writes=[bzs[b]])
                ub, bub, dsub = ublk.get()
                for g8 in range(8):
                    tp, btp = trps.get()
                    for gl in range(8):
                        g = g8 * 8 + gl
                        trp(tp[:, gl, :], u_t[:, g].rearrange("p i c -> p (i c)"), ident_b[:], [bu, b_identb], [btp])
                    cp(dve if g8 % 2 == 0 else act, ub[:, g8 * 8:(g8 + 1) * 8, :], tp[:], [btp], [bub])
                sp.dma(U_d[b].rearrange("p (g c) -> p g c", c=128), ub[:], dsub, reads=[bub], writes=[bU[b]])

            preA(0); preA(1); preB(0)
            for t in range(32):
                if t + 2 < 32:
                    preA(t + 2)
                if t + 1 < 32:
                    preB(t + 1)
                proj(t)
                if t % 8 == 7:
                    blockend(t // 8)
            S.barrier(); S.flush()

        if stop == "2a":
            return nc, S
        with ExitStack() as st:
            WXA = T(st, "WXA", [128, 64, 128], BF16); b_wxa = Buf()
            WXB = T(st, "WXB", [128, 64, 128], BF16); b_wxb = Buf()
            WY1 = T(st, "WY1", [128, 64, 128], BF16); b_wy1 = Buf()
            WY2 = T(st, "WY2", [128, 64, 128], BF16); b_wy2 = Buf()
            b_tc = Buf(); b_ts = Buf()
            RHO = T(st, "RHO", [128, 64], F32); b_rho = Buf()
            E1c = T(st, "E1c", [128, 64]); E1s = T(st, "E1s", [128, 64]); b_e1 = Buf()
            PSW = T(st, "PSW", [128, 128], F32); b_psw = Buf()
            ts(pool, PSW[:, 0:64], ident_f[:, 64:128], -1.0, None, ALU.mult, None, [b_identf], [b_psw])
            cp(pool, PSW[:, 64:128], ident_f[:, 0:64], [b_identf], [b_psw])
            with ExitStack() as sp1:
                dq = S.dma_sem()
                lamre = T(sp1, "lamre", [128, 64]); lamim = T(sp1, "lamim", [128, 64]); dt_ = T(sp1, "dt", [128, 64])
                b_in = Buf()
                sp.dma(lamre[:], lamre_d, dq, writes=[b_in])
                sp.dma(lamim[:], lamim_d, dq, writes=[b_in])
                sp.dma(dt_[:], logdt_d, dq, writes=[b_in])
                Dcol = T(sp1, "Dcol", [128, 64]); b_dcol = b_in
                sp.dma(Dcol[:], dcol_d, dq, writes=[b_dcol])
                LR = T(sp1, "LR", [128, 9, 64]); LI = T(sp1, "LI", [128, 9, 64])
                MR = T(sp1, "MR", [128, 8, 64]); MI = T(sp1, "MI", [128, 8, 64])
                LIs = T(sp1, "LIs", [128, 9, 64]); LRn = T(sp1, "LRn", [128, 9, 64]); LIn = T(sp1, "LIn", [128, 9, 64])
                MRn = T(sp1, "MRn", [128, 8, 64]); MIn = T(sp1, "MIn", [128, 8, 64])
                b_pw = Buf()
                w = T(sp1, "wk_", [128, 12, 64]); b_w = Buf()
                wi = T(sp1, "wi_", [128, 64], I32)
                FRE = T(sp1, "FRE", [128, 64]); FIMs = T(sp1, "FIMs", [128, 64]); FIMn = T(sp1, "FIMn", [128, 64]); b_f = Buf()
                mask = T(sp1, "mask", [128, 128]); b_mask = Buf()
                mset(pool, mask[:], 1.0, [b_mask])
                pool.op(lambda e: e.affine_select(out=mask[:].rearrange("p (j c) -> p j c", c=16), in_=mask[:].rearrange("p (j c) -> p j c", c=16),
                                                  pattern=[[16, 8], [0, 16]], compare_op=ALU.is_ge, fill=0.0, base=15, channel_multiplier=-1),
                        [b_mask], [b_mask])
                R = [b_in, b_w]
                actf(dt_[:], dt_[:], AF.Exp, [b_in], [b_in])
                tt(dve, w[:, 0, :], lamre[:], dt_[:], ALU.mult, R, [b_w])
                tt(dve, w[:, 1, :], lamim[:], dt_[:], ALU.mult, R, [b_w])

                def sin_of(dst, shift):
                    ts(dve, w[:, 2, :], w[:, 1, :], shift, 1.0 / (2 * math.pi), ALU.add, ALU.mult, [b_w], [b_w])
                    cp(dve, wi[:], w[:, 2, :], [b_w], [b_w])
                    cp(dve, w[:, 2, :], wi[:], [b_w], [b_w])
                    stt(dve, w[:, 2, :], w[:, 2, :], -2 * math.pi, w[:, 1, :], ALU.mult, ALU.add, [b_w], [b_w])
                    ts(dve, w[:, 3, :], w[:, 2, :], shift, None, ALU.add, None, [b_w], [b_w])
                    ts(dve, w[:, 2, :], w[:, 3, :], math.pi, -2 * math.pi, ALU.is_gt, ALU.mult, [b_w], [b_w])
                    tt(dve, w[:, 3, :], w[:, 3, :], w[:, 2, :], ALU.add, [b_w], [b_w])
                    actf(dst, w[:, 3, :], AF.Sin, [b_w], [b_w])

                sin_of(w[:, 4, :], 0.0)
                sin_of(w[:, 5, :], 0.5 * math.pi)
                actf(w[:, 6, :], w[:, 0, :], AF.Exp, [b_w], [b_w])
                actf(w[:, 7, :], w[:, 0, :], AF.Exp, [b_w], [b_w], scale=-1.0)
                actf(RHO[:], w[:, 0, :], AF.Exp, [b_w], [b_rho], scale=8.0)
                actf(w[:, 8, :], w[:, 0, :], AF.Exp, [b_w], [b_w], scale=-8.0)
                P_ = [b_w, b_pw]
                mset(pool, LR[:, 0, :], 1.0, [b_pw]); mset(pool, LI[:, 0, :], 0.0, [b_pw])
                mset(pool, MR[:, 0, :], 1.0, [b_pw]); mset(pool, MI[:, 0, :], 0.0, [b_pw])
                tt(dve, LR[:, 1, :], w[:, 6, :], w[:, 5, :], ALU.mult, P_, [b_pw])
                tt(dve, LI[:, 1, :], w[:, 6, :], w[:, 4, :], ALU.mult, P_, [b_pw])
                tt(dve, MR[:, 1, :], w[:, 7, :], w[:, 5, :], ALU.mult, P_, [b_pw])
                stt(dve, MI[:, 1, :], w[:, 7, :], -1.0, w[:, 4, :], ALU.mult, ALU.mult, P_, [b_pw])

                def cpow(XR, XI, k):
                    tt(dve, w[:, 9, :], XR[:, k, :], XR[:, 1, :], ALU.mult, P_, [b_w])
                    tt(dve, w[:, 10, :], XI[:, k, :], XI[:, 1, :], ALU.mult, P_, [b_w])
                    tt(dve, XR[:, k + 1, :], w[:, 9, :], w[:, 10, :], ALU.subtract, P_, [b_pw])
                    tt(dve, w[:, 9, :], XR[:, k, :], XI[:, 1, :], ALU.mult, P_, [b_w])
                    tt(dve, w[:, 10, :], XI[:, k, :], XR[:, 1, :], ALU.mult, P_, [b_w])
                    tt(dve, XI[:, k + 1, :], w[:, 9, :], w[:, 10, :], ALU.add, P_, [b_pw])

                for k in range(1, 8):
                    cpow(LR, LI, k)
                for k in range(1, 7):
                    cpow(MR, MI, k)
                ts(dve, LIs[:], LI[:], sgn_col[:, 0:1], None, ALU.mult, None, [b_pw, b_sgn], [b_pw])
                ts(dve, LRn[:], LR[:], sgn_col[:, 0:1], -1.0, ALU.mult, ALU.mult, [b_pw, b_sgn], [b_pw])
                ts(dve, LIn[:], LI[:], -1.0, None, ALU.mult, None, [b_pw], [b_pw])
                ts(dve, MRn[:], MR[:], sgn_col[:, 0:1], -1.0, ALU.mult, ALU.mult, [b_pw, b_sgn], [b_pw])
                ts(dve, MIn[:], MI[:], -1.0, None, ALU.mult, None, [b_pw], [b_pw])
                F_ = [b_in, b_w, b_pw, b_f]
                ts(dve, w[:, 2, :], LR[:, 1, :], -1.0, None, ALU.add, None, F_, [b_w])
                tt(dve, w[:, 3, :], lamre[:], lamre[:], ALU.mult, F_, [b_w])
                tt(dve, w[:, 9, :], lamim[:], lamim[:], ALU.mult, F_, [b_w])
                tt(dve, w[:, 3, :], w[:, 3, :], w[:, 9, :], ALU.add, F_, [b_w])
                recip(w[:, 3, :], w[:, 3, :], [b_w], [b_w])
                tt(dve, w[:, 9, :], w[:, 2, :], lamre[:], ALU.mult, F_, [b_w])
                tt(dve, w[:, 10, :], LI[:, 1, :], lamim[:], ALU.mult, F_, [b_w])
                tt(dve, w[:, 9, :], w[:, 9, :], w[:, 10, :], ALU.add, F_, [b_w])
                tt(dve, FRE[:], w[:, 9, :], w[:, 3, :], ALU.mult, F_, [b_f])
                tt(dve, w[:, 9, :], LI[:, 1, :], lamre[:], ALU.mult, F_, [b_w])
                tt(dve, w[:, 10, :], w[:, 2, :], lamim[:], ALU.mult, F_, [b_w])
                tt(dve, w[:, 9, :], w[:, 9, :], w[:, 10, :], ALU.subtract, F_, [b_w])
                tt(dve, w[:, 9, :], w[:, 9, :], w[:, 3, :], ALU.mult, F_, [b_w])
                ts(dve, FIMs[:], w[:, 9, :], sgn_col[:, 0:1], None, ALU.mult, None, [b_w, b_sgn], [b_f])
                ts(dve, FIMn[:], FIMs[:], -1.0, None, ALU.mult, None, [b_f], [b_f])
                tt(dve, E1c[:], LR[:, 8, :], w[:, 8, :], ALU.mult, P_, [b_e1])
                tt(dve, E1s[:], LI[:, 8, :], w[:, 8, :], ALU.mult, P_, [b_e1])

                def bc(a):
                    return a.unsqueeze(2).to_broadcast([128, 64, 16])

                with ExitStack() as sp2:
                    Bst = T(sp2, "Bst", [128, 64, 16]); Bsw = T(sp2, "Bsw", [128, 64, 16])
                    Cst = T(sp2, "Cst", [128, 64, 16]); Csw = T(sp2, "Csw", [128, 64, 16]); b_bc = Buf()
                    sp.dma(Bst[:], bst_d.rearrange("p (g c) -> p g c", c=16), dq, writes=[b_bc])
                    sp.dma(Bsw[:], bsw_d.rearrange("p (g c) -> p g c", c=16), dq, writes=[b_bc])
                    sp.dma(Cst[:], cst_d.rearrange("p (g c) -> p g c", c=16), dq, writes=[b_bc])
                    sp.dma(Csw[:], csw_d.rearrange("p (g c) -> p g c", c=16), dq, writes=[b_bc])
                    Bbst = T(sp2, "Bbst", [128, 64, 16]); Bbsw = T(sp2, "Bbsw", [128, 64, 16]); b_bb = Buf()
                    t1p = rot_sb(sp2, "t1p", 2, [128, 64, 16]); t2p = rot_sb(sp2, "t2p", 2, [128, 64, 16])

                    def lin2(out, bo, a0, s0, a1, s1, rd):
                        t1, bt1 = t1p.get(); t2, bt2 = t2p.get()
                        tt(dve, t1[:], a0, bc(s0), ALU.mult, rd, [bt1])
                        tt(pool, t2[:], a1, bc(s1), ALU.mult, rd, [bt2])
                        tt(dve if lin2.n % 2 == 0 else pool, out, t1[:], t2[:], ALU.add, [bt1, bt2], [bo])
                        lin2.n += 1
                    lin2.n = 0
                    rdB = [b_bc, b_f, b_pw, b_bb]
                    lin2(Bbst[:], b_bb, Bst[:], FRE[:], Bsw[:], FIMs[:], [b_bc, b_f])
                    lin2(Bbsw[:], b_bb, Bsw[:], FRE[:], Bst[:], FIMn[:], [b_bc, b_f])
                    wy1v = WY1[:].rearrange("p g (j c) -> p g j c", c=16)
                    for j in range(8):
                        lin2(wy1v[:, :, j, :], b_wy1, Cst[:], LRn[:, j + 1, :], Csw[:], LIn[:, j + 1, :], rdB)
                    pxt = rot_ps(sp2, "pxt", 2, [128, 4, 128])
                    pw2 = rot_ps(sp2, "pw2", 2, [128, 4, 128])
                    tmpw = rot_sb(sp2, "tmpw", 2, [128, 4, 128])
                    P1 = T(sp2, "P1", [128, 32, 8, 16]); b_p1 = Buf()
                    P2 = T(sp2, "P2", [128, 32, 8, 16]); b_p2 = Buf()
                    for hf in range(2):
                        gs = slice(hf * 32, (hf + 1) * 32)

                        def bch(a):
                            return a[:, gs].unsqueeze(2).to_broadcast([128, 32, 16])

                        def lin2h(out, bo, a0, s0, a1, s1, rd):
                            t1, bt1 = t1p.get(); t2, bt2 = t2p.get()
                            tt(dve, t1[:, 0:32, :], a0, bch(s0), ALU.mult, rd, [bt1])
                            tt(pool, t2[:, 0:32, :], a1, bch(s1), ALU.mult, rd, [bt2])
                            tt(dve if lin2.n % 2 == 0 else pool, out, t1[:, 0:32, :], t2[:, 0:32, :], ALU.add, [bt1, bt2], [bo])
                            lin2.n += 1
                        for i in range(8):
                            k = 7 - i
                            lin2h(P1[:, :, i, :], b_p1, Bbst[:, gs, :], LR[:, k, :], Bbsw[:, gs, :], LIs[:, k, :], rdB)
                            lin2h(P2[:, :, i, :], b_p2, Cst[:, gs, :], MRn[:, k, :], Csw[:, gs, :], MIn[:, k, :], rdB)
                        for g4 in range(8):
                            px, bpx = pxt.get()
                            p2_, bp2 = pw2.get()
                            for gl in range(4):
                                gi = g4 * 4 + gl
                                trp(px[:, gl, :], P1[:, gi].rearrange("p i c -> p (i c)"), ident_f[:], [b_p1, b_identf], [bpx])
                                mm(p2_[:, gl, :], P1[:, gi].rearrange("p i c -> p (i c)"), P2[:, gi].rearrange("p i c -> p (i c)"), True, True,
                                   [b_p1, b_p2], [bp2])
                            g0 = hf * 32 + g4 * 4
                            cp(act, WXA[:, g0:g0 + 4, :], px[:], [bpx], [b_wxa])
                            cp(act, WXB[:, g0:g0 + 4, 0:64], px[:, :, 64:128], [bpx], [b_wxb])
                            actf(WXB[:, g0:g0 + 4, 64:128], px[:, :, 0:64], AF.Copy, [bpx], [b_wxb], scale=-1.0)
                            tw, btw = tmpw.get()
                            tt(dve, tw[:], p2_[:], mask[:].unsqueeze(1).to_broadcast([128, 4, 128]), ALU.mult, [bp2, b_mask], [btw])
                            for gl in range(4):
                                g = g0 + gl
                                stt(dve, WY2[:, g, :], ident_f[:], Dcol[:, g:g + 1], tw[:, gl, :], ALU.mult, ALU.add,
                                    [b_identf, b_dcol, btw], [b_wy2])
                    S.barrier(); S.flush()
                S.barrier(); S.flush()
            if stop == "P1":
                return nc, S
            TC = T(st, "TC", [128, 64, 128], BF16)
            TS_ = T(st, "TS", [128, 64, 128], BF16)
            with ExitStack() as sp3:
                TCf = T(sp3, "TCf", [128, 32, 128]); TSf = T(sp3, "TSf", [128, 32, 128]); b_tf = Buf()
                ta = T(sp3, "ta", [128, 32, 64]); tb_ = T(sp3, "tb", [128, 32, 64]); b_ta = Buf(); b_tb = Buf()
                for hf in range(2):
                    gs = slice(hf * 32, (hf + 1) * 32)
                    cp(dve, TCf[:, :, 0], E1c[:, gs], [b_e1], [b_tf])
                    cp(dve, TSf[:, :, 0], E1s[:, gs], [b_e1], [b_tf])
                    n = 1
                    while n < 128:
                        cn = TCf[:, :, n - 1:n].to_broadcast([128, 32, n])
                        sn = TSf[:, :, n - 1:n].to_broadcast([128, 32, n])
                        tt(dve, ta[:, :, 0:n], TCf[:, :, 0:n], cn, ALU.mult, [b_tf], [b_ta])
                        tt(pool, tb_[:, :, 0:n], TSf[:, :, 0:n], sn, ALU.mult, [b_tf], [b_tb])
                        tt(dve, ta[:, :, 0:n], ta[:, :, 0:n], tb_[:, :, 0:n], ALU.subtract, [b_ta, b_tb], [b_ta])
                        tt(pool, tb_[:, :, 0:n], TCf[:, :, 0:n], sn, ALU.mult, [b_tf, b_ta], [b_tb])
                        cp(dve, TCf[:, :, n:2 * n], ta[:, :, 0:n], [b_ta], [b_tf])
                        tt(dve, ta[:, :, 0:n], TSf[:, :, 0:n], cn, ALU.mult, [b_tf], [b_ta])
                        tt(pool, TSf[:, :, n:2 * n], ta[:, :, 0:n], tb_[:, :, 0:n], ALU.add, [b_ta, b_tb], [b_tf])
                        n *= 2
                    cp(dve, TC[:, gs, :], TCf[:], [b_tf], [b_tc])
                    cp(pool, TS_[:, gs, :], TSf[:], [b_tf], [b_ts])
                S.barrier(); S.flush()

            if stop == "P":
                return nc, S
            with ExitStack() as s2:
                ubk = rot_sb(s2, "ubk", 1, [128, 64, 128], BF16, dma=True)
                xp = T(s2, "xp", [128, 64, 129], BF16); b_xp = [Buf() for _ in range(64)]
                carry = T(s2, "carry", [128, 64], F32); b_carry = [Buf() for _ in range(64)]
                gel_tm = rot_sb(s2, "geltm", 1, [128, 8, 1024], BF16)
                gelT = rot_sb(s2, "gelT", 1, [128, 8, 1024], BF16, dma=True)
                t1p = rot_sb(s2, "st1", 3, [128, 128]); t2p = rot_sb(s2, "st2", 3, [128, 128]); vp = rot_sb(s2, "sv", 5, [128, 128])
                Wp = rot_sb(s2, "sW", 5, [128, 128]); t3p = rot_sb(s2, "st3", 5, [128, 128]); t4p = rot_sb(s2, "st4", 3, [128, 128])
                pab = rot_ps(s2, "pab", 3, [128, 2, 128])
                psw_ = rot_ps(s2, "psw", 2, [128, 128])
                py = rot_ps(s2, "py", 2, [128, 128])
                ptr = rot_ps(s2, "ptr", 1, [128, 8, 128], BF16)
                rhop = rot_sb(s2, "rhot", 5, [128, 128])
                ones128 = T(s2, "ones128", [128, 128]); b_ones128 = Buf()
                mset(pool, ones128[:], 1.0, [b_ones128])
                lvl = {"2b1": 1, "2b2": 2, "2b3": 3}.get(stop, 4)
                mset(pool, carry[:], 0.0, b_carry)
                for b in range(4):
                    ub, bub, dsu = ubk.get()
                    sp.dma(ub[:], U_d[b].rearrange("p (g c) -> p g c", c=128), dsu, reads=[bU[b]], writes=[bub])
                    cp(pool, xp[:, :, 0], carry[:], b_carry, b_xp)
                    g_t, bgt = gel_tm.get()
                    G = {}

                    def s0(g):
                        ab, bab = pab.get()
                        mm(ab[:, 0, :], WXA[:, g, :], ub[:, g, :], True, True, [b_wxa, bub], [bab])
                        mm(ab[:, 1, :], WXB[:, g, :], ub[:, g, :], True, True, [b_wxb, bub], [bab])
                        G[g] = {"ab": (ab, bab)}

                    def s1(g):
                        ab, bab = G[g]["ab"]
                        t1, bt1 = t1p.get(); t2, bt2 = t2p.get()
                        tt(dve, t1[:], ab[:, 0, :], TC[:, g, :], ALU.mult, [bab, b_tc], [bt1])
                        tt(dve, t2[:], ab[:, 1, :], TS_[:, g, :], ALU.mult, [bab, b_ts], [bt2])
                        rt, brt = rhop.get()
                        actf(rt[:], ones128[:], AF.Copy, [b_ones128, b_rho], [brt], scale=RHO[:, g:g + 1])
                        G[g].update(t1=(t1, bt1), t2=(t2, bt2), rt=(rt, brt))

                    def s2(g):
                        (t1, bt1), (t2, bt2) = G[g]["t1"], G[g]["t2"]
                        v, bv = vp.get()
                        tt(pool, v[:], t1[:], t2[:], ALU.add, [bt1, bt2], [bv])
                        G[g]["v"] = (v, bv)

                    def s3(g):
                        (v, bv), (rt, brt) = G[g]["v"], G[g]["rt"]
                        W_, bW = Wp.get()
                        dve.op(lambda e, W_=W_, v=v, g=g, rt=rt: e.tensor_tensor_scan(out=W_[:], data0=rt[:], data1=v[:],
                                                                                     initial=carry[:, g:g + 1], op0=ALU.mult, op1=ALU.add),
                               [bv, brt, b_carry[g]], [bW])
                        G[g]["W"] = (W_, bW)

                    def s4(g):
                        W_, bW = G[g]["W"]
                        ws, bws = psw_.get()
                        mm(ws[:], PSW[:], W_[:], True, True, [b_psw, bW], [bws])
                        t3, bt3 = t3p.get()
                        tt(pool, t3[:], W_[:], TC[:, g, :], ALU.mult, [bW, b_tc], [bt3])
                        G[g].update(ws=(ws, bws), t3=(t3, bt3))

                    def s5(g):
                        ws, bws = G[g]["ws"]
                        t4, bt4 = t4p.get()
                        tt(dve, t4[:], ws[:], TS_[:, g, :], ALU.mult, [bws, b_ts], [bt4])
                        G[g]["t4"] = (t4, bt4)

                    def s6(g):
                        (t3, bt3), (t4, bt4) = G[g]["t3"], G[g]["t4"]
                        tt(pool, xp[:, g, 1:129], t3[:], t4[:], ALU.add, [bt3, bt4], [b_xp[g]])
                        actf(carry[:, g:g + 1], t3[:, 127:128], AF.Identity, [bt3, bt4], [b_carry[g]], bias=t4[:, 127:128])

                    def s7(g):
                        yb, byb = py.get()
                        mm(yb[:], xp[:, g, 0:128], WY1[:, g, :], True, False, [b_xp[g], b_wy1], [byb])
                        mm(yb[:], ub[:, g, :], WY2[:, g, :], False, True, [bub, b_wy2], [byb])
                        G[g]["y"] = (yb, byb)

                    def s8(g):
                        yb, byb = G.pop(g)["y"]
                        actf(g_t[:, :, g * 16:(g + 1) * 16], yb[:].rearrange("p (j c) -> p j c", c=16), AF.Gelu_apprx_tanh, [byb], [bgt])

                    stages = [s0, s1, s2, s3, s4, s5, s6, s7, s8]
                    for i in range(64 + 8):
                        for sidx in range(8, -1, -1):
                            g = i - sidx
                            if 0 <= g < 64:
                                stages[sidx](g)
                    if lvl < 4:
                        continue
                    gT, bgT, dsgT = gelT.get()
                    for ncu in range(8):
                        tp, btp = ptr.get()
                        for j in range(8):
                            trp(tp[:, j, :], g_t[:, j, ncu * 128:(ncu + 1) * 128], ident_b[:], [bgt, b_identb], [btp])
                        cp(dve if ncu % 2 == 0 else act, gT[:, ncu, :], tp[:].rearrange("p j m -> p (j m)"), [btp], [bgT])
                    sp.dma(gel_d[:, :, b * 1024:(b + 1) * 1024], gT[:], dsgT, reads=[bgT], writes=[bgel[b]])
                S.barrier(); S.flush()

        if stop in ("2b", "2b1", "2b2", "2b3"):
            return nc, S
        with ExitStack() as st:
            wst = rot_sb(st, "wst", 2, [128, 1024], F32, dma=True)
            w_glu = T(st, "w_glu", [128, 8, 1024], BF16); b_wglu = Buf()
            w_out = T(st, "w_out", [128, 8, 1024], BF16); b_wout = Buf()
            load_w_bf16(wst, w_glu, b_wglu, aglu_d, 1024, [pool, dve])
            load_w_bf16(wst, w_out, b_wout, aout_d, 1024, [pool, dve])
            gin = rot_sb(st, "gin", 2, [128, 8, 1024], BF16, dma=True)
            zin = rot_sb(st, "zin", 2, [128, 8, 1024], BF16, dma=True)
            y3 = rot_sb(st, "y3", 2, [128, 8, 1024], BF16)
            sigp = rot_sb(st, "sig", 3, [128, 512], BF16)
            xp_ = rot_sb(st, "xt", 4, [128, D], F32, dma=True)
            junk = rot_sb(st, "junk", 2, [128, D], BF16)
            ssq = rot_sb(st, "ssq", 4, [128, 4], F32)
            tbuf = rot_sb(st, "tbuf", 2, [128, D], F32)
            obuf = rot_sb(st, "obuf", 2, [128, D], F32, dma=True)
            pg = rot_ps(st, "pg", 3, [128, 512])
            po_ = rot_ps(st, "po", 2, [128, 1024])
            dq_o = S.dma_sem()
            ycur = {}

            def glu(b):
                gi, bgi, dsg = gin.get()
                zi, bzi, dsz = zin.get()
                sp.dma(gi[:], gel_d[:, :, b * 1024:(b + 1) * 1024], dsg, reads=[bgel[b]], writes=[bgi])
                sp.dma(zi[:], zs_d[:, :, b * 1024:(b + 1) * 1024], dsz, reads=[bzs[b]], writes=[bzi])
                y3t, by3 = y3.get()
                ycur[b] = (y3t, by3)
                for ec in range(8):
                    for hf in range(2):
                        p_, bp_ = pg.get()
                        for k in range(8):
                            mm(p_[:], w_glu[:, k, ec * 128:(ec + 1) * 128], gi[:, k, hf * 512:(hf + 1) * 512], k == 0, k == 7, [b_wglu, bgi], [bp_])
                        sg, bsg = sigp.get()
                        actf(sg[:], p_[:], AF.Sigmoid, [bp_, b_cols], [bsg], bias=BGLUc[:, ec:ec + 1])
                        tt(dve, sg[:], sg[:], gi[:, ec, hf * 512:(hf + 1) * 512], ALU.mult, [bsg, bgi], [bsg])
                        tt(pool, y3t[:, ec, hf * 512:(hf + 1) * 512], sg[:], zi[:, ec, hf * 512:(hf + 1) * 512], ALU.mult, [bsg, bzi], [by3])

            def outp(b):
                y3t, by3 = ycur.pop(b)
                for j in range(8):
                    xt, bx, dsx = xp_.get()
                    sp.dma(xt[:], xrows0[b, j], dsx, writes=[bx])
                    po, bpo = po_.get()
                    for nt in range(2):
                        for k in range(8):
                            mm(po[:, nt * 512:(nt + 1) * 512], y3t[:, k, j * 128:(j + 1) * 128], w_out[:, k, nt * 512:(nt + 1) * 512], k == 0, k == 7,
                               [by3, b_wout], [bpo])
                    postnorm(po[:], bpo, 0, xt, bx, junk, ssq, tbuf, obuf, h1rows0[b, j], bh1[b * 8 + j], dq_o)

            glu(0)
            for b in range(4):
                if b + 1 < 4:
                    glu(b + 1)
                outp(b)
            S.barrier(); S.flush()

        if stop == "h1":
            return nc, S

        def h1_deps(tt_):
            b = (tt_ * 128) // 1024
            return [bh1[b * 8 + j] for j in range(8)]

        with ExitStack() as st:
            wst = rot_sb(st, "wst", 2, [128, 2048], F32, dma=True)
            w_in = T(st, "w_in1", [128, 8, 2048], BF16); b_win = Buf()
            load_w_bf16(wst, w_in, b_win, bwin_d, 2048, [pool])
            xp_ = rot_sb(st, "xt", 4, [128, D], F32, dma=True)
            junk = rot_sb(st, "junk", 2, [128, D], BF16)
            ssq = rot_sb(st, "ssq", 4, [128, 4], F32)
            xs_pool = rot_sb(st, "xs", 3, [128, D], BF16)
            hT = rot_sb(st, "hT", 2, [128, 8, 512], BF16)
            qblk = rot_sb(st, "qblk", 2, [128, 8, 512], BF16, dma=True)
            zblk = rot_sb(st, "zblk", 2, [128, 4, 1024], BF16, dma=True)
            trps = rot_ps(st, "trps", 2, [128, 8, 128], BF16)
            pacc = rot_ps(st, "pacc", 4, [128, 512])
            dq_o = S.dma_sem(); dq_o2 = S.dma_sem()
            ne = [0]
            hcur = {}

            def pre(grp):
                h_t, bh = hT.get()
                hcur[grp] = (h_t, bh)
                xa = {}

                def A(tl):
                    tt_ = grp * 4 + tl
                    xt, bx, dsx = xp_.get()
                    sp.dma(xt[:], h1_d[tt_ * 128:(tt_ + 1) * 128, :], dsx, reads=h1_deps(tt_), writes=[bx])
                    xa[tl] = prenormA(xt, bx, junk, ssq, xs_pool)

                def B(tl):
                    xs, bxs = xa.pop(tl)
                    prenormB(xs, bxs, [(lambda k, h_t=h_t, tl=tl: h_t[:, k, tl * 128:(tl + 1) * 128], bh, A1c, S1c)], trps)

                A(0); A(1); B(0); A(2); B(1); A(3); B(2); B(3)

            def proj(grp):
                h_t, bh = hcur.pop(grp)
                qb_, bqb, dsqb = qblk.get()
                for h in range(8):
                    pa, bpa = pacc.get()
                    for k in range(8):
                        mm(pa[:], w_in[:, k, h * 128:(h + 1) * 128], h_t[:, k, :], k == 0, k == 7, [b_win, bh], [bpa])
                    cp(act, qb_[:, h, :], pa[:], [bpa], [bqb])
                    ne[0] += 1
                pool.dma(qT_d[:, :, grp * 512:(grp + 1) * 512], qb_[:], dsqb, reads=[bqb], writes=[bqT[grp]])
                zb_, bzb, dszb = zblk.get()
                for tl in range(4):
                    for nt in range(2):
                        pa, bpa = pacc.get()
                        for k in range(8):
                            mm(pa[:], h_t[:, k, tl * 128:(tl + 1) * 128], w_in[:, k, 1024 + nt * 512:1024 + (nt + 1) * 512], k == 0, k == 7,
                               [b_win, bh], [bpa])
                        actf(zb_[:, tl, nt * 512:(nt + 1) * 512], pa[:], AF.Silu, [bpa], [bzb])
                pool.dma(zs1_d[grp * 512:(grp + 1) * 512, :].rearrange("(t p) d -> p t d", p=128), zb_[:], dszb, reads=[bzb], writes=[bzs1[grp]])

            pre(0)
            for grp in range(8):
                if grp + 1 < 8:
                    pre(grp + 1)
                proj(grp)
            S.barrier(); S.flush()

        with ExitStack() as st:
            wst = rot_sb(st, "wst", 2, [128, 1024], F32, dma=True)
            w_k = T(st, "w_k", [128, 8, 1024], BF16); b_wk = Buf()
            w_v = T(st, "w_v", [128, 8, 1024], BF16); b_wv = Buf()
            load_w_bf16(wst, w_k, b_wk, wk_d, 1024, [pool, dve])
            load_w_bf16(wst, w_v, b_wv, wv_d, 1024, [pool, dve])
            xp_ = rot_sb(st, "xt", 4, [128, D], F32, dma=True)
            junk = rot_sb(st, "junk", 2, [128, D], BF16)
            ssq = rot_sb(st, "ssq", 4, [128, 4], F32)
            xs_pool = rot_sb(st, "xs", 3, [128, D], BF16)
            hT = rot_sb(st, "hT", 2, [128, 8, 512], BF16)
            kblk = rot_sb(st, "kblk", 2, [128, 8, 512], BF16, dma=True)
            vblk = rot_sb(st, "vblk", 2, [128, 8, 4, 129], BF16, dma=True)
            trps = rot_ps(st, "trps", 2, [128, 8, 128], BF16)
            pacc = rot_ps(st, "pacc", 4, [128, 512])
            for (vb_, bvb_, _d) in vblk.items:
                pool.op(lambda e, vb_=vb_: e.memset(vb_[:, :, :, 128:129], 1.0), (), [bvb_])
            ne = [0]
            hcur = {}

            def pre(grp):
                h_t, bh = hT.get()
                hcur[grp] = (h_t, bh)
                xa = {}

                def A(tl):
                    tt_ = grp * 4 + tl
                    xt, bx, dsx = xp_.get()
                    sp.dma(xt[:], h1_d[tt_ * 128:(tt_ + 1) * 128, :], dsx, reads=h1_deps(tt_), writes=[bx])
                    xa[tl] = prenormA(xt, bx, junk, ssq, xs_pool)

                def B(tl):
                    xs, bxs = xa.pop(tl)
                    prenormB(xs, bxs, [(lambda k, h_t=h_t, tl=tl: h_t[:, k, tl * 128:(tl + 1) * 128], bh, GKVc, None)], trps)

                A(0); A(1); B(0); A(2); B(1); A(3); B(2); B(3)

            def proj(grp):
                h_t, bh = hcur.pop(grp)
                kb_, bkb, dskb = kblk.get()
                for h in range(8):
                    pa, bpa = pacc.get()
                    for k in range(8):
                        mm(pa[:], w_k[:, k, h * 128:(h + 1) * 128], h_t[:, k, :], k == 0, k == 7, [b_wk, bh], [bpa])
                    cp(act, kb_[:, h, :], pa[:], [bpa], [bkb])
                    ne[0] += 1
                pool.dma(kT_d[:, :, grp * 512:(grp + 1) * 512], kb_[:], dskb, reads=[bkb], writes=[bkT[grp]])
                vb_, bvb, dsvb = vblk.get()
                for tl in range(4):
                    for nt in range(2):
                        pa, bpa = pacc.get()
                        for k in range(8):
                            mm(pa[:], h_t[:, k, tl * 128:(tl + 1) * 128], w_v[:, k, nt * 512:(nt + 1) * 512], k == 0, k == 7, [b_wv, bh], [bpa])
                        cp(act, vb_[:, nt * 4:(nt + 1) * 4, tl, 0:128], pa[:].rearrange("p (h e) -> p h e", e=128),
                           [bpa], [bvb])
                        ne[0] += 1
                pool.dma(v_d[:, :, grp * 4:(grp + 1) * 4, :].rearrange("h p t e -> p h (t e)"), vb_[:].rearrange("p h t e -> p h (t e)"),
                         dsvb, reads=[bvb], writes=[bvd[grp]])

            pre(0)
            for grp in range(8):
                if grp + 1 < 8:
                    pre(grp + 1)
                proj(grp)
            S.barrier(); S.flush()

        with ExitStack() as st:
            wst = rot_sb(st, "wst", 2, [128, 1024], F32, dma=True)
            w_o = T(st, "w_o", [128, 8, 1024], BF16); b_wo = Buf()
            load_w_bf16(wst, w_o, b_wo, bout_d, 1024, [pool, dve])
            qin = rot_sb(st, "qin", 2, [128, 2, 8, 512], BF16, dma=True)
            for (qt_, bqt_, _d) in qin.items:
                pool.op(lambda e, qt_=qt_: e.memset(qt_[64:128, 0], 0.0), (), [bqt_])
                pool.op(lambda e, qt_=qt_: e.memset(qt_[0:64, 1], 0.0), (), [bqt_])
            zin = rot_sb(st, "zin1", 2, [128, 4, 1024], BF16, dma=True)
            kin = rot_sb(st, "kin", 2, [128, L], BF16, dma=True)
            vin = rot_sb(st, "vin", 2, [128, 32, 129], BF16, dma=True)
            hin = rot_sb(st, "hin", 3, [128, D], F32, dma=True)
            ptp = rot_sb(st, "pt", 6, [128, 512], BF16)
            o0p = rot_sb(st, "o0", 2, [128, 4, 128], F32)
            odp = rot_sb(st, "od", 1, [128, 4, 8, 128], F32)
            rlp = rot_sb(st, "rl", 4, [128, 8], F32)
            sqp = rot_sb(st, "sq", 1, [128, 4, 8, 128], F32)
            ssn = rot_sb(st, "ssn", 2, [128, 2, 32], F32)
            yat = rot_sb(st, "yat", 1, [128, 4, 1024], BF16)
            yatT = rot_sb(st, "yatT", 2, [128, 8, 128], BF16)
            junk = rot_sb(st, "junk", 1, [128, D], BF16)
            ssq = rot_sb(st, "ssq", 4, [128, 4], F32)
            tbuf = rot_sb(st, "tbuf", 1, [128, D], F32)
            obuf = rot_sb(st, "obuf", 2, [128, D], F32, dma=True)
            bigb = PS(st, "big", [128, 4, 1024], BF16)
            bbig = [Buf() for _ in range(4)]
            pss = Rot([(bigb[:, i, :].bitcast(F32), bbig[i]) for i in range(4)])
            pacc = [[(PS(st, f"acc{p_}{i}", [128, 2, 256]), Buf()) for i in range(2)] for p_ in range(2)]
            ptr = Rot([(bigb[:, 3, :].rearrange("p (k m) -> p k m", m=128), bbig[3])])
            LA = 4
            jobs = [(qb, h) for qb in range(8) for h in range(8)]
            kv = {}

            def load_kv(job):
                qb, h = job
                nk = 4 * qb + 4
                ki, bki, dsk = kin.get()
                sp.dma(ki[:, 0:nk * 128], kT_d[:, h, 0:nk * 128], dsk, reads=bkT[0:qb + 1], writes=[bki])
                vi, bvi, dsv = vin.get()
                sp.dma(vi[:, 0:nk, :], v_d[h, :, 0:nk, :], dsv, reads=bvd[0:qb + 1], writes=[bvi])
                kv[job] = (ki, bki, vi, bvi)

            TAIL_D = 64
            pending_tail = []

            def tail_pe(qb, ya, bya):
                for jq in range(4):
                    tt_ = qb * 4 + jq
                    xt, bx, dsx = hin.get()
                    sp.dma(xt[:], h1_d[tt_ * 128:(tt_ + 1) * 128, :], dsx, reads=h1_deps(tt_), writes=[bx])
                    tp, btp = ptr.get()
                    for k in range(8):
                        trp(tp[:, k, :], ya[:, jq, k * 128:(k + 1) * 128], ident_b[:], [bya, b_identb], [btp])
                    yT, byT = yatT.get()
                    cp(dve, yT[:], tp[:], [btp], [byT])
                    po = bigb[:, 0:2, :].bitcast(F32).rearrange("p a n -> p (a n)")
                    for nt in range(2):
                        for k in range(8):
                            mm(po[:, nt * 512:(nt + 1) * 512], yT[:, k, :], w_o[:, k, nt * 512:(nt + 1) * 512], k == 0, k == 7, [byT, b_wo], [bbig[0], bbig[1]])
                    bout = Buf()
                    postnorm(po, [bbig[0], bbig[1]], 1, xt, bx, junk, ssq, tbuf, obuf, out_d[tt_ * 128:(tt_ + 1) * 128, :], bout, pool)

            load_kv(jobs[0])
            par = [0]
            for qb in range(8):
                qi, bqi, dsq = qin.get()
                sp.dma(qi[0:64, 0], qT_d[0:64, :, qb * 512:(qb + 1) * 512], dsq, reads=[bqT[qb]], writes=[bqi])
                sp.dma(qi[64:128, 1], qT_d[64:128, :, qb * 512:(qb + 1) * 512], dsq, reads=[bqT[qb]], writes=[bqi])
                zi, bzi, dsz = zin.get()
                sp.dma(zi[:], zs1_d[qb * 512:(qb + 1) * 512, :].rearrange("(t p) d -> p t d", p=128), dsz, reads=[bzs1[qb]], writes=[bzi])
                od, bod = odp.get()
                nk = 4 * qb + 4
                items = [(h, cc, kt) for h in range(8) for kt in range(nk) for cc in range(2)]
                pend = []
                o0s = {}
                accs = {}

                def stageA(it):
                    h, cc, kt = it
                    if cc == 0 and kt == (LA + 1) // 2:
                        ji = jobs.index((qb, h))
                        if ji + 1 < len(jobs):
                            load_kv(jobs[ji + 1])
                    if cc == 0 and kt == 0:
                        o0s[h] = o0p.get()
                    if kt == 0:
                        accs[(h, cc)] = pacc[cc]
                    ki, bki, vi, bvi = kv[(qb, h)]
                    ps_ = slice(cc * 64, (cc + 1) * 64)
                    r = kt - 4 * qb
                    q0 = max(r, 0) * 128
                    s_, bs_ = pss.get()
                    mm(s_[:, q0:512], ki[:, kt * 128:(kt + 1) * 128], qi[:, cc, h, q0:512], True, r < 0, [bki, bqi], [bs_])
                    if r >= 0:
                        mm(s_[:, q0:q0 + 128], ident_b[:], cmask_b[:], False, True, [b_identb, b_cmask], [bs_])
                    pt, bpt = ptp.get()
                    actf(pt[:, q0:512], s_[:, q0:512], AF.Exp, [bs_], [bpt], scale=0.125)
                    return (pt, bpt)

                def stageC(it, pt, bpt):
                    h, cc, kt = it
                    ki, bki, vi, bvi = kv[(qb, h)]
                    r = kt - 4 * qb
                    pa_ = accs[(h, cc)]
                    for jq in range(max(r, 0), 4):
                        acc, bacc = pa_[jq // 2]
                        mm(acc[:, jq % 2, 0:129], pt[:, jq * 128:(jq + 1) * 128], vi[:, kt, :],
                           kt == 0 and jq % 2 == 0, kt == 4 * qb + jq, [bpt, bvi], [bacc], sgc=True)
                    if kt != nk - 1:
                        return
                    o0, bo0 = o0s[h]
                    rl, brl = rlp.get()
                    for jq in range(4):
                        acc, bacc = pa_[jq // 2]
                        recip(rl[:, jq:jq + 1], acc[:, jq % 2, 128:129], [bacc], [brl])
                    if cc == 0:
                        for jq in range(4):
                            acc, bacc = pa_[jq // 2]
                            ts(dve, o0[:, jq, :], acc[:, jq % 2, 0:128], rl[:, jq:jq + 1], None, ALU.mult, None, [bacc, brl], [bo0])
                    else:
                        ts(dve, rl[:, 4:8], rl[:, 0:4], neglam[:, 0:1], None, ALU.mult, None, [brl, b_neglam], [brl])
                        for jq in range(4):
                            acc, bacc = pa_[jq // 2]
                            stt(dve, od[:, jq, h, :], acc[:, jq % 2, 0:128], rl[:, 4 + jq:5 + jq], o0[:, jq, :], ALU.mult, ALU.add,
                                [bacc, brl, bo0], [bod])

                for i in range(len(items) + LA):
                    if i < len(items):
                        pend.append(stageA(items[i]))
                    if i >= LA:
                        stageC(items[i - LA], *pend.pop(0))
                    if i == TAIL_D and pending_tail:
                        pending_tail.pop(0)()
                sq, bsq = sqp.get()
                sn, bsn = ssn.get()
                tt(pool, sq[:], od[:], od[:], ALU.mult, [bod], [bsq])
                dve.op(lambda e, sn=sn, sq=sq: e.reduce_sum(out=sn[:, 0, :], in_=sq[:].rearrange("p a h e -> p (a h) e"), axis=AX.X), [bsq], [bsn])
                actf(sn[:, 1, :], sn[:, 0, :], AF.Ln, [bsn, b_eps], [bsn], scale=1.0 / 128, bias=eps_col[:, 0:1])
                actf(sn[:, 1, :], sn[:, 1, :], AF.Exp, [bsn], [bsn], scale=-0.5)
                tt(dve, sq[:], od[:], sn[:, 1, :].rearrange("p (a h) -> p a h", h=8).unsqueeze(3).to_broadcast([128, 4, 8, 128]), ALU.mult,
                   [bod, bsn], [bsq])
                tt(pool, sq[:], sq[:], GS[:].unsqueeze(1).unsqueeze(1).to_broadcast([128, 4, 8, 128]), ALU.mult, [bsq, b_GS], [bsq])
                ya, bya = yat.get()
                tt(dve, ya[:], sq[:].rearrange("p a h e -> p a (h e)"), zi[:], ALU.mult, [bsq, bzi], [bya])
                pending_tail.append(lambda qb=qb, ya=ya, bya=bya: tail_pe(qb, ya, bya))
            while pending_tail:
                pending_tail.pop(0)()
            S.barrier(); S.flush()
    return nc, S


_CACHE = {}


def _get_program(stop=None):
    key = stop
    if key not in _CACHE:
        nc, S = build(None, stop)
        nc, S = build(S.record, stop)
        _CACHE[key] = nc
    return _CACHE[key]


def _prep_inputs(inp):
    f = lambda a: np.ascontiguousarray(np.asarray(a, dtype=np.float32))
    x = f(inp["x"]); c = f(inp["c"])
    dup = lambda a: np.ascontiguousarray(np.concatenate([a.T, a.T], 0))
    lam_re = f(inp["a_lam_re"])[0]; lam_im = f(inp["a_lam_im"])[0]; log_dt = f(inp["a_log_dt"])[0]
    b_re = f(inp["a_b_re"])[0]; b_im = f(inp["a_b_im"])[0]; c_re = f(inp["a_c_re"])[0]; c_im = f(inp["a_c_im"])[0]
    bre_t = b_re.transpose(1, 0, 2); bim_t = b_im.transpose(1, 0, 2)
    cre_t = c_re.transpose(2, 0, 1); cim_t = c_im.transpose(2, 0, 1)
    shared = {
        "ada_w": f(inp["ada_w"]), "ada_b": f(inp["ada_b"]), "g_pre": f(inp["g_pre"]), "g_post": f(inp["g_post"]),
        "gkv_col": np.ascontiguousarray(f(inp["g_kv"]).reshape(8, 128).T),
        "a_w_in": f(inp["a_w_in"])[0], "a_w_glu": f(inp["a_w_glu"])[0], "a_w_out": f(inp["a_w_out"])[0],
        "bglu_col": np.ascontiguousarray(f(inp["a_b_glu"])[0].reshape(8, 128).T),
        "w_k": f(inp["w_k"]), "w_v": f(inp["w_v"]), "b_w_in": f(inp["b_w_in"])[0], "b_w_out": f(inp["b_w_out"])[0],
        "lamre2": dup(lam_re), "lamim2": dup(lam_im),
        "logdt2": np.ascontiguousarray(np.broadcast_to(log_dt[None, :], (128, 64))),
        "bst": np.ascontiguousarray(np.concatenate([bre_t, bim_t], 0).reshape(128, 1024)),
        "bsw": np.ascontiguousarray(np.concatenate([bim_t, bre_t], 0).reshape(128, 1024)),
        "cst": np.ascontiguousarray(np.concatenate([cre_t, cim_t], 0).reshape(128, 1024)),
        "csw": np.ascontiguousarray(np.concatenate([cim_t, cre_t], 0).reshape(128, 1024)),
        "dcol": np.ascontiguousarray(np.tile(f(inp["a_d"])[0].reshape(64, 16).T, (8, 1))),
        "lqk": np.ascontiguousarray(np.concatenate([f(inp["b_lq1"])[0], f(inp["b_lk1"])[0], f(inp["b_lq2"])[0], f(inp["b_lk2"])[0]])[None, :]),
        "gsub": np.ascontiguousarray(f(inp["b_g_sub"])[0][None, :]),
    }
    maps = []
    for b in range(x.shape[0]):
        m = dict(shared)
        m["x"] = np.ascontiguousarray(x[b])
        m["cT"] = np.ascontiguousarray(c[b].reshape(8, 128).T)
        maps.append(m)
    return maps


def kernel(**inputs):
    stop = os.environ.get("MK_STOP") or None
    nc = _get_program(stop)
    maps = _prep_inputs(inputs)
    ncores = int(os.environ.get("MK_CORES", "8"))
    maps = maps[:ncores]
    res = run_bass_kernel_spmd(nc, maps, core_ids=list(range(len(maps))))
    outs = [np.asarray(r["out"], dtype=np.float32) for r in res.results]
    return np.stack(outs, 0)
```

```python
import math
import os
from contextlib import ExitStack

import numpy as np
import concourse.bass as bass
import concourse.mybir as mybir
from concourse.bass_utils import run_bass_kernel_spmd

F32 = mybir.dt.float32
BF16 = mybir.dt.bfloat16
I32 = mybir.dt.int32
AF = mybir.ActivationFunctionType
ALU = mybir.AluOpType
AX = mybir.AxisListType

L = 4096
D = 1024
EPS = 1e-6
LAMBDA_INIT = 0.8 - 0.6 * math.exp(-0.3 * 1)
NEG = -30000.0


class Buf:
    __slots__ = ("name", "w", "r")

    def __init__(self, name=""):
        self.name = name
        self.w = None
        self.r = []


class Tok:
    __slots__ = ("eng", "seq", "sem", "val")

    def __init__(self, eng, seq, sem, val):
        self.eng = eng
        self.seq = seq
        self.sem = sem
        self.val = val


class DmaSem:
    def __init__(self, sem, key):
        self.sem = sem
        self.key = key
        self.n = 0


class EngW:
    def __init__(self, sched, key, sem):
        self.sched = sched
        self.key = key
        self.sem = sem
        self.seq = 0
        self.cnt = 0
        self.waited_seq = {}
        self.waited_dma = {}
        self.prog = []
        self.last = None

    def _gather(self, reads, writes):
        deps = []
        for b in reads:
            if b.w is not None:
                deps.append(b.w)
        for b in writes:
            if b.w is not None:
                deps.append(b.w)
            deps.extend(b.r)
        return deps

    def _wait(self, tok):
        if tok.eng is None:
            k = tok.sem.key
            if self.waited_dma.get(k, 0) >= tok.val:
                return
            self.waited_dma[k] = tok.val
            sem, val = tok.sem.sem, tok.val
            self.prog.append(lambda e, sem=sem, val=val: e.wait_ge(sem, val))
            return
        if tok.eng is self and self.key == "pe":
            return
        k = tok.eng.key
        if self.waited_seq.get(k, -1) >= tok.seq:
            return
        self.waited_seq[k] = tok.seq
        self.sched.record.add((k, tok.seq))
        if tok.val is None:
            raise RuntimeError(f"token {k}:{tok.seq} not marked")
        sem, val = tok.sem, tok.val
        self.prog.append(lambda e, sem=sem, val=val: e.wait_ge(sem, val))

    def op(self, fn, reads=(), writes=()):
        for t in self._gather(reads, writes):
            self._wait(t)
        seq = self.seq
        self.seq += 1
        needed = self.sched.needed
        mark = needed is None or (self.key, seq) in needed
        if mark:
            self.cnt += 1
            sem = self.sem
            self.prog.append(lambda e, fn=fn, sem=sem: fn(e).then_inc(sem, 1))
            tok = Tok(self, seq, self.sem, self.cnt)
        else:
            self.prog.append(lambda e, fn=fn: fn(e))
            tok = Tok(self, seq, self.sem, None)
        self.last = tok
        for b in reads:
            b.r.append(tok)
        for b in writes:
            b.w = tok
            b.r = []
        return tok

    def dma(self, out, in_, dsem, reads=(), writes=()):
        for t in self._gather(reads, writes):
            self._wait(t)
        dsem.n += 1
        val = 16 * dsem.n
        sem = dsem.sem
        self.prog.append(lambda e, out=out, in_=in_, sem=sem: e.dma_start(out=out, in_=in_).then_inc(sem, 16))
        tok = Tok(None, -1, dsem, val)
        for b in reads:
            b.r.append(tok)
        for b in writes:
            b.w = tok
            b.r = []
        return tok


class Sched:
    def __init__(self, nc, stack, needed=None):
        self.nc = nc
        self.needed = needed
        self.record = set()
        self.stack = stack
        mk = lambda n: stack.enter_context(nc.semaphore(n))
        self.pe = EngW(self, "pe", mk("s_pe"))
        self.act = EngW(self, "act", mk("s_act"))
        self.dve = EngW(self, "dve", mk("s_dve"))
        self.pool = EngW(self, "pool", mk("s_pool"))
        self.sp = EngW(self, "sp", mk("s_sp"))
        self.engs = [self.pe, self.act, self.dve, self.pool, self.sp]
        self.ndsem = 0
        self.dsems = []

    def dma_sem(self):
        self.ndsem += 1
        key = f"dq{self.ndsem}"
        ds = DmaSem(self.stack.enter_context(self.nc.semaphore(key)), key)
        self.dsems.append(ds)
        return ds

    def barrier(self):
        toks = [w.last for w in self.engs[:4] if w.last is not None]
        for w in self.engs:
            for t in toks:
                if t.eng is not w:
                    w._wait(t)
            for ds in self.dsems:
                if ds.n > 0:
                    w._wait(Tok(None, -1, ds, 16 * ds.n))

    def flush(self):
        nc = self.nc
        with nc.Block() as block:
            @block.tensor
            def _(e):
                for f in self.pe.prog:
                    f(e)

            @block.scalar
            def _(e):
                for f in self.act.prog:
                    f(e)

            @block.vector
            def _(e):
                for f in self.dve.prog:
                    f(e)

            @block.gpsimd
            def _(e):
                for f in self.pool.prog:
                    f(e)

            @block.sync
            def _(e):
                for f in self.sp.prog:
                    f(e)
        for w in self.engs:
            w.prog = []


class Rot:
    def __init__(self, items):
        self.items = items
        self.i = 0

    def get(self):
        it = self.items[self.i % len(self.items)]
        self.i += 1
        return it


def build(needed=None, stop=None):
    nc = bass.Bass("TRN2", target_bir_lowering=False)
    di = lambda n, s: nc.dram_tensor(n, s, F32, kind="ExternalInput").ap()
    x_d = di("x", [L, D])
    cT_d = di("cT", [128, 8])
    adaw_d = di("ada_w", [2, D, 3 * D])
    adab_d = di("ada_b", [2, 3 * D])
    gpre_d = di("g_pre", [2, D])
    gpost_d = di("g_post", [2, D])
    gkvc_d = di("gkv_col", [128, 8])
    awin_d = di("a_w_in", [D, 2 * D])
    aglu_d = di("a_w_glu", [D, D])
    aout_d = di("a_w_out", [D, D])
    bgluc_d = di("bglu_col", [128, 8])
    wk_d = di("w_k", [D, D])
    wv_d = di("w_v", [D, D])
    bwin_d = di("b_w_in", [D, 2 * D])
    bout_d = di("b_w_out", [D, D])
    lamre_d = di("lamre2", [128, 64])
    lamim_d = di("lamim2", [128, 64])
    logdt_d = di("logdt2", [128, 64])
    bst_d = di("bst", [128, 1024])
    bsw_d = di("bsw", [128, 1024])
    cst_d = di("cst", [128, 1024])
    csw_d = di("csw", [128, 1024])
    dcol_d = di("dcol", [128, 64])
    lqk_d = di("lqk", [1, 256])
    gsub_d = di("gsub", [1, 128])
    out_d = nc.dram_tensor("out", [L, D], F32, kind="ExternalOutput").ap()
    scr = lambda n, s, d: nc.dram_tensor(n, s, d, kind="Internal").ap()
    U_d = scr("U_s", [4, 128, 8192], BF16)
    zs_d = scr("zs_s", [128, 8, L], BF16)
    gel_d = scr("gel_s", [128, 8, L], BF16)
    if stop == "h1":
        h1_d = out_d
    else:
        h1_d = scr("h1_s", [L, D], F32)
    qT_d = scr("qT_s", [128, 8, L], BF16)
    zs1_d = scr("zs1_s", [L, D], BF16)
    kT_d = scr("kT_s", [128, 8, L], BF16)
    v_d = scr("v_s", [8, 128, 32, 129], BF16)
    bkT = [Buf() for _ in range(8)]
    bvd = [Buf() for _ in range(8)]
    bU = [Buf() for _ in range(4)]
    bzs = [Buf() for _ in range(4)]
    bgel = [Buf() for _ in range(4)]
    bh1 = [Buf() for _ in range(32)]
    bqT = [Buf() for _ in range(8)]
    bzs1 = [Buf() for _ in range(8)]

    top = ExitStack()
    with top:
        S = Sched(nc, top, needed)
        pe, act, dve, pool, sp = S.pe, S.act, S.dve, S.pool, S.sp

        uid = [0]

        def T(st, n, s, d=F32):
            uid[0] += 1
            return st.enter_context(nc.sbuf_tensor(f"sb{uid[0]}_{n}", s, d))

        def PS(st, n, s, d=F32):
            uid[0] += 1
            return st.enter_context(nc.psum_tensor(f"ps{uid[0]}_{n}", s, d))

        def mm(out, lhsT, rhs, start, stop_, reads, writes, sgc=False):
            return pe.op(lambda e: e.matmul(out, lhsT=lhsT, rhs=rhs, start=start, stop=stop_, skip_group_check=sgc), reads, writes)

        def trp(out, in_, ident, reads, writes):
            return pe.op(lambda e: e.transpose(out, in_, ident), reads, writes)

        def actf(out, in_, func, reads, writes, **kw):
            return act.op(lambda e: e.activation(out=out, in_=in_, func=func, **kw), reads, writes)

        def tt(eng, out, in0, in1, op, reads, writes):
            return eng.op(lambda e: e.tensor_tensor(out=out, in0=in0, in1=in1, op=op), reads, writes)

        def ts(eng, out, in0, s1, s2, op0, op1, reads, writes):
            if s2 is None:
                return eng.op(lambda e: e.tensor_scalar(out=out, in0=in0, scalar1=s1, scalar2=None, op0=op0), reads, writes)
            return eng.op(lambda e: e.tensor_scalar(out=out, in0=in0, scalar1=s1, scalar2=s2, op0=op0, op1=op1), reads, writes)

        def stt(eng, out, in0, scalar, in1, op0, op1, reads, writes):
            return eng.op(lambda e: e.scalar_tensor_tensor(out=out, in0=in0, scalar=scalar, in1=in1, op0=op0, op1=op1), reads, writes)

        def cp(eng, out, in_, reads, writes):
            if eng is act:
                return act.op(lambda e: e.copy(out=out, in_=in_), reads, writes)
            return eng.op(lambda e: e.tensor_copy(out=out, in_=in_), reads, writes)

        def recip(out, in_, reads, writes):
            return dve.op(lambda e: e.reciprocal(out=out, in_=in_), reads, writes)

        def mset(eng, ap, val, writes):
            return eng.op(lambda e: e.memset(ap, val), (), writes)

        def rot_sb(st, name, n, shape, dt=F32, dma=False):
            items = []
            for i in range(n):
                t = T(st, f"{name}{i}", shape, dt)
                if dma:
                    items.append((t, Buf(), S.dma_sem()))
                else:
                    items.append((t, Buf()))
            return Rot(items)

        def rot_ps(st, name, n, shape, dt=F32):
            return Rot([(PS(st, f"{name}{i}", shape, dt), Buf()) for i in range(n)])

        def rot_ps_sub(st, name, nbanks, nsub, subshape, dt=F32):
            items = []
            for i in range(nbanks):
                t = PS(st, f"{name}{i}", [128, nsub] + list(subshape), dt)
                for j in range(nsub):
                    items.append((t[:, j], Buf()))
            return Rot(items)

        def load_w_bf16(st_pool, dst, bdst, src_d, ncols, cast_engs):
            for k in range(8):
                stg, bs, ds = st_pool.get()
                sp.dma(stg[:, 0:ncols], src_d[k * 128:(k + 1) * 128, :], ds, writes=[bs])
                eng = cast_engs[k % len(cast_engs)]
                cp(eng, dst[:, k, :], stg[:, 0:ncols], [bs], [bdst])

        ident_b = T(top, "ident_b", [128, 128], BF16); b_identb = Buf()
        ident_f = T(top, "ident_f", [128, 128], F32); b_identf = Buf()
        cmask_b = T(top, "cmask_b", [128, 128], BF16); b_cmask = Buf()
        ones_row = T(top, "ones_row", [1, 128], F32); b_ones = Buf()
        eps_col = T(top, "eps_col", [128, 1], F32); b_eps = Buf()
        sgn_col = T(top, "sgn_col", [128, 1], F32); b_sgn = Buf()
        cols = T(top, "cols", [128, 48], F32); b_cols = Buf()
        GG = T(top, "GG", [128, 2, D], F32); b_GG = Buf()
        GS = T(top, "GS", [128, 128], F32); b_GS = Buf()
        neglam = T(top, "neglam", [128, 2], F32); b_neglam = Buf()

        mset(pool, ident_f[:], 1.0, [b_identf])
        pool.op(lambda e: e.affine_select(out=ident_f[:], in_=ident_f[:], pattern=[[-1, 128]], compare_op=ALU.is_equal,
                                          fill=0.0, base=0, channel_multiplier=1), [b_identf], [b_identf])
        cp(pool, ident_b[:], ident_f[:], [b_identf], [b_identb])
        mset(pool, cmask_b[:], 0.0, [b_cmask])
        pool.op(lambda e: e.affine_select(out=cmask_b[:], in_=cmask_b[:], pattern=[[1, 128]], compare_op=ALU.is_ge,
                                          fill=NEG, base=0, channel_multiplier=-1), [b_cmask], [b_cmask])
        mset(pool, ones_row[:], 1.0, [b_ones])
        mset(pool, eps_col[:], EPS, [b_eps])
        mset(pool, sgn_col[0:64, :], -1.0, [b_sgn])
        mset(pool, sgn_col[64:128, :], 1.0, [b_sgn])

        with ExitStack() as st:
            dq = S.dma_sem()
            ct = T(st, "ct", [128, 8]); b_ct = Buf()
            sc = T(st, "sc", [128, 8]); b_sc = Buf()
            rows = T(st, "rows", [1, 2, 3 * D]); b_rows = Buf()
            adab = T(st, "adab", [1, 2, 3 * D]); b_adab = Buf()
            gpr = T(st, "gpr", [1, 2, D]); b_gpr = Buf()
            gpo = T(st, "gpo", [1, 2, D]); b_gpo = Buf()
            arow = T(st, "arow", [1, 2, D]); b_arow = Buf()
            ggrow = T(st, "ggrow", [1, 2, D]); b_ggrow = Buf()
            lqk = T(st, "lqk", [1, 256]); b_lqk = Buf()
            gsr = T(st, "gsr", [1, 128]); b_gsr = Buf()
            sm = T(st, "sm", [1, 16]); b_sm = Buf()
            awp = rot_sb(st, "awst", 3, [128, 8, 512], F32, dma=True)
            pmod = rot_ps(st, "pmod", 2, [1, 512])
            pcol = PS(st, "pcol", [128, 64]); b_pcol = Buf()
            pbc = rot_ps(st, "pbc", 2, [128, 512])

            sp.dma(ct[:], cT_d, dq, writes=[b_ct])
            sp.dma(adab[0:1, 0, :], adab_d[0:1, :], dq, writes=[b_adab])
            sp.dma(adab[0:1, 1, :], adab_d[1:2, :], dq, writes=[b_adab])
            sp.dma(gpr[0:1, 0, :], gpre_d[0:1, :], dq, writes=[b_gpr])
            sp.dma(gpr[0:1, 1, :], gpre_d[1:2, :], dq, writes=[b_gpr])
            sp.dma(gpo[0:1, 0, :], gpost_d[0:1, :], dq, writes=[b_gpo])
            sp.dma(gpo[0:1, 1, :], gpost_d[1:2, :], dq, writes=[b_gpo])
            sp.dma(lqk[:], lqk_d, dq, writes=[b_lqk])
            sp.dma(gsr[:], gsub_d, dq, writes=[b_gsr])
            sp.dma(cols[:, 32:40], gkvc_d, dq, writes=[b_cols])
            sp.dma(cols[:, 40:48], bgluc_d, dq, writes=[b_cols])
            actf(sc[:], ct[:], AF.Silu, [b_ct], [b_sc])
            for l in range(2):
                for nt in range(6):
                    stg, bs, ds = awp.get()
                    sp.dma(stg[:], adaw_d[l].rearrange("(k p) n -> p k n", p=128)[:, :, nt * 512:(nt + 1) * 512], ds, writes=[bs])
                    pm, bpm = pmod.get()
                    for k in range(8):
                        mm(pm[:], sc[:, k:k + 1], stg[:, k, :], k == 0, k == 7, [b_sc, bs], [bpm])
                    tt(dve, rows[0:1, l, nt * 512:(nt + 1) * 512], pm[:], adab[0:1, l, nt * 512:(nt + 1) * 512], ALU.add,
                       [bpm, b_adab], [b_rows])
            for l in range(2):
                stt(dve, arow[0:1, l, :], rows[0:1, l, D:2 * D], 1.0, gpr[0:1, l, :], ALU.add, ALU.mult, [b_rows, b_gpr], [b_arow])
                tt(dve, ggrow[0:1, l, :], rows[0:1, l, 2 * D:3 * D], gpo[0:1, l, :], ALU.mult, [b_rows, b_gpo], [b_ggrow])
            for l in range(2):
                for which in range(2):
                    idx = l * 2 + which
                    for k in range(8):
                        src = arow[0:1, l, k * 128:(k + 1) * 128] if which == 0 else rows[0:1, l, k * 128:(k + 1) * 128]
                        c0 = (idx * 8 + k) * 2
                        mm(pcol[:, c0:c0 + 2], src, ones_row[0:1, 0:2], True, True, [b_arow, b_rows, b_ones], [b_pcol])
            cp(dve, cols[:, 0:32], pcol[:].rearrange("p (a two) -> p a two", two=2)[:, :, 0], [b_pcol], [b_cols])
            for l in range(2):
                for hf in range(2):
                    pb, bpb = pbc.get()
                    mm(pb[:], ones_row[0:1, 0:128], ggrow[0:1, l, hf * 512:(hf + 1) * 512], True, True, [b_ones, b_ggrow], [bpb])
                    cp(act, GG[:, l, hf * 512:(hf + 1) * 512], pb[:], [bpb], [b_GG])
            tt(dve, lqk[0:1, 0:64], lqk[0:1, 0:64], lqk[0:1, 64:128], ALU.mult, [b_lqk], [b_lqk])
            tt(dve, lqk[0:1, 128:192], lqk[0:1, 128:192], lqk[0:1, 192:256], ALU.mult, [b_lqk], [b_lqk])
            dve.op(lambda e: e.reduce_sum(out=sm[0:1, 0:1], in_=lqk[0:1, 0:64], axis=AX.X), [b_lqk], [b_sm])
            dve.op(lambda e: e.reduce_sum(out=sm[0:1, 1:2], in_=lqk[0:1, 128:192], axis=AX.X), [b_lqk], [b_sm])
            actf(sm[0:1, 2:4], sm[0:1, 0:2], AF.Exp, [b_sm], [b_sm])
            tt(dve, sm[0:1, 4:5], sm[0:1, 3:4], sm[0:1, 2:3], ALU.subtract, [b_sm], [b_sm])
            ts(dve, sm[0:1, 6:8], sm[0:1, 4:5].to_broadcast([1, 2]), -LAMBDA_INIT, None, ALU.add, None, [b_sm], [b_sm])
            pb, bpb = pbc.get()
            mm(pb[:, 0:2], ones_row[0:1, 0:128], sm[0:1, 6:8], True, True, [b_ones, b_sm], [bpb])
            cp(dve, neglam[:], pb[:, 0:2], [bpb], [b_neglam])
            pb, bpb = pbc.get()
            mm(pb[:, 0:128], ones_row[0:1, 0:128], gsr[0:1, :], True, True, [b_ones, b_gsr], [bpb])
            ts(dve, GS[:], pb[:, 0:128], 1.0 - LAMBDA_INIT, None, ALU.mult, None, [bpb], [b_GS])
            S.barrier(); S.flush()

        if stop == "C":
            return nc, S
        A0c, S0c, A1c, S1c, GKVc, BGLUc = (cols[:, 0:8], cols[:, 8:16], cols[:, 16:24], cols[:, 24:32], cols[:, 32:40], cols[:, 40:48])

        def prenormA(xt, bx, junk, ssq, xs_pool):
            jk, bj = junk.get()
            s_t, bs_ = ssq.get()
            mset(pool, s_t[:, 0:1], 0.0, [bs_])
            actf(jk[:], xt[:], AF.Square, [bx], [bj, bs_], accum_out=s_t[:, 0:1])
            actf(s_t[:, 1:2], s_t[:, 0:1], AF.Sqrt, [bs_, b_eps], [bs_], scale=1.0 / D, bias=eps_col[:, 0:1])
            recip(s_t[:, 2:3], s_t[:, 1:2], [bs_], [bs_])
            xs, bxs = xs_pool.get()
            ts(dve, xs[:], xt[:], s_t[:, 2:3], None, ALU.mult, None, [bx, bs_], [bxs])
            return xs, bxs

        def prenormB(xs, bxs, dsts, trps):
            tp, btp = trps.get()
            for k in range(8):
                trp(tp[:, k, :], xs[:, k * 128:(k + 1) * 128], ident_b[:], [bxs, b_identb], [btp])
            for (dfn, bd, scl, bia) in dsts:
                for k in range(8):
                    if bia is None:
                        ts(dve, dfn(k), tp[:, k, :], scl[:, k:k + 1], None, ALU.mult, None, [btp, b_cols], [bd])
                    else:
                        ts(dve, dfn(k), tp[:, k, :], scl[:, k:k + 1], bia[:, k:k + 1], ALU.mult, ALU.add, [btp, b_cols], [bd])

        def prenorm(st_res, xt, bx, dsts, junk, ssq, xs_pool, trps, ridx):
            xs, bxs = prenormA(xt, bx, junk, ssq, xs_pool)
            prenormB(xs, bxs, dsts, trps)

        def postnorm(po, bpo, l, xt, bx, junk, ssq, tbuf, obuf, dst_ap, bdst, dq_out):
            jk, bj = junk.get()
            s_t, bs_ = ssq.get()
            mset(pool, s_t[:, 0:1], 0.0, [bs_])
            bpo = bpo if isinstance(bpo, list) else [bpo]
            actf(jk[:], po, AF.Square, bpo, [bj, bs_], accum_out=s_t[:, 0:1])
            actf(s_t[:, 1:2], s_t[:, 0:1], AF.Sqrt, [bs_, b_eps], [bs_], scale=1.0 / D, bias=eps_col[:, 0:1])
            recip(s_t[:, 2:3], s_t[:, 1:2], [bs_], [bs_])
            tb, btb = tbuf.get()
            stt(dve, tb[:], po, s_t[:, 2:3], GG[:, l, :], ALU.mult, ALU.mult, bpo + [bs_, b_GG], [btb])
            ob, bob, dso = obuf.get()
            tt(pool, ob[:], tb[:], xt[:], ALU.add, [btb, bx], [bob])
            (dq_out if isinstance(dq_out, EngW) else sp).dma(dst_ap, ob[:], dso, reads=[bob], writes=[bdst])

        xrows0 = x_d.rearrange("(b p j) d -> b j p d", p=128, j=8)
        h1rows0 = h1_d.rearrange("(b p j) d -> b j p d", p=128, j=8)
        with ExitStack() as st:
            wst = rot_sb(st, "wst", 2, [128, 2048], F32, dma=True)
            w_in = T(st, "w_in", [128, 8, 2048], BF16); b_win = Buf()
            xp_ = rot_sb(st, "xt", 4, [128, D], F32, dma=True)
            junk = rot_sb(st, "junk", 2, [128, D], BF16)
            ssq = rot_sb(st, "ssq", 4, [128, 4], F32)
            xs_pool = rot_sb(st, "xs", 3, [128, D], BF16)
            hT = rot_sb(st, "hT", 2, [128, 8, 1024], BF16)
            utm = rot_sb(st, "utm", 1, [128, 64, 8, 16], BF16)
            ublk = rot_sb(st, "ublk", 1, [128, 64, 128], BF16, dma=True)
            zsT = rot_sb(st, "zsT", 1, [128, 8, 1024], BF16, dma=True)
            trps = rot_ps(st, "trps", 2, [128, 8, 128], BF16)
            pacc = rot_ps(st, "pacc", 4, [128, 512])
            dq_o = S.dma_sem(); dq_o2 = S.dma_sem()
            load_w_bf16(wst, w_in, b_win, awin_d, 2048, [pool])
            ne = [0]
            hslots = [hT.get() for _ in range(2)]
            hbufs = [[Buf() for _ in range(8)] for _ in range(2)]
            cur_u = {}

            xsd = {}

            def preA(t):
                b, j = divmod(t, 8)
                xt, bx, dsx = xp_.get()
                sp.dma(xt[:], xrows0[b, j], dsx, writes=[bx])
                xsd[t] = prenormA(xt, bx, junk, ssq, xs_pool)

            def preB(t):
                b, j = divmod(t, 8)
                h_t = hslots[b % 2][0]
                bhj = hbufs[b % 2][j]
                xs, bxs = xsd.pop(t)
                prenormB(xs, bxs, [(lambda k, h_t=h_t, j=j: h_t[:, k, j * 128:(j + 1) * 128], bhj, A0c, S0c)], trps)

            def proj(t):
                b, j = divmod(t, 8)
                h_t = hslots[b % 2][0]
                bhj = hbufs[b % 2][j]
                if j == 0:
                    cur_u[b] = utm.get()
                u_t, bu = cur_u[b]
                for nt in range(2):
                    pa, bpa = pacc.get()
                    for k in range(8):
                        mm(pa[:], h_t[:, k, j * 128:(j + 1) * 128], w_in[:, k, nt * 512:(nt + 1) * 512], k == 0, k == 7, [bhj, b_win], [bpa])
                    cp(act, u_t[:, nt * 32:(nt + 1) * 32, j, :], pa[:].rearrange("p (g c) -> p g c", c=16), [bpa], [bu])
                    ne[0] += 1

            def blockend(b):
                h_t = hslots[b % 2][0]
                bhl = hbufs[b % 2]
                u_t, bu = cur_u[b]
                z_t, bz, dsz_ = zsT.get()
                for zc in range(8):
                    for hf in range(2):
                        pa, bpa = pacc.get()
                        for k in range(8):
                            mm(pa[:], w_in[:, k, 1024 + zc * 128:1024 + (zc + 1) * 128], h_t[:, k, hf * 512:(hf + 1) * 512], k == 0, k == 7,
                               bhl[hf * 4:(hf + 1) * 4] + [b_win], [bpa])
                        actf(z_t[:, zc, hf * 512:(hf + 1) * 512], pa[:], AF.Silu, [bpa], [bz])
                sp.dma(zs_d[:, :, b * 1024:(b + 1) * 1024], z_t[:], dsz_, reads=[bz], writes=[bzs[b]])
                ub, bub, dsub = ublk.get()
                for g8 in range(8):
                    tp, btp = trps.get()
                    for gl in range(8):
                        g = g8 * 8 + gl
                        trp(tp[:, gl, :], u_t[:, g].rearrange("p i c -> p (i c)"), ident_b[:], [bu, b_identb], [btp])
                    cp(dve if g8 % 2 == 0 else act, ub[:, g8 * 8:(g8 + 1) * 8, :], tp[:], [btp], [bub])
                sp.dma(U_d[b].rearrange("p (g c) -> p g c", c=128), ub[:], dsub, reads=[bub], writes=[bU[b]])

            preA(0); preA(1); preB(0)
            for t in range(32):
                if t + 2 < 32:
                    preA(t + 2)
                if t + 1 < 32:
                    preB(t + 1)
                proj(t)
                if t % 8 == 7:
                    blockend(t // 8)
            S.barrier(); S.flush()

        if stop == "2a":
            return nc, S
        with ExitStack() as st:
            WXA = T(st, "WXA", [128, 64, 128], BF16); b_wxa = Buf()
            WXB = T(st, "WXB", [128, 64, 128], BF16); b_wxb = Buf()
            WY1 = T(st, "WY1", [128, 64, 128], BF16); b_wy1 = Buf()
            WY2 = T(st, "WY2", [128, 64, 128], BF16); b_wy2 = Buf()
            b_tc = Buf(); b_ts = Buf()
            RHO = T(st, "RHO", [128, 64], F32); b_rho = Buf()
            E1c = T(st, "E1c", [128, 64]); E1s = T(st, "E1s", [128, 64]); b_e1 = Buf()
            PSW = T(st, "PSW", [128, 128], F32); b_psw = Buf()
            ts(pool, PSW[:, 0:64], ident_f[:, 64:128], -1.0, None, ALU.mult, None, [b_identf], [b_psw])
            cp(pool, PSW[:, 64:128], ident_f[:, 0:64], [b_identf], [b_psw])
            with ExitStack() as sp1:
                dq = S.dma_sem()
                lamre = T(sp1, "lamre", [128, 64]); lamim = T(sp1, "lamim", [128, 64]); dt_ = T(sp1, "dt", [128, 64])
                b_in = Buf()
                sp.dma(lamre[:], lamre_d, dq, writes=[b_in])
                sp.dma(lamim[:], lamim_d, dq, writes=[b_in])
                sp.dma(dt_[:], logdt_d, dq, writes=[b_in])
                Dcol = T(sp1, "Dcol", [128, 64]); b_dcol = b_in
                sp.dma(Dcol[:], dcol_d, dq, writes=[b_dcol])
                LR = T(sp1, "LR", [128, 9, 64]); LI = T(sp1, "LI", [128, 9, 64])
                MR = T(sp1, "MR", [128, 8, 64]); MI = T(sp1, "MI", [128, 8, 64])
                LIs = T(sp1, "LIs", [128, 9, 64]); LRn = T(sp1, "LRn", [128, 9, 64]); LIn = T(sp1, "LIn", [128, 9, 64])
                MRn = T(sp1, "MRn", [128, 8, 64]); MIn = T(sp1, "MIn", [128, 8, 64])
                b_pw = Buf()
                w = T(sp1, "wk_", [128, 12, 64]); b_w = Buf()
                wi = T(sp1, "wi_", [128, 64], I32)
                FRE = T(sp1, "FRE", [128, 64]); FIMs = T(sp1, "FIMs", [128, 64]); FIMn = T(sp1, "FIMn", [128, 64]); b_f = Buf()
                mask = T(sp1, "mask", [128, 128]); b_mask = Buf()
                mset(pool, mask[:], 1.0, [b_mask])
                pool.op(lambda e: e.affine_select(out=mask[:].rearrange("p (j c) -> p j c", c=16), in_=mask[:].rearrange("p (j c) -> p j c", c=16),
                                                  pattern=[[16, 8], [0, 16]], compare_op=ALU.is_ge, fill=0.0, base=15, channel_multiplier=-1),
                        [b_mask], [b_mask])
                R = [b_in, b_w]
                actf(dt_[:], dt_[:], AF.Exp, [b_in], [b_in])
                tt(dve, w[:, 0, :], lamre[:], dt_[:], ALU.mult, R, [b_w])
                tt(dve, w[:, 1, :], lamim[:], dt_[:], ALU.mult, R, [b_w])

                def sin_of(dst, shift):
                    ts(dve, w[:, 2, :], w[:, 1, :], shift, 1.0 / (2 * math.pi), ALU.add, ALU.mult, [b_w], [b_w])
                    cp(dve, wi[:], w[:, 2, :], [b_w], [b_w])
                    cp(dve, w[:, 2, :], wi[:], [b_w], [b_w])
                    stt(dve, w[:, 2, :], w[:, 2, :], -2 * math.pi, w[:, 1, :], ALU.mult, ALU.add, [b_w], [b_w])
                    ts(dve, w[:, 3, :], w[:, 2, :], shift, None, ALU.add, None, [b_w], [b_w])
                    ts(dve, w[:, 2, :], w[:, 3, :], math.pi, -2 * math.pi, ALU.is_gt, ALU.mult, [b_w], [b_w])
                    tt(dve, w[:, 3, :], w[:, 3, :], w[:, 2, :], ALU.add, [b_w], [b_w])
                    actf(dst, w[:, 3, :], AF.Sin, [b_w], [b_w])

                sin_of(w[:, 4, :], 0.0)
                sin_of(w[:, 5, :], 0.5 * math.pi)
                actf(w[:, 6, :], w[:, 0, :], AF.Exp, [b_w], [b_w])
                actf(w[:, 7, :], w[:, 0, :], AF.Exp, [b_w], [b_w], scale=-1.0)
                actf(RHO[:], w[:, 0, :], AF.Exp, [b_w], [b_rho], scale=8.0)
                actf(w[:, 8, :], w[:, 0, :], AF.Exp, [b_w], [b_w], scale=-8.0)
                P_ = [b_w, b_pw]
                mset(pool, LR[:, 0, :], 1.0, [b_pw]); mset(pool, LI[:, 0, :], 0.0, [b_pw])
                mset(pool, MR[:, 0, :], 1.0, [b_pw]); mset(pool, MI[:, 0, :], 0.0, [b_pw])
                tt(dve, LR[:, 1, :], w[:, 6, :], w[:, 5, :], ALU.mult, P_, [b_pw])
                tt(dve, LI[:, 1, :], w[:, 6, :], w[:, 4, :], ALU.mult, P_, [b_pw])
                tt(dve, MR[:, 1, :], w[:, 7, :], w[:, 5, :], ALU.mult, P_, [b_pw])
                stt(dve, MI[:, 1, :], w[:, 7, :], -1.0, w[:, 4, :], ALU.mult, ALU.mult, P_, [b_pw])

                def cpow(XR, XI, k):
                    tt(dve, w[:, 9, :], XR[:, k, :], XR[:, 1, :], ALU.mult, P_, [b_w])
                    tt(dve, w[:, 10, :], XI[:, k, :], XI[:, 1, :], ALU.mult, P_, [b_w])
                    tt(dve, XR[:, k + 1, :], w[:, 9, :], w[:, 10, :], ALU.subtract, P_, [b_pw])
                    tt(dve, w[:, 9, :], XR[:, k, :], XI[:, 1, :], ALU.mult, P_, [b_w])
                    tt(dve, w[:, 10, :], XI[:, k, :], XR[:, 1, :], ALU.mult, P_, [b_w])
                    tt(dve, XI[:, k + 1, :], w[:, 9, :], w[:, 10, :], ALU.add, P_, [b_pw])

                for k in range(1, 8):
                    cpow(LR, LI, k)
                for k in range(1, 7):
                    cpow(MR, MI, k)
                ts(dve, LIs[:], LI[:], sgn_col[:, 0:1], None, ALU.mult, None, [b_pw, b_sgn], [b_pw])
                ts(dve, LRn[:], LR[:], sgn_col[:, 0:1], -1.0, ALU.mult, ALU.mult, [b_pw, b_sgn], [b_pw])
                ts(dve, LIn[:], LI[:], -1.0, None, ALU.mult, None, [b_pw], [b_pw])
                ts(dve, MRn[:], MR[:], sgn_col[:, 0:1], -1.0, ALU.mult, ALU.mult, [b_pw, b_sgn], [b_pw])
                ts(dve, MIn[:], MI[:], -1.0, None, ALU.mult, None, [b_pw], [b_pw])
                F_ = [b_in, b_w, b_pw, b_f]
                ts(dve, w[:, 2, :], LR[:, 1, :], -1.0, None, ALU.add, None, F_, [b_w])
                tt(dve, w[:, 3, :], lamre[:], lamre[:], ALU.mult, F_, [b_w])
                tt(dve, w[:, 9, :], lamim[:], lamim[:], ALU.mult, F_, [b_w])
                tt(dve, w[:, 3, :], w[:, 3, :], w[:, 9, :], ALU.add, F_, [b_w])
                recip(w[:, 3, :], w[:, 3, :], [b_w], [b_w])
                tt(dve, w[:, 9, :], w[:, 2, :], lamre[:], ALU.mult, F_, [b_w])
                tt(dve, w[:, 10, :], LI[:, 1, :], lamim[:], ALU.mult, F_, [b_w])
                tt(dve, w[:, 9, :], w[:, 9, :], w[:, 10, :], ALU.add, F_, [b_w])
                tt(dve, FRE[:], w[:, 9, :], w[:, 3, :], ALU.mult, F_, [b_f])
                tt(dve, w[:, 9, :], LI[:, 1, :], lamre[:], ALU.mult, F_, [b_w])
                tt(dve, w[:, 10, :], w[:, 2, :], lamim[:], ALU.mult, F_, [b_w])
                tt(dve, w[:, 9, :], w[:, 9, :], w[:, 10, :], ALU.subtract, F_, [b_w])
                tt(dve, w[:, 9, :], w[:, 9, :], w[:, 3, :], ALU.mult, F_, [b_w])
                ts(dve, FIMs[:], w[:, 9, :], sgn_col[:, 0:1], None, ALU.mult, None, [b_w, b_sgn], [b_f])
                ts(dve, FIMn[:], FIMs[:], -1.0, None, ALU.mult, None, [b_f], [b_f])
                tt(dve, E1c[:], LR[:, 8, :], w[:, 8, :], ALU.mult, P_, [b_e1])
                tt(dve, E1s[:], LI[:, 8, :], w[:, 8, :], ALU.mult, P_, [b_e1])

                def bc(a):
                    return a.unsqueeze(2).to_broadcast([128, 64, 16])

                with ExitStack() as sp2:
                    Bst = T(sp2, "Bst", [128, 64, 16]); Bsw = T(sp2, "Bsw", [128, 64, 16])
                    Cst = T(sp2, "Cst", [128, 64, 16]); Csw = T(sp2, "Csw", [128, 64, 16]); b_bc = Buf()
                    sp.dma(Bst[:], bst_d.rearrange("p (g c) -> p g c", c=16), dq, writes=[b_bc])
                    sp.dma(Bsw[:], bsw_d.rearrange("p (g c) -> p g c", c=16), dq, writes=[b_bc])
                    sp.dma(Cst[:], cst_d.rearrange("p (g c) -> p g c", c=16), dq, writes=[b_bc])
                    sp.dma(Csw[:], csw_d.rearrange("p (g c) -> p g c", c=16), dq, writes=[b_bc])
                    Bbst = T(sp2, "Bbst", [128, 64, 16]); Bbsw = T(sp2, "Bbsw", [128, 64, 16]); b_bb = Buf()
                    t1p = rot_sb(sp2, "t1p", 2, [128, 64, 16]); t2p = rot_sb(sp2, "t2p", 2, [128, 64, 16])

                    def lin2(out, bo, a0, s0, a1, s1, rd):
                        t1, bt1 = t1p.get(); t2, bt2 = t2p.get()
                        tt(dve, t1[:], a0, bc(s0), ALU.mult, rd, [bt1])
                        tt(pool, t2[:], a1, bc(s1), ALU.mult, rd, [bt2])
                        tt(dve if lin2.n % 2 == 0 else pool, out, t1[:], t2[:], ALU.add, [bt1, bt2], [bo])
                        lin2.n += 1
                    lin2.n = 0
                    rdB = [b_bc, b_f, b_pw, b_bb]
                    lin2(Bbst[:], b_bb, Bst[:], FRE[:], Bsw[:], FIMs[:], [b_bc, b_f])
                    lin2(Bbsw[:], b_bb, Bsw[:], FRE[:], Bst[:], FIMn[:], [b_bc, b_f])
                    wy1v = WY1[:].rearrange("p g (j c) -> p g j c", c=16)
                    for j in range(8):
                        lin2(wy1v[:, :, j, :], b_wy1, Cst[:], LRn[:, j + 1, :], Csw[:], LIn[:, j + 1, :], rdB)
                    pxt = rot_ps(sp2, "pxt", 2, [128, 4, 128])
                    pw2 = rot_ps(sp2, "pw2", 2, [128, 4, 128])
                    tmpw = rot_sb(sp2, "tmpw", 2, [128, 4, 128])
                    P1 = T(sp2, "P1", [128, 32, 8, 16]); b_p1 = Buf()
                    P2 = T(sp2, "P2", [128, 32, 8, 16]); b_p2 = Buf()
                    for hf in range(2):
                        gs = slice(hf * 32, (hf + 1) * 32)

                        def bch(a):
                            return a[:, gs].unsqueeze(2).to_broadcast([128, 32, 16])

                        def lin2h(out, bo, a0, s0, a1, s1, rd):
                            t1, bt1 = t1p.get(); t2, bt2 = t2p.get()
                            tt(dve, t1[:, 0:32, :], a0, bch(s0), ALU.mult, rd, [bt1])
                            tt(pool, t2[:, 0:32, :], a1, bch(s1), ALU.mult, rd, [bt2])
                            tt(dve if lin2.n % 2 == 0 else pool, out, t1[:, 0:32, :], t2[:, 0:32, :], ALU.add, [bt1, bt2], [bo])
                            lin2.n += 1
                        for i in range(8):
                            k = 7 - i
                            lin2h(P1[:, :, i, :], b_p1, Bbst[:, gs, :], LR[:, k, :], Bbsw[:, gs, :], LIs[:, k, :], rdB)
                            lin2h(P2[:, :, i, :], b_p2, Cst[:, gs, :], MRn[:, k, :], Csw[:, gs, :], MIn[:, k, :], rdB)
                        for g4 in range(8):
                            px, bpx = pxt.get()
                            p2_, bp2 = pw2.get()
                            for gl in range(4):
                                gi = g4 * 4 + gl
                                trp(px[:, gl, :], P1[:, gi].rearrange("p i c -> p (i c)"), ident_f[:], [b_p1, b_identf], [bpx])
                                mm(p2_[:, gl, :], P1[:, gi].rearrange("p i c -> p (i c)"), P2[:, gi].rearrange("p i c -> p (i c)"), True, True,
                                   [b_p1, b_p2], [bp2])
                            g0 = hf * 32 + g4 * 4
                            cp(act, WXA[:, g0:g0 + 4, :], px[:], [bpx], [b_wxa])
                            cp(act, WXB[:, g0:g0 + 4, 0:64], px[:, :, 64:128], [bpx], [b_wxb])
                            actf(WXB[:, g0:g0 + 4, 64:128], px[:, :, 0:64], AF.Copy, [bpx], [b_wxb], scale=-1.0)
                            tw, btw = tmpw.get()
                            tt(dve, tw[:], p2_[:], mask[:].unsqueeze(1).to_broadcast([128, 4, 128]), ALU.mult, [bp2, b_mask], [btw])
                            for gl in range(4):
                                g = g0 + gl
                                stt(dve, WY2[:, g, :], ident_f[:], Dcol[:, g:g + 1], tw[:, gl, :], ALU.mult, ALU.add,
                                    [b_identf, b_dcol, btw], [b_wy2])
                    S.barrier(); S.flush()
                S.barrier(); S.flush()
            if stop == "P1":
                return nc, S
            TC = T(st, "TC", [128, 64, 128], BF16)
            TS_ = T(st, "TS", [128, 64, 128], BF16)
            with ExitStack() as sp3:
                TCf = T(sp3, "TCf", [128, 32, 128]); TSf = T(sp3, "TSf", [128, 32, 128]); b_tf = Buf()
                ta = T(sp3, "ta", [128, 32, 64]); tb_ = T(sp3, "tb", [128, 32, 64]); b_ta = Buf(); b_tb = Buf()
                for hf in range(2):
                    gs = slice(hf * 32, (hf + 1) * 32)
                    cp(dve, TCf[:, :, 0], E1c[:, gs], [b_e1], [b_tf])
                    cp(dve, TSf[:, :, 0], E1s[:, gs], [b_e1], [b_tf])
                    n = 1
                    while n < 128:
                        cn = TCf[:, :, n - 1:n].to_broadcast([128, 32, n])
                        sn = TSf[:, :, n - 1:n].to_broadcast([128, 32, n])
                        tt(dve, ta[:, :, 0:n], TCf[:, :, 0:n], cn, ALU.mult, [b_tf], [b_ta])
                        tt(pool, tb_[:, :, 0:n], TSf[:, :, 0:n], sn, ALU.mult, [b_tf], [b_tb])
                        tt(dve, ta[:, :, 0:n], ta[:, :, 0:n], tb_[:, :, 0:n], ALU.subtract, [b_ta, b_tb], [b_ta])
                        tt(pool, tb_[:, :, 0:n], TCf[:, :, 0:n], sn, ALU.mult, [b_tf, b_ta], [b_tb])
                        cp(dve, TCf[:, :, n:2 * n], ta[:, :, 0:n], [b_ta], [b_tf])
                        tt(dve, ta[:, :, 0:n], TSf[:, :, 0:n], cn, ALU.mult, [b_tf], [b_ta])
                        tt(pool, TSf[:, :, n:2 * n], ta[:, :, 0:n], tb_[:, :, 0:n], ALU.add, [b_ta, b_tb], [b_tf])
                        n *= 2
                    cp(dve, TC[:, gs, :], TCf[:], [b_tf], [b_tc])
                    cp(pool, TS_[:, gs, :], TSf[:], [b_tf], [b_ts])
                S.barrier(); S.flush()

            if stop == "P":
                return nc, S
            with ExitStack() as s2:
                ubk = rot_sb(s2, "ubk", 1, [128, 64, 128], BF16, dma=True)
                xp = T(s2, "xp", [128, 64, 129], BF16); b_xp = [Buf() for _ in range(64)]
                carry = T(s2, "carry", [128, 64], F32); b_carry = [Buf() for _ in range(64)]
                gel_tm = rot_sb(s2, "geltm", 1, [128, 8, 1024], BF16)
                gelT = rot_sb(s2, "gelT", 1, [128, 8, 1024], BF16, dma=True)
                t1p = rot_sb(s2, "st1", 3, [128, 128]); t2p = rot_sb(s2, "st2", 3, [128, 128]); vp = rot_sb(s2, "sv", 5, [128, 128])
                Wp = rot_sb(s2, "sW", 5, [128, 128]); t3p = rot_sb(s2, "st3", 5, [128, 128]); t4p = rot_sb(s2, "st4", 3, [128, 128])
                pab = rot_ps(s2, "pab", 3, [128, 2, 128])
                psw_ = rot_ps(s2, "psw", 2, [128, 128])
                py = rot_ps(s2, "py", 2, [128, 128])
                ptr = rot_ps(s2, "ptr", 1, [128, 8, 128], BF16)
                rhop = rot_sb(s2, "rhot", 5, [128, 128])
                ones128 = T(s2, "ones128", [128, 128]); b_ones128 = Buf()
                mset(pool, ones128[:], 1.0, [b_ones128])
                lvl = {"2b1": 1, "2b2": 2, "2b3": 3}.get(stop, 4)
                mset(pool, carry[:], 0.0, b_carry)
                for b in range(4):
                    ub, bub, dsu = ubk.get()
                    sp.dma(ub[:], U_d[b].rearrange("p (g c) -> p g c", c=128), dsu, reads=[bU[b]], writes=[bub])
                    cp(pool, xp[:, :, 0], carry[:], b_carry, b_xp)
                    g_t, bgt = gel_tm.get()
                    G = {}

                    def s0(g):
                        ab, bab = pab.get()
                        mm(ab[:, 0, :], WXA[:, g, :], ub[:, g, :], True, True, [b_wxa, bub], [bab])
                        mm(ab[:, 1, :], WXB[:, g, :], ub[:, g, :], True, True, [b_wxb, bub], [bab])
                        G[g] = {"ab": (ab, bab)}

                    def s1(g):
                        ab, bab = G[g]["ab"]
                        t1, bt1 = t1p.get(); t2, bt2 = t2p.get()
                        tt(dve, t1[:], ab[:, 0, :], TC[:, g, :], ALU.mult, [bab, b_tc], [bt1])
                        tt(dve, t2[:], ab[:, 1, :], TS_[:, g, :], ALU.mult, [bab, b_ts], [bt2])
                        rt, brt = rhop.get()
                        actf(rt[:], ones128[:], AF.Copy, [b_ones128, b_rho], [brt], scale=RHO[:, g:g + 1])
                        G[g].update(t1=(t1, bt1), t2=(t2, bt2), rt=(rt, brt))

                    def s2(g):
                        (t1, bt1), (t2, bt2) = G[g]["t1"], G[g]["t2"]
                        v, bv = vp.get()
                        tt(pool, v[:], t1[:], t2[:], ALU.add, [bt1, bt2], [bv])
                        G[g]["v"] = (v, bv)

                    def s3(g):
                        (v, bv), (rt, brt) = G[g]["v"], G[g]["rt"]
                        W_, bW = Wp.get()
                        dve.op(lambda e, W_=W_, v=v, g=g, rt=rt: e.tensor_tensor_scan(out=W_[:], data0=rt[:], data1=v[:],
                                                                                     initial=carry[:, g:g + 1], op0=ALU.mult, op1=ALU.add),
                               [bv, brt, b_carry[g]], [bW])
                        G[g]["W"] = (W_, bW)

                    def s4(g):
                        W_, bW = G[g]["W"]
                        ws, bws = psw_.get()
                        mm(ws[:], PSW[:], W_[:], True, True, [b_psw, bW], [bws])
                        t3, bt3 = t3p.get()
                        tt(pool, t3[:], W_[:], TC[:, g, :], ALU.mult, [bW, b_tc], [bt3])
                        G[g].update(ws=(ws, bws), t3=(t3, bt3))

                    def s5(g):
                        ws, bws = G[g]["ws"]
                        t4, bt4 = t4p.get()
                        tt(dve, t4[:], ws[:], TS_[:, g, :], ALU.mult, [bws, b_ts], [bt4])
                        G[g]["t4"] = (t4, bt4)

                    def s6(g):
                        (t3, bt3), (t4, bt4) = G[g]["t3"], G[g]["t4"]
                        tt(pool, xp[:, g, 1:129], t3[:], t4[:], ALU.add, [bt3, bt4], [b_xp[g]])
                        actf(carry[:, g:g + 1], t3[:, 127:128], AF.Identity, [bt3, bt4], [b_carry[g]], bias=t4[:, 127:128])

                    def s7(g):
                        yb, byb = py.get()
                        mm(yb[:], xp[:, g, 0:128], WY1[:, g, :], True, False, [b_xp[g], b_wy1], [byb])
                        mm(yb[:], ub[:, g, :], WY2[:, g, :], False, True, [bub, b_wy2], [byb])
                        G[g]["y"] = (yb, byb)

                    def s8(g):
                        yb, byb = G.pop(g)["y"]
                        actf(g_t[:, :, g * 16:(g + 1) * 16], yb[:].rearrange("p (j c) -> p j c", c=16), AF.Gelu_apprx_tanh, [byb], [bgt])

                    stages = [s0, s1, s2, s3, s4, s5, s6, s7, s8]
                    for i in range(64 + 8):
                        for sidx in range(8, -1, -1):
                            g = i - sidx
                            if 0 <= g < 64:
                                stages[sidx](g)
                    if lvl < 4:
                        continue
                    gT, bgT, dsgT = gelT.get()
                    for ncu in range(8):
                        tp, btp = ptr.get()
                        for j in range(8):
                            trp(tp[:, j, :], g_t[:, j, ncu * 128:(ncu + 1) * 128], ident_b[:], [bgt, b_identb], [btp])
                        cp(dve if ncu % 2 == 0 else act, gT[:, ncu, :], tp[:].rearrange("p j m -> p (j m)"), [btp], [bgT])
                    sp.dma(gel_d[:, :, b * 1024:(b + 1) * 1024], gT[:], dsgT, reads=[bgT], writes=[bgel[b]])
                S.barrier(); S.flush()

        if stop in ("2b", "2b1", "2b2", "2b3"):
            return nc, S
        with ExitStack() as st:
            wst = rot_sb(st, "wst", 2, [128, 1024], F32, dma=True)
            w_glu = T(st, "w_glu", [128, 8, 1024], BF16); b_wglu = Buf()
            w_out = T(st, "w_out", [128, 8, 1024], BF16); b_wout = Buf()
            load_w_bf16(wst, w_glu, b_wglu, aglu_d, 1024, [pool, dve])
            load_w_bf16(wst, w_out, b_wout, aout_d, 1024, [pool, dve])
            gin = rot_sb(st, "gin", 2, [128, 8, 1024], BF16, dma=True)
            zin = rot_sb(st, "zin", 2, [128, 8, 1024], BF16, dma=True)
            y3 = rot_sb(st, "y3", 2, [128, 8, 1024], BF16)
            sigp = rot_sb(st, "sig", 3, [128, 512], BF16)
            xp_ = rot_sb(st, "xt", 4, [128, D], F32, dma=True)
            junk = rot_sb(st, "junk", 2, [128, D], BF16)
            ssq = rot_sb(st, "ssq", 4, [128, 4], F32)
            tbuf = rot_sb(st, "tbuf", 2, [128, D], F32)
            obuf = rot_sb(st, "obuf", 2, [128, D], F32, dma=True)
            pg = rot_ps(st, "pg", 3, [128, 512])
            po_ = rot_ps(st, "po", 2, [128, 1024])
            dq_o = S.dma_sem()
            ycur = {}

            def glu(b):
                gi, bgi, dsg = gin.get()
                zi, bzi, dsz = zin.get()
                sp.dma(gi[:], gel_d[:, :, b * 1024:(b + 1) * 1024], dsg, reads=[bgel[b]], writes=[bgi])
                sp.dma(zi[:], zs_d[:, :, b * 1024:(b + 1) * 1024], dsz, reads=[bzs[b]], writes=[bzi])
                y3t, by3 = y3.get()
                ycur[b] = (y3t, by3)
                for ec in range(8):
                    for hf in range(2):
                        p_, bp_ = pg.get()
                        for k in range(8):
                            mm(p_[:], w_glu[:, k, ec * 128:(ec + 1) * 128], gi[:, k, hf * 512:(hf + 1) * 512], k == 0, k == 7, [b_wglu, bgi], [bp_])
                        sg, bsg = sigp.get()
                        actf(sg[:], p_[:], AF.Sigmoid, [bp_, b_cols], [bsg], bias=BGLUc[:, ec:ec + 1])
                        tt(dve, sg[:], sg[:], gi[:, ec, hf * 512:(hf + 1) * 512], ALU.mult, [bsg, bgi], [bsg])
                        tt(pool, y3t[:, ec, hf * 512:(hf + 1) * 512], sg[:], zi[:, ec, hf * 512:(hf + 1) * 512], ALU.mult, [bsg, bzi], [by3])

            def outp(b):
                y3t, by3 = ycur.pop(b)
                for j in range(8):
                    xt, bx, dsx = xp_.get()
                    sp.dma(xt[:], xrows0[b, j], dsx, writes=[bx])
                    po, bpo = po_.get()
                    for nt in range(2):
                        for k in range(8):
                            mm(po[:, nt * 512:(nt + 1) * 512], y3t[:, k, j * 128:(j + 1) * 128], w_out[:, k, nt * 512:(nt + 1) * 512], k == 0, k == 7,
                               [by3, b_wout], [bpo])
                    postnorm(po[:], bpo, 0, xt, bx, junk, ssq, tbuf, obuf, h1rows0[b, j], bh1[b * 8 + j], dq_o)

            glu(0)
            for b in range(4):
                if b + 1 < 4:
                    glu(b + 1)
                outp(b)
            S.barrier(); S.flush()

        if stop == "h1":
            return nc, S

        def h1_deps(tt_):
            b = (tt_ * 128) // 1024
            return [bh1[b * 8 + j] for j in range(8)]

        with ExitStack() as st:
            wst = rot_sb(st, "wst", 2, [128, 2048], F32, dma=True)
            w_in = T(st, "w_in1", [128, 8, 2048], BF16); b_win = Buf()
            load_w_bf16(wst, w_in, b_win, bwin_d, 2048, [pool])
            xp_ = rot_sb(st, "xt", 4, [128, D], F32, dma=True)
            junk = rot_sb(st, "junk", 2, [128, D], BF16)
            ssq = rot_sb(st, "ssq", 4, [128, 4], F32)
            xs_pool = rot_sb(st, "xs", 3, [128, D], BF16)
            hT = rot_sb(st, "hT", 2, [128, 8, 512], BF16)
            qblk = rot_sb(st, "qblk", 2, [128, 8, 512], BF16, dma=True)
            zblk = rot_sb(st, "zblk", 2, [128, 4, 1024], BF16, dma=True)
            trps = rot_ps(st, "trps", 2, [128, 8, 128], BF16)
            pacc = rot_ps(st, "pacc", 4, [128, 512])
            dq_o = S.dma_sem(); dq_o2 = S.dma_sem()
            ne = [0]
            hcur = {}

            def pre(grp):
                h_t, bh = hT.get()
                hcur[grp] = (h_t, bh)
                xa = {}

                def A(tl):
                    tt_ = grp * 4 + tl
                    xt, bx, dsx = xp_.get()
                    sp.dma(xt[:], h1_d[tt_ * 128:(tt_ + 1) * 128, :], dsx, reads=h1_deps(tt_), writes=[bx])
                    xa[tl] = prenormA(xt, bx, junk, ssq, xs_pool)

                def B(tl):
                    xs, bxs = xa.pop(tl)
                    prenormB(xs, bxs, [(lambda k, h_t=h_t, tl=tl: h_t[:, k, tl * 128:(tl + 1) * 128], bh, A1c, S1c)], trps)

                A(0); A(1); B(0); A(2); B(1); A(3); B(2); B(3)

            def proj(grp):
                h_t, bh = hcur.pop(grp)
                qb_, bqb, dsqb = qblk.get()
                for h in range(8):
                    pa, bpa = pacc.get()
                    for k in range(8):
                        mm(pa[:], w_in[:, k, h * 128:(h + 1) * 128], h_t[:, k, :], k == 0, k == 7, [b_win, bh], [bpa])
                    cp(act, qb_[:, h, :], pa[:], [bpa], [bqb])
                    ne[0] += 1
                pool.dma(qT_d[:, :, grp * 512:(grp + 1) * 512], qb_[:], dsqb, reads=[bqb], writes=[bqT[grp]])
                zb_, bzb, dszb = zblk.get()
                for tl in range(4):
                    for nt in range(2):
                        pa, bpa = pacc.get()
                        for k in range(8):
                            mm(pa[:], h_t[:, k, tl * 128:(tl + 1) * 128], w_in[:, k, 1024 + nt * 512:1024 + (nt + 1) * 512], k == 0, k == 7,
                               [b_win, bh], [bpa])
                        actf(zb_[:, tl, nt * 512:(nt + 1) * 512], pa[:], AF.Silu, [bpa], [bzb])
                pool.dma(zs1_d[grp * 512:(grp + 1) * 512, :].rearrange("(t p) d -> p t d", p=128), zb_[:], dszb, reads=[bzb], writes=[bzs1[grp]])

            pre(0)
            for grp in range(8):
                if grp + 1 < 8:
                    pre(grp + 1)
                proj(grp)
            S.barrier(); S.flush()

        with ExitStack() as st:
            wst = rot_sb(st, "wst", 2, [128, 1024], F32, dma=True)
            w_k = T(st, "w_k", [128, 8, 1024], BF16); b_wk = Buf()
            w_v = T(st, "w_v", [128, 8, 1024], BF16); b_wv = Buf()
            load_w_bf16(wst, w_k, b_wk, wk_d, 1024, [pool, dve])
            load_w_bf16(wst, w_v, b_wv, wv_d, 1024, [pool, dve])
            xp_ = rot_sb(st, "xt", 4, [128, D], F32, dma=True)
            junk = rot_sb(st, "junk", 2, [128, D], BF16)
            ssq = rot_sb(st, "ssq", 4, [128, 4], F32)
            xs_pool = rot_sb(st, "xs", 3, [128, D], BF16)
            hT = rot_sb(st, "hT", 2, [128, 8, 512], BF16)
            kblk = rot_sb(st, "kblk", 2, [128, 8, 512], BF16, dma=True)
            vblk = rot_sb(st, "vblk", 2, [128, 8, 4, 129], BF16, dma=True)
            trps = rot_ps(st, "trps", 2, [128, 8, 128], BF16)
            pacc = rot_ps(st, "pacc", 4, [128, 512])
            for (vb_, bvb_, _d) in vblk.items:
                pool.op(lambda e, vb_=vb_: e.memset(vb_[:, :, :, 128:129], 1.0), (), [bvb_])
            ne = [0]
            hcur = {}

            def pre(grp):
                h_t, bh = hT.get()
                hcur[grp] = (h_t, bh)
                xa = {}

                def A(tl):
                    tt_ = grp * 4 + tl
                    xt, bx, dsx = xp_.get()
                    sp.dma(xt[:], h1_d[tt_ * 128:(tt_ + 1) * 128, :], dsx, reads=h1_deps(tt_), writes=[bx])
                    xa[tl] = prenormA(xt, bx, junk, ssq, xs_pool)

                def B(tl):
                    xs, bxs = xa.pop(tl)
                    prenormB(xs, bxs, [(lambda k, h_t=h_t, tl=tl: h_t[:, k, tl * 128:(tl + 1) * 128], bh, GKVc, None)], trps)

                A(0); A(1); B(0); A(2); B(1); A(3); B(2); B(3)

            def proj(grp):
                h_t, bh = hcur.pop(grp)
                kb_, bkb, dskb = kblk.get()
                for h in range(8):
                    pa, bpa = pacc.get()
                    for k in range(8):
                        mm(pa[:], w_k[:, k, h * 128:(h + 1) * 128], h_t[:, k, :], k == 0, k == 7, [b_wk, bh], [bpa])
                    cp(act, kb_[:, h, :], pa[:], [bpa], [bkb])
                    ne[0] += 1
                pool.dma(kT_d[:, :, grp * 512:(grp + 1) * 512], kb_[:], dskb, reads=[bkb], writes=[bkT[grp]])
                vb_, bvb, dsvb = vblk.get()
                for tl in range(4):
                    for nt in range(2):
                        pa, bpa = pacc.get()
                        for k in range(8):
                            mm(pa[:], h_t[:, k, tl * 128:(tl + 1) * 128], w_v[:, k, nt * 512:(nt + 1) * 512], k == 0, k == 7, [b_wv, bh], [bpa])
                        cp(act, vb_[:, nt * 4:(nt + 1) * 4, tl, 0:128], pa[:].rearrange("p (h e) -> p h e", e=128),
                           [bpa], [bvb])
                        ne[0] += 1
                pool.dma(v_d[:, :, grp * 4:(grp + 1) * 4, :].rearrange("h p t e -> p h (t e)"), vb_[:].rearrange("p h t e -> p h (t e)"),
                         dsvb, reads=[bvb], writes=[bvd[grp]])

            pre(0)
            for grp in range(8):
                if grp + 1 < 8:
                    pre(grp + 1)
                proj(grp)
            S.barrier(); S.flush()

        with ExitStack() as st:
            wst = rot_sb(st, "wst", 2, [128, 1024], F32, dma=True)
            w_o = T(st, "w_o", [128, 8, 1024], BF16); b_wo = Buf()
            qin = rot_sb(st, "qin", 2, [128, 2, 8, 512], BF16, dma=True)
            for (qt_, bqt_, _d) in qin.items:
                pool.op(lambda e, qt_=qt_: e.memset(qt_[64:128, 0], 0.0), (), [bqt_])
                pool.op(lambda e, qt_=qt_: e.memset(qt_[0:64, 1], 0.0), (), [bqt_])
            zin = rot_sb(st, "zin1", 2, [128, 4, 1024], BF16, dma=True)
            kin = rot_sb(st, "kin", 2, [128, L], BF16, dma=True)
            vin = rot_sb(st, "vin", 2, [128, 32, 129], BF16, dma=True)
            hin = rot_sb(st, "hin", 3, [128, D], F32, dma=True)
            ptp = rot_sb(st, "pt", 4, [128, 512], BF16)
            o0p = rot_sb(st, "o0", 2, [128, 4, 128], F32)
            odp = rot_sb(st, "od", 1, [128, 4, 8, 128], F32)
            rlp = rot_sb(st, "rl", 4, [128, 8], F32)
            sqp = rot_sb(st, "sq", 1, [128, 4, 8, 128], F32)
            ssn = rot_sb(st, "ssn", 2, [128, 2, 32], F32)
            yat = rot_sb(st, "yat", 1, [128, 4, 1024], BF16)
            yatT = rot_sb(st, "yatT", 2, [128, 8, 128], BF16)
            junk = rot_sb(st, "junk", 1, [128, D], BF16)
            ssq = rot_sb(st, "ssq", 4, [128, 4], F32)
            tbuf = rot_sb(st, "tbuf", 1, [128, D], F32)
            obuf = rot_sb(st, "obuf", 2, [128, D], F32, dma=True)
            big = PS(st, "big", [128, 3, 512])
            bbig = [Buf() for _ in range(3)]
            pss = Rot([(big[:, i, :], bbig[i]) for i in range(3)])
            pacc = [[(PS(st, f"acc{p_}{i}", [128, 2, 256]), Buf()) for i in range(2)] for p_ in range(2)]
            ptr = rot_ps(st, "ptr", 1, [128, 8, 128], BF16)
            LA = 3
            jobs = [(qb, h) for qb in range(8) for h in range(8)]
            kv = {}

            def load_kv(job):
                qb, h = job
                nk = 4 * qb + 4
                ki, bki, dsk = kin.get()
                sp.dma(ki[:, 0:nk * 128], kT_d[:, h, 0:nk * 128], dsk, reads=bkT[0:qb + 1], writes=[bki])
                vi, bvi, dsv = vin.get()
                sp.dma(vi[:, 0:nk, :], v_d[h, :, 0:nk, :], dsv, reads=bvd[0:qb + 1], writes=[bvi])
                kv[job] = (ki, bki, vi, bvi)

            TAIL_D = 64
            pending_tail = []

            def tail_pe(qb, ya, bya):
                for jq in range(4):
                    tt_ = qb * 4 + jq
                    xt, bx, dsx = hin.get()
                    sp.dma(xt[:], h1_d[tt_ * 128:(tt_ + 1) * 128, :], dsx, reads=h1_deps(tt_), writes=[bx])
                    tp, btp = ptr.get()
                    for k in range(8):
                        trp(tp[:, k, :], ya[:, jq, k * 128:(k + 1) * 128], ident_b[:], [bya, b_identb], [btp])
                    yT, byT = yatT.get()
                    cp(dve, yT[:], tp[:], [btp], [byT])
                    po = big[:, 0:2, :].rearrange("p a n -> p (a n)")
                    for nt in range(2):
                        for k in range(8):
                            mm(po[:, nt * 512:(nt + 1) * 512], yT[:, k, :], w_o[:, k, nt * 512:(nt + 1) * 512], k == 0, k == 7, [byT, b_wo], [bbig[0], bbig[1]])
                    bout = Buf()
                    postnorm(po, [bbig[0], bbig[1]], 1, xt, bx, junk, ssq, tbuf, obuf, out_d[tt_ * 128:(tt_ + 1) * 128, :], bout, pool)

            load_kv(jobs[0])
            par = [0]
            for qb in range(8):
                qi, bqi, dsq = qin.get()
                sp.dma(qi[0:64, 0], qT_d[0:64, :, qb * 512:(qb + 1) * 512], dsq, reads=[bqT[qb]], writes=[bqi])
                sp.dma(qi[64:128, 1], qT_d[64:128, :, qb * 512:(qb + 1) * 512], dsq, reads=[bqT[qb]], writes=[bqi])
                zi, bzi, dsz = zin.get()
                sp.dma(zi[:], zs1_d[qb * 512:(qb + 1) * 512, :].rearrange("(t p) d -> p t d", p=128), dsz, reads=[bzs1[qb]], writes=[bzi])
                if qb == 0:
                    load_w_bf16(wst, w_o, b_wo, bout_d, 1024, [pool, dve])
                od, bod = odp.get()
                nk = 4 * qb + 4
                items = [(h, cc, kt) for h in range(8) for kt in range(nk) for cc in range(2)]
                pend = []
                o0s = {}
                accs = {}

                def stageA(it):
                    h, cc, kt = it
                    if cc == 0 and kt == LA:
                        ji = jobs.index((qb, h))
                        if ji + 1 < len(jobs):
                            load_kv(jobs[ji + 1])
                    if cc == 0 and kt == 0:
                        o0s[h] = o0p.get()
                    if kt == 0:
                        accs[(h, cc)] = pacc[cc]
                    ki, bki, vi, bvi = kv[(qb, h)]
                    ps_ = slice(cc * 64, (cc + 1) * 64)
                    r = kt - 4 * qb
                    q0 = max(r, 0) * 128
                    s_, bs_ = pss.get()
                    mm(s_[:, q0:512], ki[:, kt * 128:(kt + 1) * 128], qi[:, cc, h, q0:512], True, r < 0, [bki, bqi], [bs_])
                    if r >= 0:
                        mm(s_[:, q0:q0 + 128], ident_b[:], cmask_b[:], False, True, [b_identb, b_cmask], [bs_])
                    pt, bpt = ptp.get()
                    actf(pt[:, q0:512], s_[:, q0:512], AF.Exp, [bs_], [bpt], scale=0.125)
                    return (pt, bpt)

                def stageC(it, pt, bpt):
                    h, cc, kt = it
                    ki, bki, vi, bvi = kv[(qb, h)]
                    r = kt - 4 * qb
                    pa_ = accs[(h, cc)]
                    for jq in range(max(r, 0), 4):
                        acc, bacc = pa_[jq // 2]
                        mm(acc[:, jq % 2, 0:129], pt[:, jq * 128:(jq + 1) * 128], vi[:, kt, :],
                           kt == 0 and jq % 2 == 0, kt == 4 * qb + jq, [bpt, bvi], [bacc], sgc=True)
                    if kt != nk - 1:
                        return
                    o0, bo0 = o0s[h]
                    rl, brl = rlp.get()
                    for jq in range(4):
                        acc, bacc = pa_[jq // 2]
                        recip(rl[:, jq:jq + 1], acc[:, jq % 2, 128:129], [bacc], [brl])
                    if cc == 0:
                        for jq in range(4):
                            acc, bacc = pa_[jq // 2]
                            ts(dve, o0[:, jq, :], acc[:, jq % 2, 0:128], rl[:, jq:jq + 1], None, ALU.mult, None, [bacc, brl], [bo0])
                    else:
                        ts(dve, rl[:, 4:8], rl[:, 0:4], neglam[:, 0:1], None, ALU.mult, None, [brl, b_neglam], [brl])
                        for jq in range(4):
                            acc, bacc = pa_[jq // 2]
                            stt(dve, od[:, jq, h, :], acc[:, jq % 2, 0:128], rl[:, 4 + jq:5 + jq], o0[:, jq, :], ALU.mult, ALU.add,
                                [bacc, brl, bo0], [bod])

                for i in range(len(items) + LA):
                    if i < len(items):
                        pend.append(stageA(items[i]))
                    if i >= LA:
                        stageC(items[i - LA], *pend.pop(0))
                    if i == TAIL_D and pending_tail:
                        pending_tail.pop(0)()
                sq, bsq = sqp.get()
                sn, bsn = ssn.get()
                tt(pool, sq[:], od[:], od[:], ALU.mult, [bod], [bsq])
                dve.op(lambda e, sn=sn, sq=sq: e.reduce_sum(out=sn[:, 0, :], in_=sq[:].rearrange("p a h e -> p (a h) e"), axis=AX.X), [bsq], [bsn])
                actf(sn[:, 1, :], sn[:, 0, :], AF.Ln, [bsn, b_eps], [bsn], scale=1.0 / 128, bias=eps_col[:, 0:1])
                actf(sn[:, 1, :], sn[:, 1, :], AF.Exp, [bsn], [bsn], scale=-0.5)
                tt(dve, sq[:], od[:], sn[:, 1, :].rearrange("p (a h) -> p a h", h=8).unsqueeze(3).to_broadcast([128, 4, 8, 128]), ALU.mult,
                   [bod, bsn], [bsq])
                tt(pool, sq[:], sq[:], GS[:].unsqueeze(1).unsqueeze(1).to_broadcast([128, 4, 8, 128]), ALU.mult, [bsq, b_GS], [bsq])
                ya, bya = yat.get()
                tt(dve, ya[:], sq[:].rearrange("p a h e -> p a (h e)"), zi[:], ALU.mult, [bsq, bzi], [bya])
                pending_tail.append(lambda qb=qb, ya=ya, bya=bya: tail_pe(qb, ya, bya))
            while pending_tail:
                pending_tail.pop(0)()
            S.barrier(); S.flush()
    return nc, S


_CACHE = {}


def _get_program(stop=None):
    key = stop
    if key not in _CACHE:
        nc, S = build(None, stop)
        nc, S = build(S.record, stop)
        _CACHE[key] = nc
    return _CACHE[key]


def _prep_inputs(inp):
    f = lambda a: np.ascontiguousarray(np.asarray(a, dtype=np.float32))
    x = f(inp["x"]); c = f(inp["c"])
    dup = lambda a: np.ascontiguousarray(np.concatenate([a.T, a.T], 0))
    lam_re = f(inp["a_lam_re"])[0]; lam_im = f(inp["a_lam_im"])[0]; log_dt = f(inp["a_log_dt"])[0]
    b_re = f(inp["a_b_re"])[0]; b_im = f(inp["a_b_im"])[0]; c_re = f(inp["a_c_re"])[0]; c_im = f(inp["a_c_im"])[0]
    bre_t = b_re.transpose(1, 0, 2); bim_t = b_im.transpose(1, 0, 2)
    cre_t = c_re.transpose(2, 0, 1); cim_t = c_im.transpose(2, 0, 1)
    shared = {
        "ada_w": f(inp["ada_w"]), "ada_b": f(inp["ada_b"]), "g_pre": f(inp["g_pre"]), "g_post": f(inp["g_post"]),
        "gkv_col": np.ascontiguousarray(f(inp["g_kv"]).reshape(8, 128).T),
        "a_w_in": f(inp["a_w_in"])[0], "a_w_glu": f(inp["a_w_glu"])[0], "a_w_out": f(inp["a_w_out"])[0],
        "bglu_col": np.ascontiguousarray(f(inp["a_b_glu"])[0].reshape(8, 128).T),
        "w_k": f(inp["w_k"]), "w_v": f(inp["w_v"]), "b_w_in": f(inp["b_w_in"])[0], "b_w_out": f(inp["b_w_out"])[0],
        "lamre2": dup(lam_re), "lamim2": dup(lam_im),
        "logdt2": np.ascontiguousarray(np.broadcast_to(log_dt[None, :], (128, 64))),
        "bst": np.ascontiguousarray(np.concatenate([bre_t, bim_t], 0).reshape(128, 1024)),
        "bsw": np.ascontiguousarray(np.concatenate([bim_t, bre_t], 0).reshape(128, 1024)),
        "cst": np.ascontiguousarray(np.concatenate([cre_t, cim_t], 0).reshape(128, 1024)),
        "csw": np.ascontiguousarray(np.concatenate([cim_t, cre_t], 0).reshape(128, 1024)),
        "dcol": np.ascontiguousarray(np.tile(f(inp["a_d"])[0].reshape(64, 16).T, (8, 1))),
        "lqk": np.ascontiguousarray(np.concatenate([f(inp["b_lq1"])[0], f(inp["b_lk1"])[0], f(inp["b_lq2"])[0], f(inp["b_lk2"])[0]])[None, :]),
        "gsub": np.ascontiguousarray(f(inp["b_g_sub"])[0][None, :]),
    }
    maps = []
    for b in range(x.shape[0]):
        m = dict(shared)
        m["x"] = np.ascontiguousarray(x[b])
        m["cT"] = np.ascontiguousarray(c[b].reshape(8, 128).T)
        maps.append(m)
    return maps


def kernel(**inputs):
    stop = os.environ.get("MK_STOP") or None
    nc = _get_program(stop)
    maps = _prep_inputs(inputs)
    ncores = int(os.environ.get("MK_CORES", "8"))
    maps = maps[:ncores]
    res = run_bass_kernel_spmd(nc, maps, core_ids=list(range(len(maps))))
    outs = [np.asarray(r["out"], dtype=np.float32) for r in res.results]
    return np.stack(outs, 0)
```

```python
import math
import os
from contextlib import ExitStack

import numpy as np
import concourse.bass as bass
import concourse.mybir as mybir
from concourse.bass_utils import run_bass_kernel_spmd

F32 = mybir.dt.float32
BF16 = mybir.dt.bfloat16
I32 = mybir.dt.int32
AF = mybir.ActivationFunctionType
ALU = mybir.AluOpType
AX = mybir.AxisListType

L = 4096
D = 1024
EPS = 1e-6
LAMBDA_INIT = 0.8 - 0.6 * math.exp(-0.3 * 1)
NEG = -30000.0


class Buf:
    __slots__ = ("name", "w", "r")

    def __init__(self, name=""):
        self.name = name
        self.w = None
        self.r = []


class Tok:
    __slots__ = ("eng", "seq", "sem", "val")

    def __init__(self, eng, seq, sem, val):
        self.eng = eng
        self.seq = seq
        self.sem = sem
        self.val = val


class DmaSem:
    def __init__(self, sem, key):
        self.sem = sem
        self.key = key
        self.n = 0


class EngW:
    def __init__(self, sched, key, sem):
        self.sched = sched
        self.key = key
        self.sem = sem
        self.seq = 0
        self.cnt = 0
        self.waited_seq = {}
        self.waited_dma = {}
        self.prog = []
        self.last = None

    def _gather(self, reads, writes):
        deps = []
        for b in reads:
            if b.w is not None:
                deps.append(b.w)
        for b in writes:
            if b.w is not None:
                deps.append(b.w)
            deps.extend(b.r)
        return deps

    def _wait(self, tok):
        if tok.eng is None:
            k = tok.sem.key
            if self.waited_dma.get(k, 0) >= tok.val:
                return
            self.waited_dma[k] = tok.val
            sem, val = tok.sem.sem, tok.val
            self.prog.append(lambda e, sem=sem, val=val: e.wait_ge(sem, val))
            return
        if tok.eng is self and self.key == "pe":
            return
        k = tok.eng.key
        if self.waited_seq.get(k, -1) >= tok.seq:
            return
        self.waited_seq[k] = tok.seq
        self.sched.record.add((k, tok.seq))
        if tok.val is None:
            raise RuntimeError(f"token {k}:{tok.seq} not marked")
        sem, val = tok.sem, tok.val
        self.prog.append(lambda e, sem=sem, val=val: e.wait_ge(sem, val))

    def op(self, fn, reads=(), writes=()):
        for t in self._gather(reads, writes):
            self._wait(t)
        seq = self.seq
        self.seq += 1
        needed = self.sched.needed
        mark = needed is None or (self.key, seq) in needed
        if mark:
            self.cnt += 1
            sem = self.sem
            self.prog.append(lambda e, fn=fn, sem=sem: fn(e).then_inc(sem, 1))
            tok = Tok(self, seq, self.sem, self.cnt)
        else:
            self.prog.append(lambda e, fn=fn: fn(e))
            tok = Tok(self, seq, self.sem, None)
        self.last = tok
        for b in reads:
            b.r.append(tok)
        for b in writes:
            b.w = tok
            b.r = []
        return tok

    def dma(self, out, in_, dsem, reads=(), writes=()):
        for t in self._gather(reads, writes):
            self._wait(t)
        dsem.n += 1
        val = 16 * dsem.n
        sem = dsem.sem
        self.prog.append(lambda e, out=out, in_=in_, sem=sem: e.dma_start(out=out, in_=in_).then_inc(sem, 16))
        tok = Tok(None, -1, dsem, val)
        for b in reads:
            b.r.append(tok)
        for b in writes:
            b.w = tok
            b.r = []
        return tok


class Sched:
    def __init__(self, nc, stack, needed=None):
        self.nc = nc
        self.needed = needed
        self.record = set()
        self.stack = stack
        mk = lambda n: stack.enter_context(nc.semaphore(n))
        self.pe = EngW(self, "pe", mk("s_pe"))
        self.act = EngW(self, "act", mk("s_act"))
        self.dve = EngW(self, "dve", mk("s_dve"))
        self.pool = EngW(self, "pool", mk("s_pool"))
        self.sp = EngW(self, "sp", mk("s_sp"))
        self.engs = [self.pe, self.act, self.dve, self.pool, self.sp]
        self.ndsem = 0
        self.dsems = []

    def dma_sem(self):
        self.ndsem += 1
        key = f"dq{self.ndsem}"
        ds = DmaSem(self.stack.enter_context(self.nc.semaphore(key)), key)
        self.dsems.append(ds)
        return ds

    def barrier(self):
        toks = [w.last for w in self.engs[:4] if w.last is not None]
        for w in self.engs:
            for t in toks:
                if t.eng is not w:
                    w._wait(t)
            for ds in self.dsems:
                if ds.n > 0:
                    w._wait(Tok(None, -1, ds, 16 * ds.n))

    def flush(self):
        nc = self.nc
        with nc.Block() as block:
            @block.tensor
            def _(e):
                for f in self.pe.prog:
                    f(e)

            @block.scalar
            def _(e):
                for f in self.act.prog:
                    f(e)

            @block.vector
            def _(e):
                for f in self.dve.prog:
                    f(e)

            @block.gpsimd
            def _(e):
                for f in self.pool.prog:
                    f(e)

            @block.sync
            def _(e):
                for f in self.sp.prog:
                    f(e)
        for w in self.engs:
            w.prog = []


class Rot:
    def __init__(self, items):
        self.items = items
        self.i = 0

    def get(self):
        it = self.items[self.i % len(self.items)]
        self.i += 1
        return it


def build(needed=None, stop=None):
    nc = bass.Bass("TRN2", target_bir_lowering=False)
    di = lambda n, s: nc.dram_tensor(n, s, F32, kind="ExternalInput").ap()
    x_d = di("x", [L, D])
    cT_d = di("cT", [128, 8])
    adaw_d = di("ada_w", [2, D, 3 * D])
    adab_d = di("ada_b", [2, 3 * D])
    gpre_d = di("g_pre", [2, D])
    gpost_d = di("g_post", [2, D])
    gkvc_d = di("gkv_col", [128, 8])
    awin_d = di("a_w_in", [D, 2 * D])
    aglu_d = di("a_w_glu", [D, D])
    aout_d = di("a_w_out", [D, D])
    bgluc_d = di("bglu_col", [128, 8])
    wk_d = di("w_k", [D, D])
    wv_d = di("w_v", [D, D])
    bwin_d = di("b_w_in", [D, 2 * D])
    bout_d = di("b_w_out", [D, D])
    lamre_d = di("lamre2", [128, 64])
    lamim_d = di("lamim2", [128, 64])
    logdt_d = di("logdt2", [128, 64])
    bst_d = di("bst", [128, 1024])
    bsw_d = di("bsw", [128, 1024])
    cst_d = di("cst", [128, 1024])
    csw_d = di("csw", [128, 1024])
    dcol_d = di("dcol", [128, 64])
    lqk_d = di("lqk", [1, 256])
    gsub_d = di("gsub", [1, 128])
    out_d = nc.dram_tensor("out", [L, D], F32, kind="ExternalOutput").ap()
    scr = lambda n, s, d: nc.dram_tensor(n, s, d, kind="Internal").ap()
    U_d = scr("U_s", [4, 128, 8192], BF16)
    zs_d = scr("zs_s", [128, 8, L], BF16)
    gel_d = scr("gel_s", [128, 8, L], BF16)
    if stop == "h1":
        h1_d = out_d
    else:
        h1_d = scr("h1_s", [L, D], F32)
    qT_d = scr("qT_s", [128, 8, L], BF16)
    zs1_d = scr("zs1_s", [L, D], BF16)
    kT_d = scr("kT_s", [128, 8, L], BF16)
    v_d = scr("v_s", [8, 128, 32, 129], BF16)
    bkT = [Buf() for _ in range(8)]
    bvd = [Buf() for _ in range(8)]
    bU = [Buf() for _ in range(4)]
    bzs = [Buf() for _ in range(4)]
    bgel = [Buf() for _ in range(4)]
    bh1 = [Buf() for _ in range(32)]
    bqT = [Buf() for _ in range(8)]
    bzs1 = [Buf() for _ in range(8)]

    top = ExitStack()
    with top:
        S = Sched(nc, top, needed)
        pe, act, dve, pool, sp = S.pe, S.act, S.dve, S.pool, S.sp

        uid = [0]

        def T(st, n, s, d=F32):
            uid[0] += 1
            return st.enter_context(nc.sbuf_tensor(f"sb{uid[0]}_{n}", s, d))

        def PS(st, n, s, d=F32):
            uid[0] += 1
            return st.enter_context(nc.psum_tensor(f"ps{uid[0]}_{n}", s, d))

        def mm(out, lhsT, rhs, start, stop_, reads, writes, sgc=False):
            return pe.op(lambda e: e.matmul(out, lhsT=lhsT, rhs=rhs, start=start, stop=stop_, skip_group_check=sgc), reads, writes)

        def trp(out, in_, ident, reads, writes):
            return pe.op(lambda e: e.transpose(out, in_, ident), reads, writes)

        def actf(out, in_, func, reads, writes, **kw):
            return act.op(lambda e: e.activation(out=out, in_=in_, func=func, **kw), reads, writes)

        def tt(eng, out, in0, in1, op, reads, writes):
            return eng.op(lambda e: e.tensor_tensor(out=out, in0=in0, in1=in1, op=op), reads, writes)

        def ts(eng, out, in0, s1, s2, op0, op1, reads, writes):
            if s2 is None:
                return eng.op(lambda e: e.tensor_scalar(out=out, in0=in0, scalar1=s1, scalar2=None, op0=op0), reads, writes)
            return eng.op(lambda e: e.tensor_scalar(out=out, in0=in0, scalar1=s1, scalar2=s2, op0=op0, op1=op1), reads, writes)

        def stt(eng, out, in0, scalar, in1, op0, op1, reads, writes):
            return eng.op(lambda e: e.scalar_tensor_tensor(out=out, in0=in0, scalar=scalar, in1=in1, op0=op0, op1=op1), reads, writes)

        def cp(eng, out, in_, reads, writes):
            if eng is act:
                return act.op(lambda e: e.copy(out=out, in_=in_), reads, writes)
            return eng.op(lambda e: e.tensor_copy(out=out, in_=in_), reads, writes)

        def recip(out, in_, reads, writes):
            return dve.op(lambda e: e.reciprocal(out=out, in_=in_), reads, writes)

        def mset(eng, ap, val, writes):
            return eng.op(lambda e: e.memset(ap, val), (), writes)

        def rot_sb(st, name, n, shape, dt=F32, dma=False):
            items = []
            for i in range(n):
                t = T(st, f"{name}{i}", shape, dt)
                if dma:
                    items.append((t, Buf(), S.dma_sem()))
                else:
                    items.append((t, Buf()))
            return Rot(items)

        def rot_ps(st, name, n, shape, dt=F32):
            return Rot([(PS(st, f"{name}{i}", shape, dt), Buf()) for i in range(n)])

        def rot_ps_sub(st, name, nbanks, nsub, subshape, dt=F32):
            items = []
            for i in range(nbanks):
                t = PS(st, f"{name}{i}", [128, nsub] + list(subshape), dt)
                for j in range(nsub):
                    items.append((t[:, j], Buf()))
            return Rot(items)

        def load_w_bf16(st_pool, dst, bdst, src_d, ncols, cast_engs):
            for k in range(8):
                stg, bs, ds = st_pool.get()
                sp.dma(stg[:, 0:ncols], src_d[k * 128:(k + 1) * 128, :], ds, writes=[bs])
                eng = cast_engs[k % len(cast_engs)]
                cp(eng, dst[:, k, :], stg[:, 0:ncols], [bs], [bdst])

        ident_b = T(top, "ident_b", [128, 128], BF16); b_identb = Buf()
        ident_f = T(top, "ident_f", [128, 128], F32); b_identf = Buf()
        cmask_b = T(top, "cmask_b", [128, 128], BF16); b_cmask = Buf()
        ones_row = T(top, "ones_row", [1, 128], F32); b_ones = Buf()
        eps_col = T(top, "eps_col", [128, 1], F32); b_eps = Buf()
        sgn_col = T(top, "sgn_col", [128, 1], F32); b_sgn = Buf()
        cols = T(top, "cols", [128, 48], F32); b_cols = Buf()
        GG = T(top, "GG", [128, 2, D], F32); b_GG = Buf()
        GS = T(top, "GS", [128, 128], F32); b_GS = Buf()
        neglam = T(top, "neglam", [128, 2], F32); b_neglam = Buf()

        mset(pool, ident_f[:], 1.0, [b_identf])
        pool.op(lambda e: e.affine_select(out=ident_f[:], in_=ident_f[:], pattern=[[-1, 128]], compare_op=ALU.is_equal,
                                          fill=0.0, base=0, channel_multiplier=1), [b_identf], [b_identf])
        cp(pool, ident_b[:], ident_f[:], [b_identf], [b_identb])
        mset(pool, cmask_b[:], 0.0, [b_cmask])
        pool.op(lambda e: e.affine_select(out=cmask_b[:], in_=cmask_b[:], pattern=[[1, 128]], compare_op=ALU.is_ge,
                                          fill=NEG, base=0, channel_multiplier=-1), [b_cmask], [b_cmask])
        mset(pool, ones_row[:], 1.0, [b_ones])
        mset(pool, eps_col[:], EPS, [b_eps])
        mset(pool, sgn_col[0:64, :], -1.0, [b_sgn])
        mset(pool, sgn_col[64:128, :], 1.0, [b_sgn])

        with ExitStack() as st:
            dq = S.dma_sem()
            ct = T(st, "ct", [128, 8]); b_ct = Buf()
            sc = T(st, "sc", [128, 8]); b_sc = Buf()
            rows = T(st, "rows", [1, 2, 3 * D]); b_rows = Buf()
            adab = T(st, "adab", [1, 2, 3 * D]); b_adab = Buf()
            gpr = T(st, "gpr", [1, 2, D]); b_gpr = Buf()
            gpo = T(st, "gpo", [1, 2, D]); b_gpo = Buf()
            arow = T(st, "arow", [1, 2, D]); b_arow = Buf()
            ggrow = T(st, "ggrow", [1, 2, D]); b_ggrow = Buf()
            lqk = T(st, "lqk", [1, 256]); b_lqk = Buf()
            gsr = T(st, "gsr", [1, 128]); b_gsr = Buf()
            sm = T(st, "sm", [1, 16]); b_sm = Buf()
            awp = rot_sb(st, "awst", 3, [128, 8, 512], F32, dma=True)
            pmod = rot_ps(st, "pmod", 2, [1, 512])
            pcol = PS(st, "pcol", [128, 64]); b_pcol = Buf()
            pbc = rot_ps(st, "pbc", 2, [128, 512])

            sp.dma(ct[:], cT_d, dq, writes=[b_ct])
            sp.dma(adab[0:1, 0, :], adab_d[0:1, :], dq, writes=[b_adab])
            sp.dma(adab[0:1, 1, :], adab_d[1:2, :], dq, writes=[b_adab])
            sp.dma(gpr[0:1, 0, :], gpre_d[0:1, :], dq, writes=[b_gpr])
            sp.dma(gpr[0:1, 1, :], gpre_d[1:2, :], dq, writes=[b_gpr])
            sp.dma(gpo[0:1, 0, :], gpost_d[0:1, :], dq, writes=[b_gpo])
            sp.dma(gpo[0:1, 1, :], gpost_d[1:2, :], dq, writes=[b_gpo])
            sp.dma(lqk[:], lqk_d, dq, writes=[b_lqk])
            sp.dma(gsr[:], gsub_d, dq, writes=[b_gsr])
            sp.dma(cols[:, 32:40], gkvc_d, dq, writes=[b_cols])
            sp.dma(cols[:, 40:48], bgluc_d, dq, writes=[b_cols])
            actf(sc[:], ct[:], AF.Silu, [b_ct], [b_sc])
            for l in range(2):
                for nt in range(6):
                    stg, bs, ds = awp.get()
                    sp.dma(stg[:], adaw_d[l].rearrange("(k p) n -> p k n", p=128)[:, :, nt * 512:(nt + 1) * 512], ds, writes=[bs])
                    pm, bpm = pmod.get()
                    for k in range(8):
                        mm(pm[:], sc[:, k:k + 1], stg[:, k, :], k == 0, k == 7, [b_sc, bs], [bpm])
                    tt(dve, rows[0:1, l, nt * 512:(nt + 1) * 512], pm[:], adab[0:1, l, nt * 512:(nt + 1) * 512], ALU.add,
                       [bpm, b_adab], [b_rows])
            for l in range(2):
                stt(dve, arow[0:1, l, :], rows[0:1, l, D:2 * D], 1.0, gpr[0:1, l, :], ALU.add, ALU.mult, [b_rows, b_gpr], [b_arow])
                tt(dve, ggrow[0:1, l, :], rows[0:1, l, 2 * D:3 * D], gpo[0:1, l, :], ALU.mult, [b_rows, b_gpo], [b_ggrow])
            for l in range(2):
                for which in range(2):
                    idx = l * 2 + which
                    for k in range(8):
                        src = arow[0:1, l, k * 128:(k + 1) * 128] if which == 0 else rows[0:1, l, k * 128:(k + 1) * 128]
                        c0 = (idx * 8 + k) * 2
                        mm(pcol[:, c0:c0 + 2], src, ones_row[0:1, 0:2], True, True, [b_arow, b_rows, b_ones], [b_pcol])
            cp(dve, cols[:, 0:32], pcol[:].rearrange("p (a two) -> p a two", two=2)[:, :, 0], [b_pcol], [b_cols])
            for l in range(2):
                for hf in range(2):
                    pb, bpb = pbc.get()
                    mm(pb[:], ones_row[0:1, 0:128], ggrow[0:1, l, hf * 512:(hf + 1) * 512], True, True, [b_ones, b_ggrow], [bpb])
                    cp(act, GG[:, l, hf * 512:(hf + 1) * 512], pb[:], [bpb], [b_GG])
            tt(dve, lqk[0:1, 0:64], lqk[0:1, 0:64], lqk[0:1, 64:128], ALU.mult, [b_lqk], [b_lqk])
            tt(dve, lqk[0:1, 128:192], lqk[0:1, 128:192], lqk[0:1, 192:256], ALU.mult, [b_lqk], [b_lqk])
            dve.op(lambda e: e.reduce_sum(out=sm[0:1, 0:1], in_=lqk[0:1, 0:64], axis=AX.X), [b_lqk], [b_sm])
            dve.op(lambda e: e.reduce_sum(out=sm[0:1, 1:2], in_=lqk[0:1, 128:192], axis=AX.X), [b_lqk], [b_sm])
            actf(sm[0:1, 2:4], sm[0:1, 0:2], AF.Exp, [b_sm], [b_sm])
            tt(dve, sm[0:1, 4:5], sm[0:1, 3:4], sm[0:1, 2:3], ALU.subtract, [b_sm], [b_sm])
            ts(dve, sm[0:1, 6:8], sm[0:1, 4:5].to_broadcast([1, 2]), -LAMBDA_INIT, None, ALU.add, None, [b_sm], [b_sm])
            pb, bpb = pbc.get()
            mm(pb[:, 0:2], ones_row[0:1, 0:128], sm[0:1, 6:8], True, True, [b_ones, b_sm], [bpb])
            cp(dve, neglam[:], pb[:, 0:2], [bpb], [b_neglam])
            pb, bpb = pbc.get()
            mm(pb[:, 0:128], ones_row[0:1, 0:128], gsr[0:1, :], True, True, [b_ones, b_gsr], [bpb])
            ts(dve, GS[:], pb[:, 0:128], 1.0 - LAMBDA_INIT, None, ALU.mult, None, [bpb], [b_GS])
            S.barrier(); S.flush()

        if stop == "C":
            return nc, S
        A0c, S0c, A1c, S1c, GKVc, BGLUc = (cols[:, 0:8], cols[:, 8:16], cols[:, 16:24], cols[:, 24:32], cols[:, 32:40], cols[:, 40:48])

        def prenormA(xt, bx, junk, ssq, xs_pool):
            jk, bj = junk.get()
            s_t, bs_ = ssq.get()
            mset(pool, s_t[:, 0:1], 0.0, [bs_])
            actf(jk[:], xt[:], AF.Square, [bx], [bj, bs_], accum_out=s_t[:, 0:1])
            actf(s_t[:, 1:2], s_t[:, 0:1], AF.Sqrt, [bs_, b_eps], [bs_], scale=1.0 / D, bias=eps_col[:, 0:1])
            recip(s_t[:, 2:3], s_t[:, 1:2], [bs_], [bs_])
            xs, bxs = xs_pool.get()
            ts(dve, xs[:], xt[:], s_t[:, 2:3], None, ALU.mult, None, [bx, bs_], [bxs])
            return xs, bxs

        def prenormB(xs, bxs, dsts, trps):
            tp, btp = trps.get()
            for k in range(8):
                trp(tp[:, k, :], xs[:, k * 128:(k + 1) * 128], ident_b[:], [bxs, b_identb], [btp])
            for (dfn, bd, scl, bia) in dsts:
                for k in range(8):
                    if bia is None:
                        ts(dve, dfn(k), tp[:, k, :], scl[:, k:k + 1], None, ALU.mult, None, [btp, b_cols], [bd])
                    else:
                        ts(dve, dfn(k), tp[:, k, :], scl[:, k:k + 1], bia[:, k:k + 1], ALU.mult, ALU.add, [btp, b_cols], [bd])

        def prenorm(st_res, xt, bx, dsts, junk, ssq, xs_pool, trps, ridx):
            xs, bxs = prenormA(xt, bx, junk, ssq, xs_pool)
            prenormB(xs, bxs, dsts, trps)

        def postnorm(po, bpo, l, xt, bx, junk, ssq, tbuf, obuf, dst_ap, bdst, dq_out):
            jk, bj = junk.get()
            s_t, bs_ = ssq.get()
            mset(pool, s_t[:, 0:1], 0.0, [bs_])
            bpo = bpo if isinstance(bpo, list) else [bpo]
            actf(jk[:], po, AF.Square, bpo, [bj, bs_], accum_out=s_t[:, 0:1])
            actf(s_t[:, 1:2], s_t[:, 0:1], AF.Sqrt, [bs_, b_eps], [bs_], scale=1.0 / D, bias=eps_col[:, 0:1])
            recip(s_t[:, 2:3], s_t[:, 1:2], [bs_], [bs_])
            tb, btb = tbuf.get()
            stt(dve, tb[:], po, s_t[:, 2:3], GG[:, l, :], ALU.mult, ALU.mult, bpo + [bs_, b_GG], [btb])
            ob, bob, dso = obuf.get()
            tt(pool, ob[:], tb[:], xt[:], ALU.add, [btb, bx], [bob])
            (dq_out if isinstance(dq_out, EngW) else sp).dma(dst_ap, ob[:], dso, reads=[bob], writes=[bdst])

        xrows0 = x_d.rearrange("(b p j) d -> b j p d", p=128, j=8)
        h1rows0 = h1_d.rearrange("(b p j) d -> b j p d", p=128, j=8)
        with ExitStack() as st:
            wst = rot_sb(st, "wst", 2, [128, 2048], F32, dma=True)
            w_in = T(st, "w_in", [128, 8, 2048], BF16); b_win = Buf()
            xp_ = rot_sb(st, "xt", 4, [128, D], F32, dma=True)
            junk = rot_sb(st, "junk", 2, [128, D], BF16)
            ssq = rot_sb(st, "ssq", 4, [128, 4], F32)
            xs_pool = rot_sb(st, "xs", 3, [128, D], BF16)
            hT = rot_sb(st, "hT", 2, [128, 8, 1024], BF16)
            utm = rot_sb(st, "utm", 1, [128, 64, 8, 16], BF16)
            ublk = rot_sb(st, "ublk", 1, [128, 64, 128], BF16, dma=True)
            zsT = rot_sb(st, "zsT", 1, [128, 8, 1024], BF16, dma=True)
            trps = rot_ps(st, "trps", 2, [128, 8, 128], BF16)
            pacc = rot_ps(st, "pacc", 6, [128, 512])
            dq_o = S.dma_sem(); dq_o2 = S.dma_sem()
            load_w_bf16(wst, w_in, b_win, awin_d, 2048, [pool])
            ne = [0]
            hslots = [hT.get() for _ in range(2)]
            hbufs = [[Buf() for _ in range(8)] for _ in range(2)]
            cur_u = {}

            xsd = {}

            def preA(t):
                b, j = divmod(t, 8)
                xt, bx, dsx = xp_.get()
                sp.dma(xt[:], xrows0[b, j], dsx, writes=[bx])
                xsd[t] = prenormA(xt, bx, junk, ssq, xs_pool)

            def preB(t):
                b, j = divmod(t, 8)
                h_t = hslots[b % 2][0]
                bhj = hbufs[b % 2][j]
                xs, bxs = xsd.pop(t)
                prenormB(xs, bxs, [(lambda k, h_t=h_t, j=j: h_t[:, k, j * 128:(j + 1) * 128], bhj, A0c, S0c)], trps)

            def proj(t):
                b, j = divmod(t, 8)
                h_t = hslots[b % 2][0]
                bhj = hbufs[b % 2][j]
                if j == 0:
                    cur_u[b] = utm.get()
                u_t, bu = cur_u[b]
                for nt in range(2):
                    pa, bpa = pacc.get()
                    for k in range(8):
                        mm(pa[:], h_t[:, k, j * 128:(j + 1) * 128], w_in[:, k, nt * 512:(nt + 1) * 512], k == 0, k == 7, [bhj, b_win], [bpa])
                    cp(act, u_t[:, nt * 32:(nt + 1) * 32, j, :], pa[:].rearrange("p (g c) -> p g c", c=16), [bpa], [bu])
                    ne[0] += 1

            def blockend(b):
                h_t = hslots[b % 2][0]
                bhl = hbufs[b % 2]
                u_t, bu = cur_u[b]
                z_t, bz, dsz_ = zsT.get()
                for zc in range(8):
                    for hf in range(2):
                        pa, bpa = pacc.get()
                        for k in range(8):
                            mm(pa[:], w_in[:, k, 1024 + zc * 128:1024 + (zc + 1) * 128], h_t[:, k, hf * 512:(hf + 1) * 512], k == 0, k == 7,
                               bhl[hf * 4:(hf + 1) * 4] + [b_win], [bpa])
                        actf(z_t[:, zc, hf * 512:(hf + 1) * 512], pa[:], AF.Silu, [bpa], [bz])
                pool.dma(zs_d[:, :, b * 1024:(b + 1) * 1024], z_t[:], dsz_, reads=[bz], writes=[bzs[b]])
                ub, bub, dsub = ublk.get()
                for g8 in range(8):
                    tp, btp = trps.get()
                    for gl in range(8):
                        g = g8 * 8 + gl
                        trp(tp[:, gl, :], u_t[:, g].rearrange("p i c -> p (i c)"), ident_b[:], [bu, b_identb], [btp])
                    cp(dve if g8 % 2 == 0 else act, ub[:, g8 * 8:(g8 + 1) * 8, :], tp[:], [btp], [bub])
                pool.dma(U_d[b].rearrange("p (g c) -> p g c", c=128), ub[:], dsub, reads=[bub], writes=[bU[b]])

            preA(0); preA(1); preB(0)
            for t in range(32):
                if t + 2 < 32:
                    preA(t + 2)
                if t + 1 < 32:
                    preB(t + 1)
                proj(t)
                if t % 8 == 7:
                    blockend(t // 8)
            S.barrier(); S.flush()

        if stop == "2a":
            return nc, S
        with ExitStack() as st:
            WXA = T(st, "WXA", [128, 64, 128], BF16); b_wxa = Buf()
            WXB = T(st, "WXB", [128, 64, 128], BF16); b_wxb = Buf()
            WY1 = T(st, "WY1", [128, 64, 128], BF16); b_wy1 = Buf()
            WY2 = T(st, "WY2", [128, 64, 128], BF16); b_wy2 = Buf()
            b_tc = Buf(); b_ts = Buf()
            RHO = T(st, "RHO", [128, 64], F32); b_rho = Buf()
            E1c = T(st, "E1c", [128, 64]); E1s = T(st, "E1s", [128, 64]); b_e1 = Buf()
            PSW = T(st, "PSW", [128, 128], F32); b_psw = Buf()
            ts(pool, PSW[:, 0:64], ident_f[:, 64:128], -1.0, None, ALU.mult, None, [b_identf], [b_psw])
            cp(pool, PSW[:, 64:128], ident_f[:, 0:64], [b_identf], [b_psw])
            with ExitStack() as sp1:
                dq = S.dma_sem()
                lamre = T(sp1, "lamre", [128, 64]); lamim = T(sp1, "lamim", [128, 64]); dt_ = T(sp1, "dt", [128, 64])
                b_in = Buf()
                sp.dma(lamre[:], lamre_d, dq, writes=[b_in])
                sp.dma(lamim[:], lamim_d, dq, writes=[b_in])
                sp.dma(dt_[:], logdt_d, dq, writes=[b_in])
                Dcol = T(sp1, "Dcol", [128, 64]); b_dcol = b_in
                sp.dma(Dcol[:], dcol_d, dq, writes=[b_dcol])
                LR = T(sp1, "LR", [128, 9, 64]); LI = T(sp1, "LI", [128, 9, 64])
                MR = T(sp1, "MR", [128, 8, 64]); MI = T(sp1, "MI", [128, 8, 64])
                LIs = T(sp1, "LIs", [128, 9, 64]); LRn = T(sp1, "LRn", [128, 9, 64]); LIn = T(sp1, "LIn", [128, 9, 64])
                MRn = T(sp1, "MRn", [128, 8, 64]); MIn = T(sp1, "MIn", [128, 8, 64])
                b_pw = Buf()
                w = T(sp1, "wk_", [128, 12, 64]); b_w = Buf()
                wi = T(sp1, "wi_", [128, 64], I32)
                FRE = T(sp1, "FRE", [128, 64]); FIMs = T(sp1, "FIMs", [128, 64]); FIMn = T(sp1, "FIMn", [128, 64]); b_f = Buf()
                mask = T(sp1, "mask", [128, 128]); b_mask = Buf()
                mset(pool, mask[:], 1.0, [b_mask])
                pool.op(lambda e: e.affine_select(out=mask[:].rearrange("p (j c) -> p j c", c=16), in_=mask[:].rearrange("p (j c) -> p j c", c=16),
                                                  pattern=[[16, 8], [0, 16]], compare_op=ALU.is_ge, fill=0.0, base=15, channel_multiplier=-1),
                        [b_mask], [b_mask])
                R = [b_in, b_w]
                actf(dt_[:], dt_[:], AF.Exp, [b_in], [b_in])
                tt(dve, w[:, 0, :], lamre[:], dt_[:], ALU.mult, R, [b_w])
                tt(dve, w[:, 1, :], lamim[:], dt_[:], ALU.mult, R, [b_w])

                def sin_of(dst, shift):
                    ts(dve, w[:, 2, :], w[:, 1, :], shift, 1.0 / (2 * math.pi), ALU.add, ALU.mult, [b_w], [b_w])
                    cp(dve, wi[:], w[:, 2, :], [b_w], [b_w])
                    cp(dve, w[:, 2, :], wi[:], [b_w], [b_w])
                    stt(dve, w[:, 2, :], w[:, 2, :], -2 * math.pi, w[:, 1, :], ALU.mult, ALU.add, [b_w], [b_w])
                    ts(dve, w[:, 3, :], w[:, 2, :], shift, None, ALU.add, None, [b_w], [b_w])
                    ts(dve, w[:, 2, :], w[:, 3, :], math.pi, -2 * math.pi, ALU.is_gt, ALU.mult, [b_w], [b_w])
                    tt(dve, w[:, 3, :], w[:, 3, :], w[:, 2, :], ALU.add, [b_w], [b_w])
                    actf(dst, w[:, 3, :], AF.Sin, [b_w], [b_w])

                sin_of(w[:, 4, :], 0.0)
                sin_of(w[:, 5, :], 0.5 * math.pi)
                actf(w[:, 6, :], w[:, 0, :], AF.Exp, [b_w], [b_w])
                actf(w[:, 7, :], w[:, 0, :], AF.Exp, [b_w], [b_w], scale=-1.0)
                actf(RHO[:], w[:, 0, :], AF.Exp, [b_w], [b_rho], scale=8.0)
                actf(w[:, 8, :], w[:, 0, :], AF.Exp, [b_w], [b_w], scale=-8.0)
                P_ = [b_w, b_pw]
                mset(pool, LR[:, 0, :], 1.0, [b_pw]); mset(pool, LI[:, 0, :], 0.0, [b_pw])
                mset(pool, MR[:, 0, :], 1.0, [b_pw]); mset(pool, MI[:, 0, :], 0.0, [b_pw])
                tt(dve, LR[:, 1, :], w[:, 6, :], w[:, 5, :], ALU.mult, P_, [b_pw])
                tt(dve, LI[:, 1, :], w[:, 6, :], w[:, 4, :], ALU.mult, P_, [b_pw])
                tt(dve, MR[:, 1, :], w[:, 7, :], w[:, 5, :], ALU.mult, P_, [b_pw])
                stt(dve, MI[:, 1, :], w[:, 7, :], -1.0, w[:, 4, :], ALU.mult, ALU.mult, P_, [b_pw])

                def cpow(XR, XI, k):
                    tt(dve, w[:, 9, :], XR[:, k, :], XR[:, 1, :], ALU.mult, P_, [b_w])
                    tt(dve, w[:, 10, :], XI[:, k, :], XI[:, 1, :], ALU.mult, P_, [b_w])
                    tt(dve, XR[:, k + 1, :], w[:, 9, :], w[:, 10, :], ALU.subtract, P_, [b_pw])
                    tt(dve, w[:, 9, :], XR[:, k, :], XI[:, 1, :], ALU.mult, P_, [b_w])
                    tt(dve, w[:, 10, :], XI[:, k, :], XR[:, 1, :], ALU.mult, P_, [b_w])
                    tt(dve, XI[:, k + 1, :], w[:, 9, :], w[:, 10, :], ALU.add, P_, [b_pw])

                for k in range(1, 8):
                    cpow(LR, LI, k)
                for k in range(1, 7):
                    cpow(MR, MI, k)
                ts(dve, LIs[:], LI[:], sgn_col[:, 0:1], None, ALU.mult, None, [b_pw, b_sgn], [b_pw])
                ts(dve, LRn[:], LR[:], sgn_col[:, 0:1], -1.0, ALU.mult, ALU.mult, [b_pw, b_sgn], [b_pw])
                ts(dve, LIn[:], LI[:], -1.0, None, ALU.mult, None, [b_pw], [b_pw])
                ts(dve, MRn[:], MR[:], sgn_col[:, 0:1], -1.0, ALU.mult, ALU.mult, [b_pw, b_sgn], [b_pw])
                ts(dve, MIn[:], MI[:], -1.0, None, ALU.mult, None, [b_pw], [b_pw])
                F_ = [b_in, b_w, b_pw, b_f]
                ts(dve, w[:, 2, :], LR[:, 1, :], -1.0, None, ALU.add, None, F_, [b_w])
                tt(dve, w[:, 3, :], lamre[:], lamre[:], ALU.mult, F_, [b_w])
                tt(dve, w[:, 9, :], lamim[:], lamim[:], ALU.mult, F_, [b_w])
                tt(dve, w[:, 3, :], w[:, 3, :], w[:, 9, :], ALU.add, F_, [b_w])
                recip(w[:, 3, :], w[:, 3, :], [b_w], [b_w])
                tt(dve, w[:, 9, :], w[:, 2, :], lamre[:], ALU.mult, F_, [b_w])
                tt(dve, w[:, 10, :], LI[:, 1, :], lamim[:], ALU.mult, F_, [b_w])
                tt(dve, w[:, 9, :], w[:, 9, :], w[:, 10, :], ALU.add, F_, [b_w])
                tt(dve, FRE[:], w[:, 9, :], w[:, 3, :], ALU.mult, F_, [b_f])
                tt(dve, w[:, 9, :], LI[:, 1, :], lamre[:], ALU.mult, F_, [b_w])
                tt(dve, w[:, 10, :], w[:, 2, :], lamim[:], ALU.mult, F_, [b_w])
                tt(dve, w[:, 9, :], w[:, 9, :], w[:, 10, :], ALU.subtract, F_, [b_w])
                tt(dve, w[:, 9, :], w[:, 9, :], w[:, 3, :], ALU.mult, F_, [b_w])
                ts(dve, FIMs[:], w[:, 9, :], sgn_col[:, 0:1], None, ALU.mult, None, [b_w, b_sgn], [b_f])
                ts(dve, FIMn[:], FIMs[:], -1.0, None, ALU.mult, None, [b_f], [b_f])
                tt(dve, E1c[:], LR[:, 8, :], w[:, 8, :], ALU.mult, P_, [b_e1])
                tt(dve, E1s[:], LI[:, 8, :], w[:, 8, :], ALU.mult, P_, [b_e1])

                def bc(a):
                    return a.unsqueeze(2).to_broadcast([128, 64, 16])

                with ExitStack() as sp2:
                    Bst = T(sp2, "Bst", [128, 64, 16]); Bsw = T(sp2, "Bsw", [128, 64, 16])
                    Cst = T(sp2, "Cst", [128, 64, 16]); Csw = T(sp2, "Csw", [128, 64, 16]); b_bc = Buf()
                    sp.dma(Bst[:], bst_d.rearrange("p (g c) -> p g c", c=16), dq, writes=[b_bc])
                    sp.dma(Bsw[:], bsw_d.rearrange("p (g c) -> p g c", c=16), dq, writes=[b_bc])
                    sp.dma(Cst[:], cst_d.rearrange("p (g c) -> p g c", c=16), dq, writes=[b_bc])
                    sp.dma(Csw[:], csw_d.rearrange("p (g c) -> p g c", c=16), dq, writes=[b_bc])
                    Bbst = T(sp2, "Bbst", [128, 64, 16]); Bbsw = T(sp2, "Bbsw", [128, 64, 16]); b_bb = Buf()
                    t1p = rot_sb(sp2, "t1p", 2, [128, 64, 16]); t2p = rot_sb(sp2, "t2p", 2, [128, 64, 16])

                    def lin2(out, bo, a0, s0, a1, s1, rd):
                        t1, bt1 = t1p.get(); t2, bt2 = t2p.get()
                        tt(dve, t1[:], a0, bc(s0), ALU.mult, rd, [bt1])
                        tt(pool, t2[:], a1, bc(s1), ALU.mult, rd, [bt2])
                        tt(dve if lin2.n % 2 == 0 else pool, out, t1[:], t2[:], ALU.add, [bt1, bt2], [bo])
                        lin2.n += 1
                    lin2.n = 0
                    rdB = [b_bc, b_f, b_pw, b_bb]
                    lin2(Bbst[:], b_bb, Bst[:], FRE[:], Bsw[:], FIMs[:], [b_bc, b_f])
                    lin2(Bbsw[:], b_bb, Bsw[:], FRE[:], Bst[:], FIMn[:], [b_bc, b_f])
                    wy1v = WY1[:].rearrange("p g (j c) -> p g j c", c=16)
                    for j in range(8):
                        lin2(wy1v[:, :, j, :], b_wy1, Cst[:], LRn[:, j + 1, :], Csw[:], LIn[:, j + 1, :], rdB)
                    pxt = rot_ps(sp2, "pxt", 2, [128, 4, 128])
                    pw2 = rot_ps(sp2, "pw2", 2, [128, 4, 128])
                    tmpw = rot_sb(sp2, "tmpw", 2, [128, 4, 128])
                    P1 = T(sp2, "P1", [128, 32, 8, 16]); b_p1 = Buf()
                    P2 = T(sp2, "P2", [128, 32, 8, 16]); b_p2 = Buf()
                    for hf in range(2):
                        gs = slice(hf * 32, (hf + 1) * 32)

                        def bch(a):
                            return a[:, gs].unsqueeze(2).to_broadcast([128, 32, 16])

                        def lin2h(out, bo, a0, s0, a1, s1, rd):
                            t1, bt1 = t1p.get(); t2, bt2 = t2p.get()
                            tt(dve, t1[:, 0:32, :], a0, bch(s0), ALU.mult, rd, [bt1])
                            tt(pool, t2[:, 0:32, :], a1, bch(s1), ALU.mult, rd, [bt2])
                            tt(dve if lin2.n % 2 == 0 else pool, out, t1[:, 0:32, :], t2[:, 0:32, :], ALU.add, [bt1, bt2], [bo])
                            lin2.n += 1
                        for i in range(8):
                            k = 7 - i
                            lin2h(P1[:, :, i, :], b_p1, Bbst[:, gs, :], LR[:, k, :], Bbsw[:, gs, :], LIs[:, k, :], rdB)
                            lin2h(P2[:, :, i, :], b_p2, Cst[:, gs, :], MRn[:, k, :], Csw[:, gs, :], MIn[:, k, :], rdB)
                        for g4 in range(8):
                            px, bpx = pxt.get()
                            p2_, bp2 = pw2.get()
                            for gl in range(4):
                                gi = g4 * 4 + gl
                                trp(px[:, gl, :], P1[:, gi].rearrange("p i c -> p (i c)"), ident_f[:], [b_p1, b_identf], [bpx])
                                mm(p2_[:, gl, :], P1[:, gi].rearrange("p i c -> p (i c)"), P2[:, gi].rearrange("p i c -> p (i c)"), True, True,
                                   [b_p1, b_p2], [bp2])
                            g0 = hf * 32 + g4 * 4
                            cp(act, WXA[:, g0:g0 + 4, :], px[:], [bpx], [b_wxa])
                            cp(act, WXB[:, g0:g0 + 4, 0:64], px[:, :, 64:128], [bpx], [b_wxb])
                            actf(WXB[:, g0:g0 + 4, 64:128], px[:, :, 0:64], AF.Copy, [bpx], [b_wxb], scale=-1.0)
                            tw, btw = tmpw.get()
                            tt(dve, tw[:], p2_[:], mask[:].unsqueeze(1).to_broadcast([128, 4, 128]), ALU.mult, [bp2, b_mask], [btw])
                            for gl in range(4):
                                g = g0 + gl
                                stt(dve, WY2[:, g, :], ident_f[:], Dcol[:, g:g + 1], tw[:, gl, :], ALU.mult, ALU.add,
                                    [b_identf, b_dcol, btw], [b_wy2])
                    S.barrier(); S.flush()
                S.barrier(); S.flush()
            if stop == "P1":
                return nc, S
            TC = T(st, "TC", [128, 64, 128], BF16)
            TS_ = T(st, "TS", [128, 64, 128], BF16)
            with ExitStack() as sp3:
                TCf = T(sp3, "TCf", [128, 32, 128]); TSf = T(sp3, "TSf", [128, 32, 128]); b_tf = Buf()
                ta = T(sp3, "ta", [128, 32, 64]); tb_ = T(sp3, "tb", [128, 32, 64]); b_ta = Buf(); b_tb = Buf()
                for hf in range(2):
                    gs = slice(hf * 32, (hf + 1) * 32)
                    cp(dve, TCf[:, :, 0], E1c[:, gs], [b_e1], [b_tf])
                    cp(dve, TSf[:, :, 0], E1s[:, gs], [b_e1], [b_tf])
                    n = 1
                    while n < 128:
                        cn = TCf[:, :, n - 1:n].to_broadcast([128, 32, n])
                        sn = TSf[:, :, n - 1:n].to_broadcast([128, 32, n])
                        tt(dve, ta[:, :, 0:n], TCf[:, :, 0:n], cn, ALU.mult, [b_tf], [b_ta])
                        tt(pool, tb_[:, :, 0:n], TSf[:, :, 0:n], sn, ALU.mult, [b_tf], [b_tb])
                        tt(dve, ta[:, :, 0:n], ta[:, :, 0:n], tb_[:, :, 0:n], ALU.subtract, [b_ta, b_tb], [b_ta])
                        tt(pool, tb_[:, :, 0:n], TCf[:, :, 0:n], sn, ALU.mult, [b_tf, b_ta], [b_tb])
                        cp(dve, TCf[:, :, n:2 * n], ta[:, :, 0:n], [b_ta], [b_tf])
                        tt(dve, ta[:, :, 0:n], TSf[:, :, 0:n], cn, ALU.mult, [b_tf], [b_ta])
                        tt(pool, TSf[:, :, n:2 * n], ta[:, :, 0:n], tb_[:, :, 0:n], ALU.add, [b_ta, b_tb], [b_tf])
                        n *= 2
                    cp(dve, TC[:, gs, :], TCf[:], [b_tf], [b_tc])
                    cp(pool, TS_[:, gs, :], TSf[:], [b_tf], [b_ts])
                S.barrier(); S.flush()

            if stop == "P":
                return nc, S
            with ExitStack() as s2:
                ubk = rot_sb(s2, "ubk", 1, [128, 64, 128], BF16, dma=True)
                xp = T(s2, "xp", [128, 64, 129], BF16); b_xp = [Buf() for _ in range(64)]
                carry = T(s2, "carry", [128, 64], F32); b_carry = [Buf() for _ in range(64)]
                gel_tm = rot_sb(s2, "geltm", 1, [128, 8, 1024], BF16)
                gelT = rot_sb(s2, "gelT", 1, [128, 8, 1024], BF16, dma=True)
                t1p = rot_sb(s2, "st1", 3, [128, 128]); t2p = rot_sb(s2, "st2", 3, [128, 128]); vp = rot_sb(s2, "sv", 5, [128, 128])
                Wp = rot_sb(s2, "sW", 5, [128, 128]); t3p = rot_sb(s2, "st3", 5, [128, 128]); t4p = rot_sb(s2, "st4", 3, [128, 128])
                pab = rot_ps(s2, "pab", 3, [128, 2, 128])
                psw_ = rot_ps(s2, "psw", 2, [128, 128])
                py = rot_ps(s2, "py", 2, [128, 128])
                ptr = rot_ps(s2, "ptr", 1, [128, 8, 128], BF16)
                rhop = rot_sb(s2, "rhot", 5, [128, 128])
                ones128 = T(s2, "ones128", [128, 128]); b_ones128 = Buf()
                mset(pool, ones128[:], 1.0, [b_ones128])
                lvl = {"2b1": 1, "2b2": 2, "2b3": 3}.get(stop, 4)
                mset(pool, carry[:], 0.0, b_carry)
                for b in range(4):
                    ub, bub, dsu = ubk.get()
                    sp.dma(ub[:], U_d[b].rearrange("p (g c) -> p g c", c=128), dsu, reads=[bU[b]], writes=[bub])
                    cp(pool, xp[:, :, 0], carry[:], b_carry, b_xp)
                    g_t, bgt = gel_tm.get()
                    G = {}

                    def s0(g):
                        ab, bab = pab.get()
                        mm(ab[:, 0, :], WXA[:, g, :], ub[:, g, :], True, True, [b_wxa, bub], [bab])
                        mm(ab[:, 1, :], WXB[:, g, :], ub[:, g, :], True, True, [b_wxb, bub], [bab])
                        G[g] = {"ab": (ab, bab)}

                    def s1(g):
                        ab, bab = G[g]["ab"]
                        t1, bt1 = t1p.get(); t2, bt2 = t2p.get()
                        tt(dve, t1[:], ab[:, 0, :], TC[:, g, :], ALU.mult, [bab, b_tc], [bt1])
                        tt(dve, t2[:], ab[:, 1, :], TS_[:, g, :], ALU.mult, [bab, b_ts], [bt2])
                        rt, brt = rhop.get()
                        actf(rt[:], ones128[:], AF.Copy, [b_ones128, b_rho], [brt], scale=RHO[:, g:g + 1])
                        G[g].update(t1=(t1, bt1), t2=(t2, bt2), rt=(rt, brt))

                    def s2(g):
                        (t1, bt1), (t2, bt2) = G[g]["t1"], G[g]["t2"]
                        v, bv = vp.get()
                        tt(pool, v[:], t1[:], t2[:], ALU.add, [bt1, bt2], [bv])
                        G[g]["v"] = (v, bv)

                    def s3(g):
                        (v, bv), (rt, brt) = G[g]["v"], G[g]["rt"]
                        W_, bW = Wp.get()
                        dve.op(lambda e, W_=W_, v=v, g=g, rt=rt: e.tensor_tensor_scan(out=W_[:], data0=rt[:], data1=v[:],
                                                                                     initial=carry[:, g:g + 1], op0=ALU.mult, op1=ALU.add),
                               [bv, brt, b_carry[g]], [bW])
                        G[g]["W"] = (W_, bW)

                    def s4(g):
                        W_, bW = G[g]["W"]
                        ws, bws = psw_.get()
                        mm(ws[:], PSW[:], W_[:], True, True, [b_psw, bW], [bws])
                        t3, bt3 = t3p.get()
                        tt(pool, t3[:], W_[:], TC[:, g, :], ALU.mult, [bW, b_tc], [bt3])
                        G[g].update(ws=(ws, bws), t3=(t3, bt3))

                    def s5(g):
                        ws, bws = G[g]["ws"]
                        t4, bt4 = t4p.get()
                        tt(dve, t4[:], ws[:], TS_[:, g, :], ALU.mult, [bws, b_ts], [bt4])
                        G[g]["t4"] = (t4, bt4)

                    def s6(g):
                        (t3, bt3), (t4, bt4) = G[g]["t3"], G[g]["t4"]
                        tt(pool, xp[:, g, 1:129], t3[:], t4[:], ALU.add, [bt3, bt4], [b_xp[g]])
                        actf(carry[:, g:g + 1], t3[:, 127:128], AF.Identity, [bt3, bt4], [b_carry[g]], bias=t4[:, 127:128])

                    def s7(g):
                        yb, byb = py.get()
                        mm(yb[:], xp[:, g, 0:128], WY1[:, g, :], True, False, [b_xp[g], b_wy1], [byb])
                        mm(yb[:], ub[:, g, :], WY2[:, g, :], False, True, [bub, b_wy2], [byb])
                        G[g]["y"] = (yb, byb)

                    def s8(g):
                        yb, byb = G.pop(g)["y"]
                        actf(g_t[:, :, g * 16:(g + 1) * 16], yb[:].rearrange("p (j c) -> p j c", c=16), AF.Gelu_apprx_tanh, [byb], [bgt])

                    stages = [s0, s1, s2, s3, s4, s5, s6, s7, s8]
                    for i in range(64 + 8):
                        for sidx in range(8, -1, -1):
                            g = i - sidx
                            if 0 <= g < 64:
                                stages[sidx](g)
                    if lvl < 4:
                        continue
                    gT, bgT, dsgT = gelT.get()
                    for ncu in range(8):
                        tp, btp = ptr.get()
                        for j in range(8):
                            trp(tp[:, j, :], g_t[:, j, ncu * 128:(ncu + 1) * 128], ident_b[:], [bgt, b_identb], [btp])
                        cp(dve if ncu % 2 == 0 else act, gT[:, ncu, :], tp[:].rearrange("p j m -> p (j m)"), [btp], [bgT])
                    sp.dma(gel_d[:, :, b * 1024:(b + 1) * 1024], gT[:], dsgT, reads=[bgT], writes=[bgel[b]])
                S.barrier(); S.flush()

        if stop in ("2b", "2b1", "2b2", "2b3"):
            return nc, S
        with ExitStack() as st:
            wst = rot_sb(st, "wst", 2, [128, 1024], F32, dma=True)
            w_glu = T(st, "w_glu", [128, 8, 1024], BF16); b_wglu = Buf()
            w_out = T(st, "w_out", [128, 8, 1024], BF16); b_wout = Buf()
            load_w_bf16(wst, w_glu, b_wglu, aglu_d, 1024, [pool, dve])
            load_w_bf16(wst, w_out, b_wout, aout_d, 1024, [pool, dve])
            gin = rot_sb(st, "gin", 2, [128, 8, 1024], BF16, dma=True)
            zin = rot_sb(st, "zin", 2, [128, 8, 1024], BF16, dma=True)
            y3 = rot_sb(st, "y3", 2, [128, 8, 1024], BF16)
            sigp = rot_sb(st, "sig", 3, [128, 512], BF16)
            xp_ = rot_sb(st, "xt", 4, [128, D], F32, dma=True)
            junk = rot_sb(st, "junk", 2, [128, D], BF16)
            ssq = rot_sb(st, "ssq", 4, [128, 4], F32)
            tbuf = rot_sb(st, "tbuf", 2, [128, D], F32)
            obuf = rot_sb(st, "obuf", 2, [128, D], F32, dma=True)
            pg = rot_ps(st, "pg", 4, [128, 512])
            po_ = rot_ps(st, "po", 2, [128, 1024])
            dq_o = S.dma_sem()
            ycur = {}

            def glu(b):
                gi, bgi, dsg = gin.get()
                zi, bzi, dsz = zin.get()
                sp.dma(gi[:], gel_d[:, :, b * 1024:(b + 1) * 1024], dsg, reads=[bgel[b]], writes=[bgi])
                sp.dma(zi[:], zs_d[:, :, b * 1024:(b + 1) * 1024], dsz, reads=[bzs[b]], writes=[bzi])
                y3t, by3 = y3.get()
                ycur[b] = (y3t, by3)
                for ec in range(8):
                    for hf in range(2):
                        p_, bp_ = pg.get()
                        for k in range(8):
                            mm(p_[:], w_glu[:, k, ec * 128:(ec + 1) * 128], gi[:, k, hf * 512:(hf + 1) * 512], k == 0, k == 7, [b_wglu, bgi], [bp_])
                        sg, bsg = sigp.get()
                        actf(sg[:], p_[:], AF.Sigmoid, [bp_, b_cols], [bsg], bias=BGLUc[:, ec:ec + 1])
                        tt(dve, sg[:], sg[:], gi[:, ec, hf * 512:(hf + 1) * 512], ALU.mult, [bsg, bgi], [bsg])
                        tt(pool, y3t[:, ec, hf * 512:(hf + 1) * 512], sg[:], zi[:, ec, hf * 512:(hf + 1) * 512], ALU.mult, [bsg, bzi], [by3])

            def outp(b):
                y3t, by3 = ycur.pop(b)
                for j in range(8):
                    xt, bx, dsx = xp_.get()
                    sp.dma(xt[:], xrows0[b, j], dsx, writes=[bx])
                    po, bpo = po_.get()
                    for nt in range(2):
                        for k in range(8):
                            mm(po[:, nt * 512:(nt + 1) * 512], y3t[:, k, j * 128:(j + 1) * 128], w_out[:, k, nt * 512:(nt + 1) * 512], k == 0, k == 7,
                               [by3, b_wout], [bpo])
                    postnorm(po[:], bpo, 0, xt, bx, junk, ssq, tbuf, obuf, h1rows0[b, j], bh1[b * 8 + j], pool)

            glu(0)
            for b in range(4):
                if b + 1 < 4:
                    glu(b + 1)
                outp(b)
            S.barrier(); S.flush()

        if stop == "h1":
            return nc, S

        def h1_deps(tt_):
            b = (tt_ * 128) // 1024
            return [bh1[b * 8 + j] for j in range(8)]

        with ExitStack() as st:
            wst = rot_sb(st, "wst", 2, [128, 2048], F32, dma=True)
            w_in = T(st, "w_in1", [128, 8, 2048], BF16); b_win = Buf()
            load_w_bf16(wst, w_in, b_win, bwin_d, 2048, [pool])
            xp_ = rot_sb(st, "xt", 4, [128, D], F32, dma=True)
            junk = rot_sb(st, "junk", 2, [128, D], BF16)
            ssq = rot_sb(st, "ssq", 4, [128, 4], F32)
            xs_pool = rot_sb(st, "xs", 3, [128, D], BF16)
            hT = rot_sb(st, "hT", 2, [128, 8, 512], BF16)
            qblk = rot_sb(st, "qblk", 2, [128, 8, 512], BF16, dma=True)
            zblk = rot_sb(st, "zblk", 2, [128, 4, 1024], BF16, dma=True)
            trps = rot_ps(st, "trps", 2, [128, 8, 128], BF16)
            pacc = rot_ps(st, "pacc", 6, [128, 512])
            dq_o = S.dma_sem(); dq_o2 = S.dma_sem()
            ne = [0]
            hcur = {}

            def pre(grp):
                h_t, bh = hT.get()
                hcur[grp] = (h_t, bh)
                xa = {}

                def A(tl):
                    tt_ = grp * 4 + tl
                    xt, bx, dsx = xp_.get()
                    sp.dma(xt[:], h1_d[tt_ * 128:(tt_ + 1) * 128, :], dsx, reads=h1_deps(tt_), writes=[bx])
                    xa[tl] = prenormA(xt, bx, junk, ssq, xs_pool)

                def B(tl):
                    xs, bxs = xa.pop(tl)
                    prenormB(xs, bxs, [(lambda k, h_t=h_t, tl=tl: h_t[:, k, tl * 128:(tl + 1) * 128], bh, A1c, S1c)], trps)

                A(0); A(1); B(0); A(2); B(1); A(3); B(2); B(3)

            def proj(grp):
                h_t, bh = hcur.pop(grp)
                qb_, bqb, dsqb = qblk.get()
                for h in range(8):
                    pa, bpa = pacc.get()
                    for k in range(8):
                        mm(pa[:], w_in[:, k, h * 128:(h + 1) * 128], h_t[:, k, :], k == 0, k == 7, [b_win, bh], [bpa])
                    cp(act, qb_[:, h, :], pa[:], [bpa], [bqb])
                    ne[0] += 1
                pool.dma(qT_d[:, :, grp * 512:(grp + 1) * 512], qb_[:], dsqb, reads=[bqb], writes=[bqT[grp]])
                zb_, bzb, dszb = zblk.get()
                for tl in range(4):
                    for nt in range(2):
                        pa, bpa = pacc.get()
                        for k in range(8):
                            mm(pa[:], h_t[:, k, tl * 128:(tl + 1) * 128], w_in[:, k, 1024 + nt * 512:1024 + (nt + 1) * 512], k == 0, k == 7,
                               [b_win, bh], [bpa])
                        actf(zb_[:, tl, nt * 512:(nt + 1) * 512], pa[:], AF.Silu, [bpa], [bzb])
                pool.dma(zs1_d[grp * 512:(grp + 1) * 512, :].rearrange("(t p) d -> p t d", p=128), zb_[:], dszb, reads=[bzb], writes=[bzs1[grp]])

            pre(0)
            for grp in range(8):
                if grp + 1 < 8:
                    pre(grp + 1)
                proj(grp)
            S.barrier(); S.flush()

        with ExitStack() as st:
            wst = rot_sb(st, "wst", 2, [128, 1024], F32, dma=True)
            w_k = T(st, "w_k", [128, 8, 1024], BF16); b_wk = Buf()
            w_v = T(st, "w_v", [128, 8, 1024], BF16); b_wv = Buf()
            load_w_bf16(wst, w_k, b_wk, wk_d, 1024, [pool, dve])
            load_w_bf16(wst, w_v, b_wv, wv_d, 1024, [pool, dve])
            xp_ = rot_sb(st, "xt", 4, [128, D], F32, dma=True)
            junk = rot_sb(st, "junk", 2, [128, D], BF16)
            ssq = rot_sb(st, "ssq", 4, [128, 4], F32)
            xs_pool = rot_sb(st, "xs", 3, [128, D], BF16)
            hT = rot_sb(st, "hT", 2, [128, 8, 512], BF16)
            kblk = rot_sb(st, "kblk", 2, [128, 8, 512], BF16, dma=True)
            vblk = rot_sb(st, "vblk", 2, [128, 8, 4, 129], BF16, dma=True)
            trps = rot_ps(st, "trps", 2, [128, 8, 128], BF16)
            pacc = rot_ps(st, "pacc", 6, [128, 512])
            for (vb_, bvb_, _d) in vblk.items:
                pool.op(lambda e, vb_=vb_: e.memset(vb_[:, :, :, 128:129], 1.0), (), [bvb_])
            ne = [0]
            hcur = {}

            def pre(grp):
                h_t, bh = hT.get()
                hcur[grp] = (h_t, bh)
                xa = {}

                def A(tl):
                    tt_ = grp * 4 + tl
                    xt, bx, dsx = xp_.get()
                    sp.dma(xt[:], h1_d[tt_ * 128:(tt_ + 1) * 128, :], dsx, reads=h1_deps(tt_), writes=[bx])
                    xa[tl] = prenormA(xt, bx, junk, ssq, xs_pool)

                def B(tl):
                    xs, bxs = xa.pop(tl)
                    prenormB(xs, bxs, [(lambda k, h_t=h_t, tl=tl: h_t[:, k, tl * 128:(tl + 1) * 128], bh, GKVc, None)], trps)

                A(0); A(1); B(0); A(2); B(1); A(3); B(2); B(3)

            def proj(grp):
                h_t, bh = hcur.pop(grp)
                kb_, bkb, dskb = kblk.get()
                for h in range(8):
                    pa, bpa = pacc.get()
                    for k in range(8):
                        mm(pa[:], w_k[:, k, h * 128:(h + 1) * 128], h_t[:, k, :], k == 0, k == 7, [b_wk, bh], [bpa])
                    cp(act, kb_[:, h, :], pa[:], [bpa], [bkb])
                    ne[0] += 1
                pool.dma(kT_d[:, :, grp * 512:(grp + 1) * 512], kb_[:], dskb, reads=[bkb], writes=[bkT[grp]])
                vb_, bvb, dsvb = vblk.get()
                for tl in range(4):
                    for nt in range(2):
                        pa, bpa = pacc.get()
                        for k in range(8):
                            mm(pa[:], h_t[:, k, tl * 128:(tl + 1) * 128], w_v[:, k, nt * 512:(nt + 1) * 512], k == 0, k == 7, [b_wv, bh], [bpa])
                        cp(act, vb_[:, nt * 4:(nt + 1) * 4, tl, 0:128], pa[:].rearrange("p (h e) -> p h e", e=128),
                           [bpa], [bvb])
                        ne[0] += 1
                pool.dma(v_d[:, :, grp * 4:(grp + 1) * 4, :].rearrange("h p t e -> p h (t e)"), vb_[:].rearrange("p h t e -> p h (t e)"),
                         dsvb, reads=[bvb], writes=[bvd[grp]])

            pre(0)
            for grp in range(8):
                if grp + 1 < 8:
                    pre(grp + 1)
                proj(grp)
            S.barrier(); S.flush()

        with ExitStack() as st:
            wst = rot_sb(st, "wst", 2, [128, 1024], F32, dma=True)
            w_o = T(st, "w_o", [128, 8, 1024], BF16); b_wo = Buf()
            load_w_bf16(wst, w_o, b_wo, bout_d, 1024, [pool, dve])
            qin = rot_sb(st, "qin", 2, [128, 2, 8, 512], BF16, dma=True)
            for (qt_, bqt_, _d) in qin.items:
                pool.op(lambda e, qt_=qt_: e.memset(qt_[64:128, 0], 0.0), (), [bqt_])
                pool.op(lambda e, qt_=qt_: e.memset(qt_[0:64, 1], 0.0), (), [bqt_])
            zin = rot_sb(st, "zin1", 2, [128, 4, 1024], BF16, dma=True)
            kin = rot_sb(st, "kin", 2, [128, L], BF16, dma=True)
            vin = rot_sb(st, "vin", 2, [128, 32, 129], BF16, dma=True)
            hin = rot_sb(st, "hin", 3, [128, D], F32, dma=True)
            ptp = rot_sb(st, "pt", 4, [128, 512], BF16)
            o0p = rot_sb(st, "o0", 2, [128, 4, 128], F32)
            odp = rot_sb(st, "od", 1, [128, 4, 8, 128], F32)
            rlp = rot_sb(st, "rl", 4, [128, 8], F32)
            sqp = rot_sb(st, "sq", 1, [128, 4, 8, 128], F32)
            ssn = rot_sb(st, "ssn", 2, [128, 2, 32], F32)
            yat = rot_sb(st, "yat", 1, [128, 4, 1024], BF16)
            yatT = rot_sb(st, "yatT", 2, [128, 8, 128], BF16)
            junk = rot_sb(st, "junk", 1, [128, D], BF16)
            ssq = rot_sb(st, "ssq", 4, [128, 4], F32)
            tbuf = rot_sb(st, "tbuf", 1, [128, D], F32)
            obuf = rot_sb(st, "obuf", 2, [128, D], F32, dma=True)
            big = PS(st, "big", [128, 3, 512])
            bbig = [Buf() for _ in range(3)]
            pss = Rot([(big[:, i, :], bbig[i]) for i in range(3)])
            pacc = [[(PS(st, f"acc{p_}{i}", [128, 2, 256]), Buf()) for i in range(2)] for p_ in range(2)]
            ptr = rot_ps(st, "ptr", 1, [128, 8, 128], BF16)
            LA = 3
            jobs = [(qb, h) for qb in range(8) for h in range(8)]
            kv = {}

            def load_kv(job):
                qb, h = job
                nk = 4 * qb + 4
                ki, bki, dsk = kin.get()
                sp.dma(ki[:, 0:nk * 128], kT_d[:, h, 0:nk * 128], dsk, reads=bkT[0:qb + 1], writes=[bki])
                vi, bvi, dsv = vin.get()
                sp.dma(vi[:, 0:nk, :], v_d[h, :, 0:nk, :], dsv, reads=bvd[0:qb + 1], writes=[bvi])
                kv[job] = (ki, bki, vi, bvi)

            TAIL_D = 64
            pending_tail = []

            def tail_pe(qb, ya, bya):
                for jq in range(4):
                    tt_ = qb * 4 + jq
                    xt, bx, dsx = hin.get()
                    sp.dma(xt[:], h1_d[tt_ * 128:(tt_ + 1) * 128, :], dsx, reads=h1_deps(tt_), writes=[bx])
                    tp, btp = ptr.get()
                    for k in range(8):
                        trp(tp[:, k, :], ya[:, jq, k * 128:(k + 1) * 128], ident_b[:], [bya, b_identb], [btp])
                    yT, byT = yatT.get()
                    cp(dve, yT[:], tp[:], [btp], [byT])
                    po = big[:, 0:2, :].rearrange("p a n -> p (a n)")
                    for nt in range(2):
                        for k in range(8):
                            mm(po[:, nt * 512:(nt + 1) * 512], yT[:, k, :], w_o[:, k, nt * 512:(nt + 1) * 512], k == 0, k == 7, [byT, b_wo], [bbig[0], bbig[1]])
                    bout = Buf()
                    postnorm(po, [bbig[0], bbig[1]], 1, xt, bx, junk, ssq, tbuf, obuf, out_d[tt_ * 128:(tt_ + 1) * 128, :], bout, pool)

            load_kv(jobs[0])
            par = [0]
            for qb in range(8):
                qi, bqi, dsq = qin.get()
                sp.dma(qi[0:64, 0], qT_d[0:64, :, qb * 512:(qb + 1) * 512], dsq, reads=[bqT[qb]], writes=[bqi])
                sp.dma(qi[64:128, 1], qT_d[64:128, :, qb * 512:(qb + 1) * 512], dsq, reads=[bqT[qb]], writes=[bqi])
                zi, bzi, dsz = zin.get()
                sp.dma(zi[:], zs1_d[qb * 512:(qb + 1) * 512, :].rearrange("(t p) d -> p t d", p=128), dsz, reads=[bzs1[qb]], writes=[bzi])
                od, bod = odp.get()
                nk = 4 * qb + 4
                items = [(h, cc, kt) for h in range(8) for kt in range(nk) for cc in range(2)]
                pend = []
                o0s = {}
                accs = {}

                def stageA(it):
                    h, cc, kt = it
                    if cc == 0 and kt == LA:
                        ji = jobs.index((qb, h))
                        if ji + 1 < len(jobs):
                            load_kv(jobs[ji + 1])
                    if cc == 0 and kt == 0:
                        o0s[h] = o0p.get()
                    if kt == 0:
                        accs[(h, cc)] = pacc[cc]
                    ki, bki, vi, bvi = kv[(qb, h)]
                    ps_ = slice(cc * 64, (cc + 1) * 64)
                    r = kt - 4 * qb
                    q0 = max(r, 0) * 128
                    s_, bs_ = pss.get()
                    mm(s_[:, q0:512], ki[:, kt * 128:(kt + 1) * 128], qi[:, cc, h, q0:512], True, r < 0, [bki, bqi], [bs_])
                    if r >= 0:
                        mm(s_[:, q0:q0 + 128], ident_b[:], cmask_b[:], False, True, [b_identb, b_cmask], [bs_])
                    pt, bpt = ptp.get()
                    actf(pt[:, q0:512], s_[:, q0:512], AF.Exp, [bs_], [bpt], scale=0.125)
                    return (pt, bpt)

                def stageC(it, pt, bpt):
                    h, cc, kt = it
                    ki, bki, vi, bvi = kv[(qb, h)]
                    r = kt - 4 * qb
                    pa_ = accs[(h, cc)]
                    for jq in range(max(r, 0), 4):
                        acc, bacc = pa_[jq // 2]
                        mm(acc[:, jq % 2, 0:129], pt[:, jq * 128:(jq + 1) * 128], vi[:, kt, :],
                           kt == 0 and jq % 2 == 0, kt == 4 * qb + jq, [bpt, bvi], [bacc], sgc=True)
                    if kt != nk - 1:
                        return
                    o0, bo0 = o0s[h]
                    rl, brl = rlp.get()
                    for jq in range(4):
                        acc, bacc = pa_[jq // 2]
                        recip(rl[:, jq:jq + 1], acc[:, jq % 2, 128:129], [bacc], [brl])
                    if cc == 0:
                        for jq in range(4):
                            acc, bacc = pa_[jq // 2]
                            ts(dve, o0[:, jq, :], acc[:, jq % 2, 0:128], rl[:, jq:jq + 1], None, ALU.mult, None, [bacc, brl], [bo0])
                    else:
                        ts(dve, rl[:, 4:8], rl[:, 0:4], neglam[:, 0:1], None, ALU.mult, None, [brl, b_neglam], [brl])
                        for jq in range(4):
                            acc, bacc = pa_[jq // 2]
                            stt(dve, od[:, jq, h, :], acc[:, jq % 2, 0:128], rl[:, 4 + jq:5 + jq], o0[:, jq, :], ALU.mult, ALU.add,
                                [bacc, brl, bo0], [bod])

                for i in range(len(items) + LA):
                    if i < len(items):
                        pend.append(stageA(items[i]))
                    if i >= LA:
                        stageC(items[i - LA], *pend.pop(0))
                    if i == TAIL_D and pending_tail:
                        pending_tail.pop(0)()
                sq, bsq = sqp.get()
                sn, bsn = ssn.get()
                tt(pool, sq[:], od[:], od[:], ALU.mult, [bod], [bsq])
                dve.op(lambda e, sn=sn, sq=sq: e.reduce_sum(out=sn[:, 0, :], in_=sq[:].rearrange("p a h e -> p (a h) e"), axis=AX.X), [bsq], [bsn])
                actf(sn[:, 1, :], sn[:, 0, :], AF.Ln, [bsn, b_eps], [bsn], scale=1.0 / 128, bias=eps_col[:, 0:1])
                actf(sn[:, 1, :], sn[:, 1, :], AF.Exp, [bsn], [bsn], scale=-0.5)
                tt(dve, sq[:], od[:], sn[:, 1, :].rearrange("p (a h) -> p a h", h=8).unsqueeze(3).to_broadcast([128, 4, 8, 128]), ALU.mult,
                   [bod, bsn], [bsq])
                tt(pool, sq[:], sq[:], GS[:].unsqueeze(1).unsqueeze(1).to_broadcast([128, 4, 8, 128]), ALU.mult, [bsq, b_GS], [bsq])
                ya, bya = yat.get()
                tt(dve, ya[:], sq[:].rearrange("p a h e -> p a (h e)"), zi[:], ALU.mult, [bsq, bzi], [bya])
                pending_tail.append(lambda qb=qb, ya=ya, bya=bya: tail_pe(qb, ya, bya))
            while pending_tail:
                pending_tail.pop(0)()
            S.barrier(); S.flush()
    return nc, S


_CACHE = {}


def _get_program(stop=None):
    key = stop
    if key not in _CACHE:
        nc, S = build(None, stop)
        nc, S = build(S.record, stop)
        _CACHE[key] = nc
    return _CACHE[key]


def _prep_inputs(inp):
    f = lambda a: np.ascontiguousarray(np.asarray(a, dtype=np.float32))
    x = f(inp["x"]); c = f(inp["c"])
    dup = lambda a: np.ascontiguousarray(np.concatenate([a.T, a.T], 0))
    lam_re = f(inp["a_lam_re"])[0]; lam_im = f(inp["a_lam_im"])[0]; log_dt = f(inp["a_log_dt"])[0]
    b_re = f(inp["a_b_re"])[0]; b_im = f(inp["a_b_im"])[0]; c_re = f(inp["a_c_re"])[0]; c_im = f(inp["a_c_im"])[0]
    bre_t = b_re.transpose(1, 0, 2); bim_t = b_im.transpose(1, 0, 2)
    cre_t = c_re.transpose(2, 0, 1); cim_t = c_im.transpose(2, 0, 1)
    shared = {
        "ada_w": f(inp["ada_w"]), "ada_b": f(inp["ada_b"]), "g_pre": f(inp["g_pre"]), "g_post": f(inp["g_post"]),
        "gkv_col": np.ascontiguousarray(f(inp["g_kv"]).reshape(8, 128).T),
        "a_w_in": f(inp["a_w_in"])[0], "a_w_glu": f(inp["a_w_glu"])[0], "a_w_out": f(inp["a_w_out"])[0],
        "bglu_col": np.ascontiguousarray(f(inp["a_b_glu"])[0].reshape(8, 128).T),
        "w_k": f(inp["w_k"]), "w_v": f(inp["w_v"]), "b_w_in": f(inp["b_w_in"])[0], "b_w_out": f(inp["b_w_out"])[0],
        "lamre2": dup(lam_re), "lamim2": dup(lam_im),
        "logdt2": np.ascontiguousarray(np.broadcast_to(log_dt[None, :], (128, 64))),
        "bst": np.ascontiguousarray(np.concatenate([bre_t, bim_t], 0).reshape(128, 1024)),
        "bsw": np.ascontiguousarray(np.concatenate([bim_t, bre_t], 0).reshape(128, 1024)),
        "cst": np.ascontiguousarray(np.concatenate([cre_t, cim_t], 0).reshape(128, 1024)),
        "csw": np.ascontiguousarray(np.concatenate([cim_t, cre_t], 0).reshape(128, 1024)),
        "dcol": np.ascontiguousarray(np.tile(f(inp["a_d"])[0].reshape(64, 16).T, (8, 1))),
        "lqk": np.ascontiguousarray(np.concatenate([f(inp["b_lq1"])[0], f(inp["b_lk1"])[0], f(inp["b_lq2"])[0], f(inp["b_lk2"])[0]])[None, :]),
        "gsub": np.ascontiguousarray(f(inp["b_g_sub"])[0][None, :]),
    }
    maps = []
    for b in range(x.shape[0]):
        m = dict(shared)
        m["x"] = np.ascontiguousarray(x[b])
        m["cT"] = np.ascontiguousarray(c[b].reshape(8, 128).T)
        maps.append(m)
    return maps


def kernel(**inputs):
    stop = os.environ.get("MK_STOP") or None
    nc = _get_program(stop)
    maps = _prep_inputs(inputs)
    ncores = int(os.environ.get("MK_CORES", "8"))
    maps = maps[:ncores]
    res = run_bass_kernel_spmd(nc, maps, core_ids=list(range(len(maps))))
    outs = [np.asarray(r["out"], dtype=np.float32) for r in res.results]
    return np.stack(outs, 0)
```

```python
import math
import os
from contextlib import ExitStack

import numpy as np
import concourse.bass as bass
import concourse.mybir as mybir
from concourse.bass_utils import run_bass_kernel_spmd

F32 = mybir.dt.float32
BF16 = mybir.dt.bfloat16
I32 = mybir.dt.int32
AF = mybir.ActivationFunctionType
ALU = mybir.AluOpType
AX = mybir.AxisListType

L = 4096
D = 1024
EPS = 1e-6
LAMBDA_INIT = 0.8 - 0.6 * math.exp(-0.3 * 1)
NEG = -30000.0


class Buf:
    __slots__ = ("name", "w", "r")

    def __init__(self, name=""):
        self.name = name
        self.w = None
        self.r = []


class Tok:
    __slots__ = ("eng", "seq", "sem", "val")

    def __init__(self, eng, seq, sem, val):
        self.eng = eng
        self.seq = seq
        self.sem = sem
        self.val = val


class DmaSem:
    def __init__(self, sem, key):
        self.sem = sem
        self.key = key
        self.n = 0


class EngW:
    def __init__(self, sched, key, sem):
        self.sched = sched
        self.key = key
        self.sem = sem
        self.seq = 0
        self.cnt = 0
        self.waited_seq = {}
        self.waited_dma = {}
        self.prog = []
        self.last = None

    def _gather(self, reads, writes):
        deps = []
        for b in reads:
            if b.w is not None:
                deps.append(b.w)
        for b in writes:
            if b.w is not None:
                deps.append(b.w)
            deps.extend(b.r)
        return deps

    def _wait(self, tok):
        if tok.eng is None:
            k = tok.sem.key
            if self.waited_dma.get(k, 0) >= tok.val:
                return
            self.waited_dma[k] = tok.val
            sem, val = tok.sem.sem, tok.val
            self.prog.append(lambda e, sem=sem, val=val: e.wait_ge(sem, val))
            return
        if tok.eng is self and self.key == "pe":
            return
        k = tok.eng.key
        if self.waited_seq.get(k, -1) >= tok.seq:
            return
        self.waited_seq[k] = tok.seq
        self.sched.record.add((k, tok.seq))
        if tok.val is None:
            raise RuntimeError(f"token {k}:{tok.seq} not marked")
        sem, val = tok.sem, tok.val
        self.prog.append(lambda e, sem=sem, val=val: e.wait_ge(sem, val))

    def op(self, fn, reads=(), writes=()):
        for t in self._gather(reads, writes):
            self._wait(t)
        seq = self.seq
        self.seq += 1
        needed = self.sched.needed
        mark = needed is None or (self.key, seq) in needed
        if mark:
            self.cnt += 1
            sem = self.sem
            self.prog.append(lambda e, fn=fn, sem=sem: fn(e).then_inc(sem, 1))
            tok = Tok(self, seq, self.sem, self.cnt)
        else:
            self.prog.append(lambda e, fn=fn: fn(e))
            tok = Tok(self, seq, self.sem, None)
        self.last = tok
        for b in reads:
            b.r.append(tok)
        for b in writes:
            b.w = tok
            b.r = []
        return tok

    def dma(self, out, in_, dsem, reads=(), writes=()):
        for t in self._gather(reads, writes):
            self._wait(t)
        dsem.n += 1
        val = 16 * dsem.n
        sem = dsem.sem
        self.prog.append(lambda e, out=out, in_=in_, sem=sem: e.dma_start(out=out, in_=in_).then_inc(sem, 16))
        tok = Tok(None, -1, dsem, val)
        for b in reads:
            b.r.append(tok)
        for b in writes:
            b.w = tok
            b.r = []
        return tok


class Sched:
    def __init__(self, nc, stack, needed=None):
        self.nc = nc
        self.needed = needed
        self.record = set()
        self.stack = stack
        mk = lambda n: stack.enter_context(nc.semaphore(n))
        self.pe = EngW(self, "pe", mk("s_pe"))
        self.act = EngW(self, "act", mk("s_act"))
        self.dve = EngW(self, "dve", mk("s_dve"))
        self.pool = EngW(self, "pool", mk("s_pool"))
        self.sp = EngW(self, "sp", mk("s_sp"))
        self.engs = [self.pe, self.act, self.dve, self.pool, self.sp]
        self.ndsem = 0
        self.dsems = []

    def dma_sem(self):
        self.ndsem += 1
        key = f"dq{self.ndsem}"
        ds = DmaSem(self.stack.enter_context(self.nc.semaphore(key)), key)
        self.dsems.append(ds)
        return ds

    def barrier(self):
        toks = [w.last for w in self.engs[:4] if w.last is not None]
        for w in self.engs:
            for t in toks:
                if t.eng is not w:
                    w._wait(t)
            for ds in self.dsems:
                if ds.n > 0:
                    w._wait(Tok(None, -1, ds, 16 * ds.n))

    def flush(self):
        nc = self.nc
        with nc.Block() as block:
            @block.tensor
            def _(e):
                for f in self.pe.prog:
                    f(e)

            @block.scalar
            def _(e):
                for f in self.act.prog:
                    f(e)

            @block.vector
            def _(e):
                for f in self.dve.prog:
                    f(e)

            @block.gpsimd
            def _(e):
                for f in self.pool.prog:
                    f(e)

            @block.sync
            def _(e):
                for f in self.sp.prog:
                    f(e)
        for w in self.engs:
            w.prog = []


class Rot:
    def __init__(self, items):
        self.items = items
        self.i = 0

    def get(self):
        it = self.items[self.i % len(self.items)]
        self.i += 1
        return it


def build(needed=None, stop=None):
    nc = bass.Bass("TRN2", target_bir_lowering=False)
    di = lambda n, s: nc.dram_tensor(n, s, F32, kind="ExternalInput").ap()
    x_d = di("x", [L, D])
    cT_d = di("cT", [128, 8])
    adaw_d = di("ada_w", [2, D, 3 * D])
    adab_d = di("ada_b", [2, 3 * D])
    gpre_d = di("g_pre", [2, D])
    gpost_d = di("g_post", [2, D])
    gkvc_d = di("gkv_col", [128, 8])
    awin_d = di("a_w_in", [D, 2 * D])
    aglu_d = di("a_w_glu", [D, D])
    aout_d = di("a_w_out", [D, D])
    bgluc_d = di("bglu_col", [128, 8])
    wk_d = di("w_k", [D, D])
    wv_d = di("w_v", [D, D])
    bwin_d = di("b_w_in", [D, 2 * D])
    bout_d = di("b_w_out", [D, D])
    lamre_d = di("lamre2", [128, 64])
    lamim_d = di("lamim2", [128, 64])
    logdt_d = di("logdt2", [128, 64])
    bst_d = di("bst", [128, 1024])
    bsw_d = di("bsw", [128, 1024])
    cst_d = di("cst", [128, 1024])
    csw_d = di("csw", [128, 1024])
    dcol_d = di("dcol", [128, 64])
    lqk_d = di("lqk", [1, 256])
    gsub_d = di("gsub", [1, 128])
    out_d = nc.dram_tensor("out", [L, D], F32, kind="ExternalOutput").ap()
    scr = lambda n, s, d: nc.dram_tensor(n, s, d, kind="Internal").ap()
    U_d = scr("U_s", [4, 128, 8192], BF16)
    zs_d = scr("zs_s", [128, 8, L], BF16)
    gel_d = scr("gel_s", [128, 8, L], BF16)
    if stop == "h1":
        h1_d = out_d
    else:
        h1_d = scr("h1_s", [L, D], F32)
    qT_d = scr("qT_s", [128, 8, L], BF16)
    zs1_d = scr("zs1_s", [L, D], BF16)
    kT_d = scr("kT_s", [128, 8, L], BF16)
    v_d = scr("v_s", [8, 128, 32, 129], BF16)
    bkT = [Buf() for _ in range(8)]
    bvd = [Buf() for _ in range(8)]
    bU = [Buf() for _ in range(4)]
    bzs = [Buf() for _ in range(4)]
    bgel = [Buf() for _ in range(4)]
    bh1 = [Buf() for _ in range(32)]
    bqT = [Buf() for _ in range(8)]
    bzs1 = [Buf() for _ in range(8)]

    top = ExitStack()
    with top:
        S = Sched(nc, top, needed)
        pe, act, dve, pool, sp = S.pe, S.act, S.dve, S.pool, S.sp

        uid = [0]

        def T(st, n, s, d=F32):
            uid[0] += 1
            return st.enter_context(nc.sbuf_tensor(f"sb{uid[0]}_{n}", s, d))

        def PS(st, n, s, d=F32):
            uid[0] += 1
            return st.enter_context(nc.psum_tensor(f"ps{uid[0]}_{n}", s, d))

        def mm(out, lhsT, rhs, start, stop_, reads, writes, sgc=False):
            return pe.op(lambda e: e.matmul(out, lhsT=lhsT, rhs=rhs, start=start, stop=stop_, skip_group_check=sgc), reads, writes)

        def trp(out, in_, ident, reads, writes):
            return pe.op(lambda e: e.transpose(out, in_, ident), reads, writes)

        def actf(out, in_, func, reads, writes, **kw):
            return act.op(lambda e: e.activation(out=out, in_=in_, func=func, **kw), reads, writes)

        def tt(eng, out, in0, in1, op, reads, writes):
            return eng.op(lambda e: e.tensor_tensor(out=out, in0=in0, in1=in1, op=op), reads, writes)

        def ts(eng, out, in0, s1, s2, op0, op1, reads, writes):
            if s2 is None:
                return eng.op(lambda e: e.tensor_scalar(out=out, in0=in0, scalar1=s1, scalar2=None, op0=op0), reads, writes)
            return eng.op(lambda e: e.tensor_scalar(out=out, in0=in0, scalar1=s1, scalar2=s2, op0=op0, op1=op1), reads, writes)

        def stt(eng, out, in0, scalar, in1, op0, op1, reads, writes):
            return eng.op(lambda e: e.scalar_tensor_tensor(out=out, in0=in0, scalar=scalar, in1=in1, op0=op0, op1=op1), reads, writes)

        def cp(eng, out, in_, reads, writes):
            if eng is act:
                return act.op(lambda e: e.copy(out=out, in_=in_), reads, writes)
            return eng.op(lambda e: e.tensor_copy(out=out, in_=in_), reads, writes)

        def recip(out, in_, reads, writes):
            return dve.op(lambda e: e.reciprocal(out=out, in_=in_), reads, writes)

        def mset(eng, ap, val, writes):
            return eng.op(lambda e: e.memset(ap, val), (), writes)

        def rot_sb(st, name, n, shape, dt=F32, dma=False):
            items = []
            for i in range(n):
                t = T(st, f"{name}{i}", shape, dt)
                if dma:
                    items.append((t, Buf(), S.dma_sem()))
                else:
                    items.append((t, Buf()))
            return Rot(items)

        def rot_ps(st, name, n, shape, dt=F32):
            return Rot([(PS(st, f"{name}{i}", shape, dt), Buf()) for i in range(n)])

        def rot_ps_sub(st, name, nbanks, nsub, subshape, dt=F32):
            items = []
            for i in range(nbanks):
                t = PS(st, f"{name}{i}", [128, nsub] + list(subshape), dt)
                for j in range(nsub):
                    items.append((t[:, j], Buf()))
            return Rot(items)

        def load_w_bf16(st_pool, dst, bdst, src_d, ncols, cast_engs):
            for k in range(8):
                stg, bs, ds = st_pool.get()
                sp.dma(stg[:, 0:ncols], src_d[k * 128:(k + 1) * 128, :], ds, writes=[bs])
                eng = cast_engs[k % len(cast_engs)]
                cp(eng, dst[:, k, :], stg[:, 0:ncols], [bs], [bdst])

        ident_b = T(top, "ident_b", [128, 128], BF16); b_identb = Buf()
        ident_f = T(top, "ident_f", [128, 128], F32); b_identf = Buf()
        cmask_b = T(top, "cmask_b", [128, 128], BF16); b_cmask = Buf()
        ones_row = T(top, "ones_row", [1, 128], F32); b_ones = Buf()
        eps_col = T(top, "eps_col", [128, 1], F32); b_eps = Buf()
        sgn_col = T(top, "sgn_col", [128, 1], F32); b_sgn = Buf()
        cols = T(top, "cols", [128, 48], F32); b_cols = Buf()
        GG = T(top, "GG", [128, 2, D], F32); b_GG = Buf()
        GS = T(top, "GS", [128, 128], F32); b_GS = Buf()
        neglam = T(top, "neglam", [128, 2], F32); b_neglam = Buf()

        mset(pool, ident_f[:], 1.0, [b_identf])
        pool.op(lambda e: e.affine_select(out=ident_f[:], in_=ident_f[:], pattern=[[-1, 128]], compare_op=ALU.is_equal,
                                          fill=0.0, base=0, channel_multiplier=1), [b_identf], [b_identf])
        cp(pool, ident_b[:], ident_f[:], [b_identf], [b_identb])
        mset(pool, cmask_b[:], 0.0, [b_cmask])
        pool.op(lambda e: e.affine_select(out=cmask_b[:], in_=cmask_b[:], pattern=[[1, 128]], compare_op=ALU.is_ge,
                                          fill=NEG, base=0, channel_multiplier=-1), [b_cmask], [b_cmask])
        mset(pool, ones_row[:], 1.0, [b_ones])
        mset(pool, eps_col[:], EPS, [b_eps])
        mset(pool, sgn_col[0:64, :], -1.0, [b_sgn])
        mset(pool, sgn_col[64:128, :], 1.0, [b_sgn])

        with ExitStack() as st:
            dq = S.dma_sem()
            ct = T(st, "ct", [128, 8]); b_ct = Buf()
            sc = T(st, "sc", [128, 8]); b_sc = Buf()
            rows = T(st, "rows", [1, 2, 3 * D]); b_rows = Buf()
            adab = T(st, "adab", [1, 2, 3 * D]); b_adab = Buf()
            gpr = T(st, "gpr", [1, 2, D]); b_gpr = Buf()
            gpo = T(st, "gpo", [1, 2, D]); b_gpo = Buf()
            arow = T(st, "arow", [1, 2, D]); b_arow = Buf()
            ggrow = T(st, "ggrow", [1, 2, D]); b_ggrow = Buf()
            lqk = T(st, "lqk", [1, 256]); b_lqk = Buf()
            gsr = T(st, "gsr", [1, 128]); b_gsr = Buf()
            sm = T(st, "sm", [1, 16]); b_sm = Buf()
            awp = rot_sb(st, "awst", 3, [128, 8, 512], F32, dma=True)
            pmod = rot_ps(st, "pmod", 2, [1, 512])
            pcol = PS(st, "pcol", [128, 64]); b_pcol = Buf()
            pbc = rot_ps(st, "pbc", 2, [128, 512])

            sp.dma(ct[:], cT_d, dq, writes=[b_ct])
            sp.dma(adab[0:1, 0, :], adab_d[0:1, :], dq, writes=[b_adab])
            sp.dma(adab[0:1, 1, :], adab_d[1:2, :], dq, writes=[b_adab])
            sp.dma(gpr[0:1, 0, :], gpre_d[0:1, :], dq, writes=[b_gpr])
            sp.dma(gpr[0:1, 1, :], gpre_d[1:2, :], dq, writes=[b_gpr])
            sp.dma(gpo[0:1, 0, :], gpost_d[0:1, :], dq, writes=[b_gpo])
            sp.dma(gpo[0:1, 1, :], gpost_d[1:2, :], dq, writes=[b_gpo])
            sp.dma(lqk[:], lqk_d, dq, writes=[b_lqk])
            sp.dma(gsr[:], gsub_d, dq, writes=[b_gsr])
            sp.dma(cols[:, 32:40], gkvc_d, dq, writes=[b_cols])
            sp.dma(cols[:, 40:48], bgluc_d, dq, writes=[b_cols])
            actf(sc[:], ct[:], AF.Silu, [b_ct], [b_sc])
            for l in range(2):
                for nt in range(6):
                    stg, bs, ds = awp.get()
                    sp.dma(stg[:], adaw_d[l].rearrange("(k p) n -> p k n", p=128)[:, :, nt * 512:(nt + 1) * 512], ds, writes=[bs])
                    pm, bpm = pmod.get()
                    for k in range(8):
                        mm(pm[:], sc[:, k:k + 1], stg[:, k, :], k == 0, k == 7, [b_sc, bs], [bpm])
                    tt(dve, rows[0:1, l, nt * 512:(nt + 1) * 512], pm[:], adab[0:1, l, nt * 512:(nt + 1) * 512], ALU.add,
                       [bpm, b_adab], [b_rows])
            for l in range(2):
                stt(dve, arow[0:1, l, :], rows[0:1, l, D:2 * D], 1.0, gpr[0:1, l, :], ALU.add, ALU.mult, [b_rows, b_gpr], [b_arow])
                tt(dve, ggrow[0:1, l, :], rows[0:1, l, 2 * D:3 * D], gpo[0:1, l, :], ALU.mult, [b_rows, b_gpo], [b_ggrow])
            for l in range(2):
                for which in range(2):
                    idx = l * 2 + which
                    for k in range(8):
                        src = arow[0:1, l, k * 128:(k + 1) * 128] if which == 0 else rows[0:1, l, k * 128:(k + 1) * 128]
                        c0 = (idx * 8 + k) * 2
                        mm(pcol[:, c0:c0 + 2], src, ones_row[0:1, 0:2], True, True, [b_arow, b_rows, b_ones], [b_pcol])
            cp(dve, cols[:, 0:32], pcol[:].rearrange("p (a two) -> p a two", two=2)[:, :, 0], [b_pcol], [b_cols])
            for l in range(2):
                for hf in range(2):
                    pb, bpb = pbc.get()
                    mm(pb[:], ones_row[0:1, 0:128], ggrow[0:1, l, hf * 512:(hf + 1) * 512], True, True, [b_ones, b_ggrow], [bpb])
                    cp(act, GG[:, l, hf * 512:(hf + 1) * 512], pb[:], [bpb], [b_GG])
            tt(dve, lqk[0:1, 0:64], lqk[0:1, 0:64], lqk[0:1, 64:128], ALU.mult, [b_lqk], [b_lqk])
            tt(dve, lqk[0:1, 128:192], lqk[0:1, 128:192], lqk[0:1, 192:256], ALU.mult, [b_lqk], [b_lqk])
            dve.op(lambda e: e.reduce_sum(out=sm[0:1, 0:1], in_=lqk[0:1, 0:64], axis=AX.X), [b_lqk], [b_sm])
            dve.op(lambda e: e.reduce_sum(out=sm[0:1, 1:2], in_=lqk[0:1, 128:192], axis=AX.X), [b_lqk], [b_sm])
            actf(sm[0:1, 2:4], sm[0:1, 0:2], AF.Exp, [b_sm], [b_sm])
            tt(dve, sm[0:1, 4:5], sm[0:1, 3:4], sm[0:1, 2:3], ALU.subtract, [b_sm], [b_sm])
            ts(dve, sm[0:1, 6:8], sm[0:1, 4:5].to_broadcast([1, 2]), -LAMBDA_INIT, None, ALU.add, None, [b_sm], [b_sm])
            pb, bpb = pbc.get()
            mm(pb[:, 0:2], ones_row[0:1, 0:128], sm[0:1, 6:8], True, True, [b_ones, b_sm], [bpb])
            cp(dve, neglam[:], pb[:, 0:2], [bpb], [b_neglam])
            pb, bpb = pbc.get()
            mm(pb[:, 0:128], ones_row[0:1, 0:128], gsr[0:1, :], True, True, [b_ones, b_gsr], [bpb])
            ts(dve, GS[:], pb[:, 0:128], 1.0 - LAMBDA_INIT, None, ALU.mult, None, [bpb], [b_GS])
            S.barrier(); S.flush()

        if stop == "C":
            return nc, S
        A0c, S0c, A1c, S1c, GKVc, BGLUc = (cols[:, 0:8], cols[:, 8:16], cols[:, 16:24], cols[:, 24:32], cols[:, 32:40], cols[:, 40:48])

        def prenormA(xt, bx, junk, ssq, xs_pool):
            jk, bj = junk.get()
            s_t, bs_ = ssq.get()
            mset(pool, s_t[:, 0:1], 0.0, [bs_])
            actf(jk[:], xt[:], AF.Square, [bx], [bj, bs_], accum_out=s_t[:, 0:1])
            actf(s_t[:, 1:2], s_t[:, 0:1], AF.Sqrt, [bs_, b_eps], [bs_], scale=1.0 / D, bias=eps_col[:, 0:1])
            recip(s_t[:, 2:3], s_t[:, 1:2], [bs_], [bs_])
            xs, bxs = xs_pool.get()
            ts(dve, xs[:], xt[:], s_t[:, 2:3], None, ALU.mult, None, [bx, bs_], [bxs])
            return xs, bxs

        def prenormB(xs, bxs, dsts, trps):
            tp, btp = trps.get()
            for k in range(8):
                trp(tp[:, k, :], xs[:, k * 128:(k + 1) * 128], ident_b[:], [bxs, b_identb], [btp])
            for (dfn, bd, scl, bia) in dsts:
                for k in range(8):
                    if bia is None:
                        ts(dve, dfn(k), tp[:, k, :], scl[:, k:k + 1], None, ALU.mult, None, [btp, b_cols], [bd])
                    else:
                        ts(dve, dfn(k), tp[:, k, :], scl[:, k:k + 1], bia[:, k:k + 1], ALU.mult, ALU.add, [btp, b_cols], [bd])

        def prenorm(st_res, xt, bx, dsts, junk, ssq, xs_pool, trps, ridx):
            xs, bxs = prenormA(xt, bx, junk, ssq, xs_pool)
            prenormB(xs, bxs, dsts, trps)

        def postnorm(po, bpo, l, xt, bx, junk, ssq, tbuf, obuf, dst_ap, bdst, dq_out):
            jk, bj = junk.get()
            s_t, bs_ = ssq.get()
            mset(pool, s_t[:, 0:1], 0.0, [bs_])
            bpo = bpo if isinstance(bpo, list) else [bpo]
            actf(jk[:], po, AF.Square, bpo, [bj, bs_], accum_out=s_t[:, 0:1])
            actf(s_t[:, 1:2], s_t[:, 0:1], AF.Sqrt, [bs_, b_eps], [bs_], scale=1.0 / D, bias=eps_col[:, 0:1])
            recip(s_t[:, 2:3], s_t[:, 1:2], [bs_], [bs_])
            tb, btb = tbuf.get()
            stt(dve, tb[:], po, s_t[:, 2:3], GG[:, l, :], ALU.mult, ALU.mult, bpo + [bs_, b_GG], [btb])
            ob, bob, dso = obuf.get()
            tt(pool, ob[:], tb[:], xt[:], ALU.add, [btb, bx], [bob])
            (dq_out if isinstance(dq_out, EngW) else sp).dma(dst_ap, ob[:], dso, reads=[bob], writes=[bdst])

        xrows0 = x_d.rearrange("(b p j) d -> b j p d", p=128, j=8)
        h1rows0 = h1_d.rearrange("(b p j) d -> b j p d", p=128, j=8)
        with ExitStack() as st:
            wst = rot_sb(st, "wst", 2, [128, 2048], F32, dma=True)
            w_in = T(st, "w_in", [128, 8, 2048], BF16); b_win = Buf()
            xp_ = rot_sb(st, "xt", 4, [128, D], F32, dma=True)
            junk = rot_sb(st, "junk", 2, [128, D], BF16)
            ssq = rot_sb(st, "ssq", 4, [128, 4], F32)
            xs_pool = rot_sb(st, "xs", 3, [128, D], BF16)
            hT = rot_sb(st, "hT", 2, [128, 8, 1024], BF16)
            utm = rot_sb(st, "utm", 1, [128, 64, 8, 16], BF16)
            ublk = rot_sb(st, "ublk", 1, [128, 64, 128], BF16, dma=True)
            zsT = rot_sb(st, "zsT", 1, [128, 8, 1024], BF16, dma=True)
            trps = rot_ps(st, "trps", 2, [128, 8, 128], BF16)
            pacc = rot_ps(st, "pacc", 6, [128, 512])
            dq_o = S.dma_sem(); dq_o2 = S.dma_sem()
            load_w_bf16(wst, w_in, b_win, awin_d, 2048, [pool])
            ne = [0]
            hslots = [hT.get() for _ in range(2)]
            hbufs = [[Buf() for _ in range(8)] for _ in range(2)]
            cur_u = {}

            xsd = {}

            def preA(t):
                b, j = divmod(t, 8)
                xt, bx, dsx = xp_.get()
                sp.dma(xt[:], xrows0[b, j], dsx, writes=[bx])
                xsd[t] = prenormA(xt, bx, junk, ssq, xs_pool)

            def preB(t):
                b, j = divmod(t, 8)
                h_t = hslots[b % 2][0]
                bhj = hbufs[b % 2][j]
                xs, bxs = xsd.pop(t)
                prenormB(xs, bxs, [(lambda k, h_t=h_t, j=j: h_t[:, k, j * 128:(j + 1) * 128], bhj, A0c, S0c)], trps)

            def proj(t):
                b, j = divmod(t, 8)
                h_t = hslots[b % 2][0]
                bhj = hbufs[b % 2][j]
                if j == 0:
                    cur_u[b] = utm.get()
                u_t, bu = cur_u[b]
                for nt in range(2):
                    pa, bpa = pacc.get()
                    for k in range(8):
                        mm(pa[:], h_t[:, k, j * 128:(j + 1) * 128], w_in[:, k, nt * 512:(nt + 1) * 512], k == 0, k == 7, [bhj, b_win], [bpa])
                    cp(act, u_t[:, nt * 32:(nt + 1) * 32, j, :], pa[:].rearrange("p (g c) -> p g c", c=16), [bpa], [bu])
                    ne[0] += 1

            def blockend(b):
                h_t = hslots[b % 2][0]
                bhl = hbufs[b % 2]
                u_t, bu = cur_u[b]
                z_t, bz, dsz_ = zsT.get()
                for zc in range(8):
                    for hf in range(2):
                        pa, bpa = pacc.get()
                        for k in range(8):
                            mm(pa[:], w_in[:, k, 1024 + zc * 128:1024 + (zc + 1) * 128], h_t[:, k, hf * 512:(hf + 1) * 512], k == 0, k == 7,
                               bhl[hf * 4:(hf + 1) * 4] + [b_win], [bpa])
                        actf(z_t[:, zc, hf * 512:(hf + 1) * 512], pa[:], AF.Silu, [bpa], [bz])
                pool.dma(zs_d[:, :, b * 1024:(b + 1) * 1024], z_t[:], dsz_, reads=[bz], writes=[bzs[b]])
                ub, bub, dsub = ublk.get()
                for g8 in range(8):
                    tp, btp = trps.get()
                    for gl in range(8):
                        g = g8 * 8 + gl
                        trp(tp[:, gl, :], u_t[:, g].rearrange("p i c -> p (i c)"), ident_b[:], [bu, b_identb], [btp])
                    cp(dve if g8 % 2 == 0 else act, ub[:, g8 * 8:(g8 + 1) * 8, :], tp[:], [btp], [bub])
                pool.dma(U_d[b].rearrange("p (g c) -> p g c", c=128), ub[:], dsub, reads=[bub], writes=[bU[b]])

            preA(0); preA(1); preB(0)
            for t in range(32):
                if t + 2 < 32:
                    preA(t + 2)
                if t + 1 < 32:
                    preB(t + 1)
                proj(t)
                if t % 8 == 7:
                    blockend(t // 8)
            S.barrier(); S.flush()

        if stop == "2a":
            return nc, S
        with ExitStack() as st:
            WXA = T(st, "WXA", [128, 64, 128], BF16); b_wxa = Buf()
            WXB = T(st, "WXB", [128, 64, 128], BF16); b_wxb = Buf()
            WY1 = T(st, "WY1", [128, 64, 128], BF16); b_wy1 = Buf()
            WY2 = T(st, "WY2", [128, 64, 128], BF16); b_wy2 = Buf()
            b_tc = Buf(); b_ts = Buf()
            RHO = T(st, "RHO", [128, 64], F32); b_rho = Buf()
            E1c = T(st, "E1c", [128, 64]); E1s = T(st, "E1s", [128, 64]); b_e1 = Buf()
            PSW = T(st, "PSW", [128, 128], F32); b_psw = Buf()
            ts(pool, PSW[:, 0:64], ident_f[:, 64:128], -1.0, None, ALU.mult, None, [b_identf], [b_psw])
            cp(pool, PSW[:, 64:128], ident_f[:, 0:64], [b_identf], [b_psw])
            with ExitStack() as sp1:
                dq = S.dma_sem()
                lamre = T(sp1, "lamre", [128, 64]); lamim = T(sp1, "lamim", [128, 64]); dt_ = T(sp1, "dt", [128, 64])
                b_in = Buf()
                sp.dma(lamre[:], lamre_d, dq, writes=[b_in])
                sp.dma(lamim[:], lamim_d, dq, writes=[b_in])
                sp.dma(dt_[:], logdt_d, dq, writes=[b_in])
                Dcol = T(sp1, "Dcol", [128, 64]); b_dcol = b_in
                sp.dma(Dcol[:], dcol_d, dq, writes=[b_dcol])
                LR = T(sp1, "LR", [128, 9, 64]); LI = T(sp1, "LI", [128, 9, 64])
                MR = T(sp1, "MR", [128, 8, 64]); MI = T(sp1, "MI", [128, 8, 64])
                LIs = T(sp1, "LIs", [128, 9, 64]); LRn = T(sp1, "LRn", [128, 9, 64]); LIn = T(sp1, "LIn", [128, 9, 64])
                MRn = T(sp1, "MRn", [128, 8, 64]); MIn = T(sp1, "MIn", [128, 8, 64])
                b_pw = Buf()
                w = T(sp1, "wk_", [128, 12, 64]); b_w = Buf()
                wi = T(sp1, "wi_", [128, 64], I32)
                FRE = T(sp1, "FRE", [128, 64]); FIMs = T(sp1, "FIMs", [128, 64]); FIMn = T(sp1, "FIMn", [128, 64]); b_f = Buf()
                mask = T(sp1, "mask", [128, 128]); b_mask = Buf()
                mset(pool, mask[:], 1.0, [b_mask])
                pool.op(lambda e: e.affine_select(out=mask[:].rearrange("p (j c) -> p j c", c=16), in_=mask[:].rearrange("p (j c) -> p j c", c=16),
                                                  pattern=[[16, 8], [0, 16]], compare_op=ALU.is_ge, fill=0.0, base=15, channel_multiplier=-1),
                        [b_mask], [b_mask])
                R = [b_in, b_w]
                actf(dt_[:], dt_[:], AF.Exp, [b_in], [b_in])
                tt(dve, w[:, 0, :], lamre[:], dt_[:], ALU.mult, R, [b_w])
                tt(dve, w[:, 1, :], lamim[:], dt_[:], ALU.mult, R, [b_w])

                def sin_of(dst, shift):
                    ts(dve, w[:, 2, :], w[:, 1, :], shift, 1.0 / (2 * math.pi), ALU.add, ALU.mult, [b_w], [b_w])
                    cp(dve, wi[:], w[:, 2, :], [b_w], [b_w])
                    cp(dve, w[:, 2, :], wi[:], [b_w], [b_w])
                    stt(dve, w[:, 2, :], w[:, 2, :], -2 * math.pi, w[:, 1, :], ALU.mult, ALU.add, [b_w], [b_w])
                    ts(dve, w[:, 3, :], w[:, 2, :], shift, None, ALU.add, None, [b_w], [b_w])
                    ts(dve, w[:, 2, :], w[:, 3, :], math.pi, -2 * math.pi, ALU.is_gt, ALU.mult, [b_w], [b_w])
                    tt(dve, w[:, 3, :], w[:, 3, :], w[:, 2, :], ALU.add, [b_w], [b_w])
                    actf(dst, w[:, 3, :], AF.Sin, [b_w], [b_w])

                sin_of(w[:, 4, :], 0.0)
                sin_of(w[:, 5, :], 0.5 * math.pi)
                actf(w[:, 6, :], w[:, 0, :], AF.Exp, [b_w], [b_w])
                actf(w[:, 7, :], w[:, 0, :], AF.Exp, [b_w], [b_w], scale=-1.0)
                actf(RHO[:], w[:, 0, :], AF.Exp, [b_w], [b_rho], scale=8.0)
                actf(w[:, 8, :], w[:, 0, :], AF.Exp, [b_w], [b_w], scale=-8.0)
                P_ = [b_w, b_pw]
                mset(pool, LR[:, 0, :], 1.0, [b_pw]); mset(pool, LI[:, 0, :], 0.0, [b_pw])
                mset(pool, MR[:, 0, :], 1.0, [b_pw]); mset(pool, MI[:, 0, :], 0.0, [b_pw])
                tt(dve, LR[:, 1, :], w[:, 6, :], w[:, 5, :], ALU.mult, P_, [b_pw])
                tt(dve, LI[:, 1, :], w[:, 6, :], w[:, 4, :], ALU.mult, P_, [b_pw])
                tt(dve, MR[:, 1, :], w[:, 7, :], w[:, 5, :], ALU.mult, P_, [b_pw])
                stt(dve, MI[:, 1, :], w[:, 7, :], -1.0, w[:, 4, :], ALU.mult, ALU.mult, P_, [b_pw])

                def cpow(XR, XI, k):
                    tt(dve, w[:, 9, :], XR[:, k, :], XR[:, 1, :], ALU.mult, P_, [b_w])
                    tt(dve, w[:, 10, :], XI[:, k, :], XI[:, 1, :], ALU.mult, P_, [b_w])
                    tt(dve, XR[:, k + 1, :], w[:, 9, :], w[:, 10, :], ALU.subtract, P_, [b_pw])
                    tt(dve, w[:, 9, :], XR[:, k, :], XI[:, 1, :], ALU.mult, P_, [b_w])
                    tt(dve, w[:, 10, :], XI[:, k, :], XR[:, 1, :], ALU.mult, P_, [b_w])
                    tt(dve, XI[:, k + 1, :], w[:, 9, :], w[:, 10, :], ALU.add, P_, [b_pw])

                for k in range(1, 8):
                    cpow(LR, LI, k)
                for k in range(1, 7):
                    cpow(MR, MI, k)
                ts(dve, LIs[:], LI[:], sgn_col[:, 0:1], None, ALU.mult, None, [b_pw, b_sgn], [b_pw])
                ts(dve, LRn[:], LR[:], sgn_col[:, 0:1], -1.0, ALU.mult, ALU.mult, [b_pw, b_sgn], [b_pw])
                ts(dve, LIn[:], LI[:], -1.0, None, ALU.mult, None, [b_pw], [b_pw])
                ts(dve, MRn[:], MR[:], sgn_col[:, 0:1], -1.0, ALU.mult, ALU.mult, [b_pw, b_sgn], [b_pw])
                ts(dve, MIn[:], MI[:], -1.0, None, ALU.mult, None, [b_pw], [b_pw])
                F_ = [b_in, b_w, b_pw, b_f]
                ts(dve, w[:, 2, :], LR[:, 1, :], -1.0, None, ALU.add, None, F_, [b_w])
                tt(dve, w[:, 3, :], lamre[:], lamre[:], ALU.mult, F_, [b_w])
                tt(dve, w[:, 9, :], lamim[:], lamim[:], ALU.mult, F_, [b_w])
                tt(dve, w[:, 3, :], w[:, 3, :], w[:, 9, :], ALU.add, F_, [b_w])
                recip(w[:, 3, :], w[:, 3, :], [b_w], [b_w])
                tt(dve, w[:, 9, :], w[:, 2, :], lamre[:], ALU.mult, F_, [b_w])
                tt(dve, w[:, 10, :], LI[:, 1, :], lamim[:], ALU.mult, F_, [b_w])
                tt(dve, w[:, 9, :], w[:, 9, :], w[:, 10, :], ALU.add, F_, [b_w])
                tt(dve, FRE[:], w[:, 9, :], w[:, 3, :], ALU.mult, F_, [b_f])
                tt(dve, w[:, 9, :], LI[:, 1, :], lamre[:], ALU.mult, F_, [b_w])
                tt(dve, w[:, 10, :], w[:, 2, :], lamim[:], ALU.mult, F_, [b_w])
                tt(dve, w[:, 9, :], w[:, 9, :], w[:, 10, :], ALU.subtract, F_, [b_w])
                tt(dve, w[:, 9, :], w[:, 9, :], w[:, 3, :], ALU.mult, F_, [b_w])
                ts(dve, FIMs[:], w[:, 9, :], sgn_col[:, 0:1], None, ALU.mult, None, [b_w, b_sgn], [b_f])
                ts(dve, FIMn[:], FIMs[:], -1.0, None, ALU.mult, None, [b_f], [b_f])
                tt(dve, E1c[:], LR[:, 8, :], w[:, 8, :], ALU.mult, P_, [b_e1])
                tt(dve, E1s[:], LI[:, 8, :], w[:, 8, :], ALU.mult, P_, [b_e1])

                def bc(a):
                    return a.unsqueeze(2).to_broadcast([128, 64, 16])

                with ExitStack() as sp2:
                    Bst = T(sp2, "Bst", [128, 64, 16]); Bsw = T(sp2, "Bsw", [128, 64, 16])
                    Cst = T(sp2, "Cst", [128, 64, 16]); Csw = T(sp2, "Csw", [128, 64, 16]); b_bc = Buf()
                    sp.dma(Bst[:], bst_d.rearrange("p (g c) -> p g c", c=16), dq, writes=[b_bc])
                    sp.dma(Bsw[:], bsw_d.rearrange("p (g c) -> p g c", c=16), dq, writes=[b_bc])
                    sp.dma(Cst[:], cst_d.rearrange("p (g c) -> p g c", c=16), dq, writes=[b_bc])
                    sp.dma(Csw[:], csw_d.rearrange("p (g c) -> p g c", c=16), dq, writes=[b_bc])
                    Bbst = T(sp2, "Bbst", [128, 64, 16]); Bbsw = T(sp2, "Bbsw", [128, 64, 16]); b_bb = Buf()
                    t1p = rot_sb(sp2, "t1p", 2, [128, 64, 16]); t2p = rot_sb(sp2, "t2p", 2, [128, 64, 16])

                    def lin2(out, bo, a0, s0, a1, s1, rd):
                        t1, bt1 = t1p.get(); t2, bt2 = t2p.get()
                        tt(dve, t1[:], a0, bc(s0), ALU.mult, rd, [bt1])
                        tt(pool, t2[:], a1, bc(s1), ALU.mult, rd, [bt2])
                        tt(dve if lin2.n % 2 == 0 else pool, out, t1[:], t2[:], ALU.add, [bt1, bt2], [bo])
                        lin2.n += 1
                    lin2.n = 0
                    rdB = [b_bc, b_f, b_pw, b_bb]
                    lin2(Bbst[:], b_bb, Bst[:], FRE[:], Bsw[:], FIMs[:], [b_bc, b_f])
                    lin2(Bbsw[:], b_bb, Bsw[:], FRE[:], Bst[:], FIMn[:], [b_bc, b_f])
                    wy1v = WY1[:].rearrange("p g (j c) -> p g j c", c=16)
                    for j in range(8):
                        lin2(wy1v[:, :, j, :], b_wy1, Cst[:], LRn[:, j + 1, :], Csw[:], LIn[:, j + 1, :], rdB)
                    pxt = rot_ps(sp2, "pxt", 2, [128, 4, 128])
                    pw2 = rot_ps(sp2, "pw2", 2, [128, 4, 128])
                    tmpw = rot_sb(sp2, "tmpw", 2, [128, 4, 128])
                    P1 = T(sp2, "P1", [128, 32, 8, 16]); b_p1 = Buf()
                    P2 = T(sp2, "P2", [128, 32, 8, 16]); b_p2 = Buf()
                    for hf in range(2):
                        gs = slice(hf * 32, (hf + 1) * 32)

                        def bch(a):
                            return a[:, gs].unsqueeze(2).to_broadcast([128, 32, 16])

                        def lin2h(out, bo, a0, s0, a1, s1, rd):
                            t1, bt1 = t1p.get(); t2, bt2 = t2p.get()
                            tt(dve, t1[:, 0:32, :], a0, bch(s0), ALU.mult, rd, [bt1])
                            tt(pool, t2[:, 0:32, :], a1, bch(s1), ALU.mult, rd, [bt2])
                            tt(dve if lin2.n % 2 == 0 else pool, out, t1[:, 0:32, :], t2[:, 0:32, :], ALU.add, [bt1, bt2], [bo])
                            lin2.n += 1
                        for i in range(8):
                            k = 7 - i
                            lin2h(P1[:, :, i, :], b_p1, Bbst[:, gs, :], LR[:, k, :], Bbsw[:, gs, :], LIs[:, k, :], rdB)
                            lin2h(P2[:, :, i, :], b_p2, Cst[:, gs, :], MRn[:, k, :], Csw[:, gs, :], MIn[:, k, :], rdB)
                        for g4 in range(8):
                            px, bpx = pxt.get()
                            p2_, bp2 = pw2.get()
                            for gl in range(4):
                                gi = g4 * 4 + gl
                                trp(px[:, gl, :], P1[:, gi].rearrange("p i c -> p (i c)"), ident_f[:], [b_p1, b_identf], [bpx])
                                mm(p2_[:, gl, :], P1[:, gi].rearrange("p i c -> p (i c)"), P2[:, gi].rearrange("p i c -> p (i c)"), True, True,
                                   [b_p1, b_p2], [bp2])
                            g0 = hf * 32 + g4 * 4
                            cp(act, WXA[:, g0:g0 + 4, :], px[:], [bpx], [b_wxa])
                            cp(act, WXB[:, g0:g0 + 4, 0:64], px[:, :, 64:128], [bpx], [b_wxb])
                            actf(WXB[:, g0:g0 + 4, 64:128], px[:, :, 0:64], AF.Copy, [bpx], [b_wxb], scale=-1.0)
                            tw, btw = tmpw.get()
                            tt(dve, tw[:], p2_[:], mask[:].unsqueeze(1).to_broadcast([128, 4, 128]), ALU.mult, [bp2, b_mask], [btw])
                            for gl in range(4):
                                g = g0 + gl
                                stt(dve, WY2[:, g, :], ident_f[:], Dcol[:, g:g + 1], tw[:, gl, :], ALU.mult, ALU.add,
                                    [b_identf, b_dcol, btw], [b_wy2])
                    S.barrier(); S.flush()
                S.barrier(); S.flush()
            if stop == "P1":
                return nc, S
            TC = T(st, "TC", [128, 64, 128], BF16)
            TS_ = T(st, "TS", [128, 64, 128], BF16)
            with ExitStack() as sp3:
                TCf = T(sp3, "TCf", [128, 32, 128]); TSf = T(sp3, "TSf", [128, 32, 128]); b_tf = Buf()
                ta = T(sp3, "ta", [128, 32, 64]); tb_ = T(sp3, "tb", [128, 32, 64]); b_ta = Buf(); b_tb = Buf()
                for hf in range(2):
                    gs = slice(hf * 32, (hf + 1) * 32)
                    cp(dve, TCf[:, :, 0], E1c[:, gs], [b_e1], [b_tf])
                    cp(dve, TSf[:, :, 0], E1s[:, gs], [b_e1], [b_tf])
                    n = 1
                    while n < 128:
                        cn = TCf[:, :, n - 1:n].to_broadcast([128, 32, n])
                        sn = TSf[:, :, n - 1:n].to_broadcast([128, 32, n])
                        tt(dve, ta[:, :, 0:n], TCf[:, :, 0:n], cn, ALU.mult, [b_tf], [b_ta])
                        tt(pool, tb_[:, :, 0:n], TSf[:, :, 0:n], sn, ALU.mult, [b_tf], [b_tb])
                        tt(dve, ta[:, :, 0:n], ta[:, :, 0:n], tb_[:, :, 0:n], ALU.subtract, [b_ta, b_tb], [b_ta])
                        tt(pool, tb_[:, :, 0:n], TCf[:, :, 0:n], sn, ALU.mult, [b_tf, b_ta], [b_tb])
                        cp(dve, TCf[:, :, n:2 * n], ta[:, :, 0:n], [b_ta], [b_tf])
                        tt(dve, ta[:, :, 0:n], TSf[:, :, 0:n], cn, ALU.mult, [b_tf], [b_ta])
                        tt(pool, TSf[:, :, n:2 * n], ta[:, :, 0:n], tb_[:, :, 0:n], ALU.add, [b_ta, b_tb], [b_tf])
                        n *= 2
                    cp(dve, TC[:, gs, :], TCf[:], [b_tf], [b_tc])
                    cp(pool, TS_[:, gs, :], TSf[:], [b_tf], [b_ts])
                S.barrier(); S.flush()

            if stop == "P":
                return nc, S
            with ExitStack() as s2:
                ubk = rot_sb(s2, "ubk", 1, [128, 64, 128], BF16, dma=True)
                xp = T(s2, "xp", [128, 64, 129], BF16); b_xp = [Buf() for _ in range(64)]
                carry = T(s2, "carry", [128, 64], F32); b_carry = [Buf() for _ in range(64)]
                gel_tm = rot_sb(s2, "geltm", 1, [128, 8, 1024], BF16)
                gelT = rot_sb(s2, "gelT", 1, [128, 8, 1024], BF16, dma=True)
                t1p = rot_sb(s2, "st1", 3, [128, 128]); t2p = rot_sb(s2, "st2", 3, [128, 128]); vp = rot_sb(s2, "sv", 5, [128, 128])
                Wp = rot_sb(s2, "sW", 5, [128, 128]); t3p = rot_sb(s2, "st3", 5, [128, 128]); t4p = rot_sb(s2, "st4", 3, [128, 128])
                pab = rot_ps(s2, "pab", 3, [128, 2, 128])
                psw_ = rot_ps(s2, "psw", 2, [128, 128])
                py = rot_ps(s2, "py", 2, [128, 128])
                ptr = rot_ps(s2, "ptr", 1, [128, 8, 128], BF16)
                rhop = rot_sb(s2, "rhot", 5, [128, 128])
                ones128 = T(s2, "ones128", [128, 128]); b_ones128 = Buf()
                mset(pool, ones128[:], 1.0, [b_ones128])
                lvl = {"2b1": 1, "2b2": 2, "2b3": 3}.get(stop, 4)
                mset(pool, carry[:], 0.0, b_carry)
                for b in range(4):
                    ub, bub, dsu = ubk.get()
                    sp.dma(ub[:], U_d[b].rearrange("p (g c) -> p g c", c=128), dsu, reads=[bU[b]], writes=[bub])
                    cp(pool, xp[:, :, 0], carry[:], b_carry, b_xp)
                    g_t, bgt = gel_tm.get()
                    G = {}

                    def s0(g):
                        ab, bab = pab.get()
                        mm(ab[:, 0, :], WXA[:, g, :], ub[:, g, :], True, True, [b_wxa, bub], [bab])
                        mm(ab[:, 1, :], WXB[:, g, :], ub[:, g, :], True, True, [b_wxb, bub], [bab])
                        G[g] = {"ab": (ab, bab)}

                    def s1(g):
                        ab, bab = G[g]["ab"]
                        t1, bt1 = t1p.get(); t2, bt2 = t2p.get()
                        tt(dve, t1[:], ab[:, 0, :], TC[:, g, :], ALU.mult, [bab, b_tc], [bt1])
                        tt(dve, t2[:], ab[:, 1, :], TS_[:, g, :], ALU.mult, [bab, b_ts], [bt2])
                        rt, brt = rhop.get()
                        actf(rt[:], ones128[:], AF.Copy, [b_ones128, b_rho], [brt], scale=RHO[:, g:g + 1])
                        G[g].update(t1=(t1, bt1), t2=(t2, bt2), rt=(rt, brt))

                    def s2(g):
                        (t1, bt1), (t2, bt2) = G[g]["t1"], G[g]["t2"]
                        v, bv = vp.get()
                        tt(pool, v[:], t1[:], t2[:], ALU.add, [bt1, bt2], [bv])
                        G[g]["v"] = (v, bv)

                    def s3(g):
                        (v, bv), (rt, brt) = G[g]["v"], G[g]["rt"]
                        W_, bW = Wp.get()
                        dve.op(lambda e, W_=W_, v=v, g=g, rt=rt: e.tensor_tensor_scan(out=W_[:], data0=rt[:], data1=v[:],
                                                                                     initial=carry[:, g:g + 1], op0=ALU.mult, op1=ALU.add),
                               [bv, brt, b_carry[g]], [bW])
                        G[g]["W"] = (W_, bW)

                    def s4(g):
                        W_, bW = G[g]["W"]
                        ws, bws = psw_.get()
                        mm(ws[:], PSW[:], W_[:], True, True, [b_psw, bW], [bws])
                        t3, bt3 = t3p.get()
                        tt(pool, t3[:], W_[:], TC[:, g, :], ALU.mult, [bW, b_tc], [bt3])
                        G[g].update(ws=(ws, bws), t3=(t3, bt3))

                    def s5(g):
                        ws, bws = G[g]["ws"]
                        t4, bt4 = t4p.get()
                        tt(dve, t4[:], ws[:], TS_[:, g, :], ALU.mult, [bws, b_ts], [bt4])
                        G[g]["t4"] = (t4, bt4)

                    def s6(g):
                        (t3, bt3), (t4, bt4) = G[g]["t3"], G[g]["t4"]
                        tt(pool, xp[:, g, 1:129], t3[:], t4[:], ALU.add, [bt3, bt4], [b_xp[g]])
                        actf(carry[:, g:g + 1], t3[:, 127:128], AF.Identity, [bt3, bt4], [b_carry[g]], bias=t4[:, 127:128])

                    def s7(g):
                        yb, byb = py.get()
                        mm(yb[:], xp[:, g, 0:128], WY1[:, g, :], True, False, [b_xp[g], b_wy1], [byb])
                        mm(yb[:], ub[:, g, :], WY2[:, g, :], False, True, [bub, b_wy2], [byb])
                        G[g]["y"] = (yb, byb)

                    def s8(g):
                        yb, byb = G.pop(g)["y"]
                        actf(g_t[:, :, g * 16:(g + 1) * 16], yb[:].rearrange("p (j c) -> p j c", c=16), AF.Gelu_apprx_tanh, [byb], [bgt])

                    stages = [s0, s1, s2, s3, s4, s5, s6, s7, s8]
                    for i in range(64 + 8):
                        for sidx in range(8, -1, -1):
                            g = i - sidx
                            if 0 <= g < 64:
                                stages[sidx](g)
                    if lvl < 4:
                        continue
                    gT, bgT, dsgT = gelT.get()
                    for ncu in range(8):
                        tp, btp = ptr.get()
                        for j in range(8):
                            trp(tp[:, j, :], g_t[:, j, ncu * 128:(ncu + 1) * 128], ident_b[:], [bgt, b_identb], [btp])
                        cp(dve if ncu % 2 == 0 else act, gT[:, ncu, :], tp[:].rearrange("p j m -> p (j m)"), [btp], [bgT])
                    sp.dma(gel_d[:, :, b * 1024:(b + 1) * 1024], gT[:], dsgT, reads=[bgT], writes=[bgel[b]])
                S.barrier(); S.flush()

        if stop in ("2b", "2b1", "2b2", "2b3"):
            return nc, S
        with ExitStack() as st:
            wst = rot_sb(st, "wst", 2, [128, 1024], F32, dma=True)
            w_glu = T(st, "w_glu", [128, 8, 1024], BF16); b_wglu = Buf()
            w_out = T(st, "w_out", [128, 8, 1024], BF16); b_wout = Buf()
            load_w_bf16(wst, w_glu, b_wglu, aglu_d, 1024, [pool, dve])
            load_w_bf16(wst, w_out, b_wout, aout_d, 1024, [pool, dve])
            gin = rot_sb(st, "gin", 2, [128, 8, 1024], BF16, dma=True)
            zin = rot_sb(st, "zin", 2, [128, 8, 1024], BF16, dma=True)
            y3 = rot_sb(st, "y3", 2, [128, 8, 1024], BF16)
            sigp = rot_sb(st, "sig", 3, [128, 512], BF16)
            xp_ = rot_sb(st, "xt", 4, [128, D], F32, dma=True)
            junk = rot_sb(st, "junk", 2, [128, D], BF16)
            ssq = rot_sb(st, "ssq", 4, [128, 4], F32)
            tbuf = rot_sb(st, "tbuf", 2, [128, D], F32)
            obuf = rot_sb(st, "obuf", 2, [128, D], F32, dma=True)
            pg = rot_ps(st, "pg", 4, [128, 512])
            po_ = rot_ps(st, "po", 2, [128, 1024])
            dq_o = S.dma_sem()
            ycur = {}

            def glu(b):
                gi, bgi, dsg = gin.get()
                zi, bzi, dsz = zin.get()
                sp.dma(gi[:], gel_d[:, :, b * 1024:(b + 1) * 1024], dsg, reads=[bgel[b]], writes=[bgi])
                sp.dma(zi[:], zs_d[:, :, b * 1024:(b + 1) * 1024], dsz, reads=[bzs[b]], writes=[bzi])
                y3t, by3 = y3.get()
                ycur[b] = (y3t, by3)
                for ec in range(8):
                    for hf in range(2):
                        p_, bp_ = pg.get()
                        for k in range(8):
                            mm(p_[:], w_glu[:, k, ec * 128:(ec + 1) * 128], gi[:, k, hf * 512:(hf + 1) * 512], k == 0, k == 7, [b_wglu, bgi], [bp_])
                        sg, bsg = sigp.get()
                        actf(sg[:], p_[:], AF.Sigmoid, [bp_, b_cols], [bsg], bias=BGLUc[:, ec:ec + 1])
                        tt(dve, sg[:], sg[:], gi[:, ec, hf * 512:(hf + 1) * 512], ALU.mult, [bsg, bgi], [bsg])
                        tt(pool, y3t[:, ec, hf * 512:(hf + 1) * 512], sg[:], zi[:, ec, hf * 512:(hf + 1) * 512], ALU.mult, [bsg, bzi], [by3])

            def outp(b):
                y3t, by3 = ycur.pop(b)
                for j in range(8):
                    xt, bx, dsx = xp_.get()
                    sp.dma(xt[:], xrows0[b, j], dsx, writes=[bx])
                    po, bpo = po_.get()
                    for nt in range(2):
                        for k in range(8):
                            mm(po[:, nt * 512:(nt + 1) * 512], y3t[:, k, j * 128:(j + 1) * 128], w_out[:, k, nt * 512:(nt + 1) * 512], k == 0, k == 7,
                               [by3, b_wout], [bpo])
                    postnorm(po[:], bpo, 0, xt, bx, junk, ssq, tbuf, obuf, h1rows0[b, j], bh1[b * 8 + j], pool)

            glu(0)
            for b in range(4):
                if b + 1 < 4:
                    glu(b + 1)
                outp(b)
            S.barrier(); S.flush()

        if stop == "h1":
            return nc, S

        def h1_deps(tt_):
            b = (tt_ * 128) // 1024
            return [bh1[b * 8 + j] for j in range(8)]

        with ExitStack() as st:
            wst = rot_sb(st, "wst", 2, [128, 2048], F32, dma=True)
            w_in = T(st, "w_in1", [128, 8, 2048], BF16); b_win = Buf()
            load_w_bf16(wst, w_in, b_win, bwin_d, 2048, [pool])
            xp_ = rot_sb(st, "xt", 4, [128, D], F32, dma=True)
            junk = rot_sb(st, "junk", 2, [128, D], BF16)
            ssq = rot_sb(st, "ssq", 4, [128, 4], F32)
            xs_pool = rot_sb(st, "xs", 3, [128, D], BF16)
            hT = rot_sb(st, "hT", 2, [128, 8, 512], BF16)
            qblk = rot_sb(st, "qblk", 2, [128, 8, 512], BF16, dma=True)
            zblk = rot_sb(st, "zblk", 2, [128, 4, 1024], BF16, dma=True)
            trps = rot_ps(st, "trps", 2, [128, 8, 128], BF16)
            pacc = rot_ps(st, "pacc", 6, [128, 512])
            dq_o = S.dma_sem(); dq_o2 = S.dma_sem()
            ne = [0]
            hcur = {}

            def pre(grp):
                h_t, bh = hT.get()
                hcur[grp] = (h_t, bh)
                xa = {}

                def A(tl):
                    tt_ = grp * 4 + tl
                    xt, bx, dsx = xp_.get()
                    sp.dma(xt[:], h1_d[tt_ * 128:(tt_ + 1) * 128, :], dsx, reads=h1_deps(tt_), writes=[bx])
                    xa[tl] = prenormA(xt, bx, junk, ssq, xs_pool)

                def B(tl):
                    xs, bxs = xa.pop(tl)
                    prenormB(xs, bxs, [(lambda k, h_t=h_t, tl=tl: h_t[:, k, tl * 128:(tl + 1) * 128], bh, A1c, S1c)], trps)

                A(0); A(1); B(0); A(2); B(1); A(3); B(2); B(3)

            def proj(grp):
                h_t, bh = hcur.pop(grp)
                qb_, bqb, dsqb = qblk.get()
                for h in range(8):
                    pa, bpa = pacc.get()
                    for k in range(8):
                        mm(pa[:], w_in[:, k, h * 128:(h + 1) * 128], h_t[:, k, :], k == 0, k == 7, [b_win, bh], [bpa])
                    cp(act, qb_[:, h, :], pa[:], [bpa], [bqb])
                    ne[0] += 1
                pool.dma(qT_d[:, :, grp * 512:(grp + 1) * 512], qb_[:], dsqb, reads=[bqb], writes=[bqT[grp]])
                zb_, bzb, dszb = zblk.get()
                for tl in range(4):
                    for nt in range(2):
                        pa, bpa = pacc.get()
                        for k in range(8):
                            mm(pa[:], h_t[:, k, tl * 128:(tl + 1) * 128], w_in[:, k, 1024 + nt * 512:1024 + (nt + 1) * 512], k == 0, k == 7,
                               [b_win, bh], [bpa])
                        actf(zb_[:, tl, nt * 512:(nt + 1) * 512], pa[:], AF.Silu, [bpa], [bzb])
                pool.dma(zs1_d[grp * 512:(grp + 1) * 512, :].rearrange("(t p) d -> p t d", p=128), zb_[:], dszb, reads=[bzb], writes=[bzs1[grp]])

            pre(0)
            for grp in range(8):
                if grp + 1 < 8:
                    pre(grp + 1)
                proj(grp)
            S.barrier(); S.flush()

        with ExitStack() as st:
            wst = rot_sb(st, "wst", 2, [128, 1024], F32, dma=True)
            w_k = T(st, "w_k", [128, 8, 1024], BF16); b_wk = Buf()
            w_v = T(st, "w_v", [128, 8, 1024], BF16); b_wv = Buf()
            load_w_bf16(wst, w_k, b_wk, wk_d, 1024, [pool, dve])
            load_w_bf16(wst, w_v, b_wv, wv_d, 1024, [pool, dve])
            xp_ = rot_sb(st, "xt", 4, [128, D], F32, dma=True)
            junk = rot_sb(st, "junk", 2, [128, D], BF16)
            ssq = rot_sb(st, "ssq", 4, [128, 4], F32)
            xs_pool = rot_sb(st, "xs", 3, [128, D], BF16)
            hT = rot_sb(st, "hT", 2, [128, 8, 512], BF16)
            kblk = rot_sb(st, "kblk", 2, [128, 8, 512], BF16, dma=True)
            vblk = rot_sb(st, "vblk", 2, [128, 8, 4, 129], BF16, dma=True)
            trps = rot_ps(st, "trps", 2, [128, 8, 128], BF16)
            pacc = rot_ps(st, "pacc", 6, [128, 512])
            for (vb_, bvb_, _d) in vblk.items:
                pool.op(lambda e, vb_=vb_: e.memset(vb_[:, :, :, 128:129], 1.0), (), [bvb_])
            ne = [0]
            hcur = {}

            def pre(grp):
                h_t, bh = hT.get()
                hcur[grp] = (h_t, bh)
                xa = {}

                def A(tl):
                    tt_ = grp * 4 + tl
                    xt, bx, dsx = xp_.get()
                    sp.dma(xt[:], h1_d[tt_ * 128:(tt_ + 1) * 128, :], dsx, reads=h1_deps(tt_), writes=[bx])
                    xa[tl] = prenormA(xt, bx, junk, ssq, xs_pool)

                def B(tl):
                    xs, bxs = xa.pop(tl)
                    prenormB(xs, bxs, [(lambda k, h_t=h_t, tl=tl: h_t[:, k, tl * 128:(tl + 1) * 128], bh, GKVc, None)], trps)

                A(0); A(1); B(0); A(2); B(1); A(3); B(2); B(3)

            def proj(grp):
                h_t, bh = hcur.pop(grp)
                kb_, bkb, dskb = kblk.get()
                for h in range(8):
                    pa, bpa = pacc.get()
                    for k in range(8):
                        mm(pa[:], w_k[:, k, h * 128:(h + 1) * 128], h_t[:, k, :], k == 0, k == 7, [b_wk, bh], [bpa])
                    cp(act, kb_[:, h, :], pa[:], [bpa], [bkb])
                    ne[0] += 1
                pool.dma(kT_d[:, :, grp * 512:(grp + 1) * 512], kb_[:], dskb, reads=[bkb], writes=[bkT[grp]])
                vb_, bvb, dsvb = vblk.get()
                for tl in range(4):
                    for nt in range(2):
                        pa, bpa = pacc.get()
                        for k in range(8):
                            mm(pa[:], h_t[:, k, tl * 128:(tl + 1) * 128], w_v[:, k, nt * 512:(nt + 1) * 512], k == 0, k == 7, [b_wv, bh], [bpa])
                        cp(act, vb_[:, nt * 4:(nt + 1) * 4, tl, 0:128], pa[:].rearrange("p (h e) -> p h e", e=128),
                           [bpa], [bvb])
                        ne[0] += 1
                pool.dma(v_d[:, :, grp * 4:(grp + 1) * 4, :].rearrange("h p t e -> p h (t e)"), vb_[:].rearrange("p h t e -> p h (t e)"),
                         dsvb, reads=[bvb], writes=[bvd[grp]])

            pre(0)
            for grp in range(8):
                if grp + 1 < 8:
                    pre(grp + 1)
                proj(grp)
            S.barrier(); S.flush()

        with ExitStack() as st:
            wst = rot_sb(st, "wst", 2, [128, 1024], F32, dma=True)
            w_o = T(st, "w_o", [128, 8, 1024], BF16); b_wo = Buf()
            load_w_bf16(wst, w_o, b_wo, bout_d, 1024, [pool, dve])
            qin = rot_sb(st, "qin", 2, [128, 2, 8, 512], BF16, dma=True)
            for (qt_, bqt_, _d) in qin.items:
                pool.op(lambda e, qt_=qt_: e.memset(qt_[64:128, 0], 0.0), (), [bqt_])
                pool.op(lambda e, qt_=qt_: e.memset(qt_[0:64, 1], 0.0), (), [bqt_])
            zin = rot_sb(st, "zin1", 2, [128, 4, 1024], BF16, dma=True)
            kin = rot_sb(st, "kin", 2, [128, L], BF16, dma=True)
            vin = rot_sb(st, "vin", 2, [128, 32, 129], BF16, dma=True)
            hin = rot_sb(st, "hin", 4, [128, D], F32, dma=True)
            ptp = rot_sb(st, "pt", 4, [128, 512], BF16)
            o0p = rot_sb(st, "o0", 2, [128, 4, 128], F32)
            odp = rot_sb(st, "od", 1, [128, 4, 8, 128], F32)
            rlp = rot_sb(st, "rl", 4, [128, 8], F32)
            sqp = rot_sb(st, "sq", 1, [128, 4, 8, 128], F32)
            ssn = rot_sb(st, "ssn", 2, [128, 2, 32], F32)
            yat = rot_sb(st, "yat", 1, [128, 4, 1024], BF16)
            yatT = rot_sb(st, "yatT", 2, [128, 8, 128], BF16)
            junk = rot_sb(st, "junk", 1, [128, D], BF16)
            ssq = rot_sb(st, "ssq", 4, [128, 4], F32)
            tbuf = rot_sb(st, "tbuf", 1, [128, D], F32)
            obuf = rot_sb(st, "obuf", 2, [128, D], F32, dma=True)
            big = PS(st, "big", [128, 3, 512])
            bbig = [Buf() for _ in range(3)]
            pss = Rot([(big[:, i, :], bbig[i]) for i in range(3)])
            pacc = [[(PS(st, f"acc{p_}{i}", [128, 2, 256]), Buf()) for i in range(2)] for p_ in range(2)]
            ptr = rot_ps(st, "ptr", 1, [128, 8, 128], BF16)
            LA = 3
            jobs = [(qb, h) for qb in range(8) for h in range(8)]
            kv = {}

            def load_kv(job):
                qb, h = job
                nk = 4 * qb + 4
                ki, bki, dsk = kin.get()
                sp.dma(ki[:, 0:nk * 128], kT_d[:, h, 0:nk * 128], dsk, reads=bkT[0:qb + 1], writes=[bki])
                vi, bvi, dsv = vin.get()
                sp.dma(vi[:, 0:nk, :], v_d[h, :, 0:nk, :], dsv, reads=bvd[0:qb + 1], writes=[bvi])
                kv[job] = (ki, bki, vi, bvi)

            TAIL_D = 64
            pending_tail = []

            def tail_pe(qb, ya, bya):
                for jq in range(4):
                    tt_ = qb * 4 + jq
                    xt, bx, dsx = hin.get()
                    sp.dma(xt[:], h1_d[tt_ * 128:(tt_ + 1) * 128, :], dsx, reads=h1_deps(tt_), writes=[bx])
                    tp, btp = ptr.get()
                    for k in range(8):
                        trp(tp[:, k, :], ya[:, jq, k * 128:(k + 1) * 128], ident_b[:], [bya, b_identb], [btp])
                    yT, byT = yatT.get()
                    cp(dve, yT[:], tp[:], [btp], [byT])
                    po = big[:, 0:2, :].rearrange("p a n -> p (a n)")
                    for nt in range(2):
                        for k in range(8):
                            mm(po[:, nt * 512:(nt + 1) * 512], yT[:, k, :], w_o[:, k, nt * 512:(nt + 1) * 512], k == 0, k == 7, [byT, b_wo], [bbig[0], bbig[1]])
                    bout = Buf()
                    postnorm(po, [bbig[0], bbig[1]], 1, xt, bx, junk, ssq, tbuf, obuf, out_d[tt_ * 128:(tt_ + 1) * 128, :], bout, pool)

            load_kv(jobs[0])
            par = [0]
            for qb in range(8):
                qi, bqi, dsq = qin.get()
                sp.dma(qi[0:64, 0], qT_d[0:64, :, qb * 512:(qb + 1) * 512], dsq, reads=[bqT[qb]], writes=[bqi])
                sp.dma(qi[64:128, 1], qT_d[64:128, :, qb * 512:(qb + 1) * 512], dsq, reads=[bqT[qb]], writes=[bqi])
                zi, bzi, dsz = zin.get()
                sp.dma(zi[:], zs1_d[qb * 512:(qb + 1) * 512, :].rearrange("(t p) d -> p t d", p=128), dsz, reads=[bzs1[qb]], writes=[bzi])
                od, bod = odp.get()
                nk = 4 * qb + 4
                items = [(h, cc, kt) for h in range(8) for kt in range(nk) for cc in range(2)]
                pend = []
                o0s = {}
                accs = {}

                def stageA(it):
                    h, cc, kt = it
                    if cc == 0 and kt == LA:
                        ji = jobs.index((qb, h))
                        if ji + 1 < len(jobs):
                            load_kv(jobs[ji + 1])
                    if cc == 0 and kt == 0:
                        o0s[h] = o0p.get()
                    if kt == 0:
                        accs[(h, cc)] = pacc[cc]
                    ki, bki, vi, bvi = kv[(qb, h)]
                    ps_ = slice(cc * 64, (cc + 1) * 64)
                    r = kt - 4 * qb
                    q0 = max(r, 0) * 128
                    s_, bs_ = pss.get()
                    mm(s_[:, q0:512], ki[:, kt * 128:(kt + 1) * 128], qi[:, cc, h, q0:512], True, r < 0, [bki, bqi], [bs_])
                    if r >= 0:
                        mm(s_[:, q0:q0 + 128], ident_b[:], cmask_b[:], False, True, [b_identb, b_cmask], [bs_])
                    pt, bpt = ptp.get()
                    actf(pt[:, q0:512], s_[:, q0:512], AF.Exp, [bs_], [bpt], scale=0.125)
                    return (pt, bpt)

                def stageC(it, pt, bpt):
                    h, cc, kt = it
                    ki, bki, vi, bvi = kv[(qb, h)]
                    r = kt - 4 * qb
                    pa_ = accs[(h, cc)]
                    for jq in range(max(r, 0), 4):
                        acc, bacc = pa_[jq // 2]
                        mm(acc[:, jq % 2, 0:129], pt[:, jq * 128:(jq + 1) * 128], vi[:, kt, :],
                           kt == 0 and jq % 2 == 0, kt == 4 * qb + jq, [bpt, bvi], [bacc], sgc=True)
                    if kt != nk - 1:
                        return
                    o0, bo0 = o0s[h]
                    rl, brl = rlp.get()
                    for jq in range(4):
                        acc, bacc = pa_[jq // 2]
                        recip(rl[:, jq:jq + 1], acc[:, jq % 2, 128:129], [bacc], [brl])
                    if cc == 0:
                        for jq in range(4):
                            acc, bacc = pa_[jq // 2]
                            ts(dve, o0[:, jq, :], acc[:, jq % 2, 0:128], rl[:, jq:jq + 1], None, ALU.mult, None, [bacc, brl], [bo0])
                    else:
                        ts(dve, rl[:, 4:8], rl[:, 0:4], neglam[:, 0:1], None, ALU.mult, None, [brl, b_neglam], [brl])
                        for jq in range(4):
                            acc, bacc = pa_[jq // 2]
                            stt(dve, od[:, jq, h, :], acc[:, jq % 2, 0:128], rl[:, 4 + jq:5 + jq], o0[:, jq, :], ALU.mult, ALU.add,
                                [bacc, brl, bo0], [bod])

                for i in range(len(items) + LA):
                    if i < len(items):
                        pend.append(stageA(items[i]))
                    if i >= LA:
                        stageC(items[i - LA], *pend.pop(0))
                    if i == TAIL_D and pending_tail:
                        pending_tail.pop(0)()
                sq, bsq = sqp.get()
                sn, bsn = ssn.get()
                tt(pool, sq[:], od[:], od[:], ALU.mult, [bod], [bsq])
                dve.op(lambda e, sn=sn, sq=sq: e.reduce_sum(out=sn[:, 0, :], in_=sq[:].rearrange("p a h e -> p (a h) e"), axis=AX.X), [bsq], [bsn])
                actf(sn[:, 1, :], sn[:, 0, :], AF.Ln, [bsn, b_eps], [bsn], scale=1.0 / 128, bias=eps_col[:, 0:1])
                actf(sn[:, 1, :], sn[:, 1, :], AF.Exp, [bsn], [bsn], scale=-0.5)
                tt(dve, sq[:], od[:], sn[:, 1, :].rearrange("p (a h) -> p a h", h=8).unsqueeze(3).to_broadcast([128, 4, 8, 128]), ALU.mult,
                   [bod, bsn], [bsq])
                tt(pool, sq[:], sq[:], GS[:].unsqueeze(1).unsqueeze(1).to_broadcast([128, 4, 8, 128]), ALU.mult, [bsq, b_GS], [bsq])
                ya, bya = yat.get()
                tt(dve, ya[:], sq[:].rearrange("p a h e -> p a (h e)"), zi[:], ALU.mult, [bsq, bzi], [bya])
                pending_tail.append(lambda qb=qb, ya=ya, bya=bya: tail_pe(qb, ya, bya))
            while pending_tail:
                pending_tail.pop(0)()
            S.barrier(); S.flush()
    return nc, S


_CACHE = {}


def _get_program(stop=None):
    key = stop
    if key not in _CACHE:
        nc, S = build(None, stop)
        nc, S = build(S.record, stop)
        _CACHE[key] = nc
    return _CACHE[key]


def _prep_inputs(inp):
    f = lambda a: np.ascontiguousarray(np.asarray(a, dtype=np.float32))
    x = f(inp["x"]); c = f(inp["c"])
    dup = lambda a: np.ascontiguousarray(np.concatenate([a.T, a.T], 0))
    lam_re = f(inp["a_lam_re"])[0]; lam_im = f(inp["a_lam_im"])[0]; log_dt = f(inp["a_log_dt"])[0]
    b_re = f(inp["a_b_re"])[0]; b_im = f(inp["a_b_im"])[0]; c_re = f(inp["a_c_re"])[0]; c_im = f(inp["a_c_im"])[0]
    bre_t = b_re.transpose(1, 0, 2); bim_t = b_im.transpose(1, 0, 2)
    cre_t = c_re.transpose(2, 0, 1); cim_t = c_im.transpose(2, 0, 1)
    shared = {
        "ada_w": f(inp["ada_w"]), "ada_b": f(inp["ada_b"]), "g_pre": f(inp["g_pre"]), "g_post": f(inp["g_post"]),
        "gkv_col": np.ascontiguousarray(f(inp["g_kv"]).reshape(8, 128).T),
        "a_w_in": f(inp["a_w_in"])[0], "a_w_glu": f(inp["a_w_glu"])[0], "a_w_out": f(inp["a_w_out"])[0],
        "bglu_col": np.ascontiguousarray(f(inp["a_b_glu"])[0].reshape(8, 128).T),
        "w_k": f(inp["w_k"]), "w_v": f(inp["w_v"]), "b_w_in": f(inp["b_w_in"])[0], "b_w_out": f(inp["b_w_out"])[0],
        "lamre2": dup(lam_re), "lamim2": dup(lam_im),
        "logdt2": np.ascontiguousarray(np.broadcast_to(log_dt[None, :], (128, 64))),
        "bst": np.ascontiguousarray(np.concatenate([bre_t, bim_t], 0).reshape(128, 1024)),
        "bsw": np.ascontiguousarray(np.concatenate([bim_t, bre_t], 0).reshape(128, 1024)),
        "cst": np.ascontiguousarray(np.concatenate([cre_t, cim_t], 0).reshape(128, 1024)),
        "csw": np.ascontiguousarray(np.concatenate([cim_t, cre_t], 0).reshape(128, 1024)),
        "dcol": np.ascontiguousarray(np.tile(f(inp["a_d"])[0].reshape(64, 16).T, (8, 1))),
        "lqk": np.ascontiguousarray(np.concatenate([f(inp["b_lq1"])[0], f(inp["b_lk1"])[0], f(inp["b_lq2"])[0], f(inp["b_lk2"])[0]])[None, :]),
        "gsub": np.ascontiguousarray(f(inp["b_g_sub"])[0][None, :]),
    }
    maps = []
    for b in range(x.shape[0]):
        m = dict(shared)
        m["x"] = np.ascontiguousarray(x[b])
        m["cT"] = np.ascontiguousarray(c[b].reshape(8, 128).T)
        maps.append(m)
    return maps


def kernel(**inputs):
    stop = os.environ.get("MK_STOP") or None
    nc = _get_program(stop)
    maps = _prep_inputs(inputs)
    ncores = int(os.environ.get("MK_CORES", "8"))
    maps = maps[:ncores]
    res = run_bass_kernel_spmd(nc, maps, core_ids=list(range(len(maps))))
    outs = [np.asarray(r["out"], dtype=np.float32) for r in res.results]
    return np.stack(outs, 0)
```
